# Optimizing a Trainium2 kernel written in Bass

```python
import jax, jax.numpy as jnp
from jax import lax
import numpy as np

D_MODEL = 1024
BATCH = 8
SEQ = 2048
DEPTH = 1

CHUNK = 64
Q_BLOCK = 128
POOL_WIDTH = D_MODEL // 2
POOL_WINDOWS = (2, 4, 8, 16)
N_POOL_GROUPS = len(POOL_WINDOWS)
POOL_GROUP = POOL_WIDTH // N_POOL_GROUPS
SB_HEAD_DIM = 64
SB_HEADS = (D_MODEL // 2) // SB_HEAD_DIM
SB_WIDTH = SB_HEADS * SB_HEAD_DIM
IN_WIDTH = POOL_WIDTH + 3 * SB_WIDTH
N_BRANCHES = 2
D_FF = 4 * D_MODEL
RMS_EPS = 1e-6

kernel_name = "hybrid_pool_stickbreak_block"


def rmsnorm(x, gain):
    xf = x.astype(jnp.float32)
    y = xf * lax.rsqrt(jnp.mean(xf * xf, axis=-1, keepdims=True) + RMS_EPS)
    return (y * gain.astype(jnp.float32)).astype(x.dtype)


def multiscale_pool(u, w_pool_mix, pool_scale):
    S = u.shape[1]
    uf = u.astype(jnp.float32)
    cs = jnp.concatenate([jnp.zeros_like(uf[:, :1]), jnp.cumsum(uf, axis=1)], axis=1)
    hi = jnp.arange(S) + 1
    outs = []
    for g, w in enumerate(POOL_WINDOWS):
        c0, c1 = g * POOL_GROUP, (g + 1) * POOL_GROUP
        lo = jnp.maximum(hi - w, 0)
        csg = cs[:, :, c0:c1]
        win_sum = csg[:, hi] - csg[:, lo]
        count = (hi - lo).astype(jnp.float32)[None, :, None]
        pooled = win_sum / count - uf[:, :, c0:c1]
        outs.append(jnp.einsum('bsc,cd->bsd', pooled.astype(u.dtype), w_pool_mix[g]))
    y = jnp.concatenate(outs, axis=-1)
    return y * pool_scale


def stick_breaking_attention(q, k, v):
    S = q.shape[2]
    scale = SB_HEAD_DIM ** -0.5
    outs = []
    for blk in range(S // Q_BLOCK):
        t0, t1 = blk * Q_BLOCK, (blk + 1) * Q_BLOCK
        qb, kb, vb = q[:, :, t0:t1], k[:, :, :t1], v[:, :, :t1]
        z = jnp.einsum('bhqd,bhkd->bhqk', qb, kb).astype(jnp.float32) * scale
        mask = jnp.arange(t1)[None, :] < jnp.arange(t0, t1)[:, None]
        log_1mb = jnp.where(mask, jax.nn.log_sigmoid(-z), 0.0)
        tail = lax.cumsum(log_1mb, axis=3, reverse=True) - log_1mb
        attn = jnp.where(mask, jnp.exp(jax.nn.log_sigmoid(z) + tail), 0.0)
        outs.append(jnp.einsum('bhqk,bhkd->bhqd', attn.astype(vb.dtype), vb))
    return jnp.concatenate(outs, axis=2)


def setup_inputs(seed: int = 0) -> dict:
    key = jax.random.key(seed)
    ks = jax.random.split(key, 20)
    f32 = jnp.float32
    nrm = lambda k, shape, fan_in: jax.random.normal(k, shape, f32) * (fan_in ** -0.5)
    gain = lambda k, n: 1.0 + 0.02 * jax.random.normal(k, (n,), f32)
    return {
        "x": jax.random.normal(ks[0], (BATCH, SEQ, D_MODEL), f32),
        "g_pre_mix": gain(ks[1], D_MODEL),
        "w_in": nrm(ks[2], (D_MODEL, IN_WIDTH), D_MODEL),
        "w_pool_mix": nrm(ks[3], (N_POOL_GROUPS, POOL_GROUP, POOL_GROUP), POOL_GROUP),
        "pool_scale": gain(ks[4], POOL_WIDTH),
        "w_br_pool": nrm(ks[5], (POOL_WIDTH, D_MODEL), POOL_WIDTH),
        "w_br_sb": nrm(ks[6], (SB_WIDTH, D_MODEL), SB_WIDTH),
        "w_gate": nrm(ks[7], (D_MODEL, N_BRANCHES * D_MODEL), D_MODEL),
        "b_gate": 0.01 * jax.random.normal(ks[8], (N_BRANCHES * D_MODEL,), f32),
        "w_out": nrm(ks[9], (D_MODEL, D_MODEL), D_MODEL),
        "g_post_mix": gain(ks[10], D_MODEL),
        "g_pre_mlp": gain(ks[11], D_MODEL),
        "w_up": nrm(ks[12], (D_MODEL, D_FF), D_MODEL),
        "w_down": nrm(ks[13], (D_FF, D_MODEL), D_FF),
        "g_post_mlp": gain(ks[14], D_MODEL),
    }


def reference(x, g_pre_mix, w_in, w_pool_mix, pool_scale, w_br_pool, w_br_sb,
              w_gate, b_gate, w_out, g_post_mix, g_pre_mlp, w_up, w_down, g_post_mlp):
    B, S, _ = x.shape
    for _layer in range(DEPTH):
        h = rmsnorm(x, g_pre_mix)
        proj = jnp.einsum('bsd,de->bse', h, w_in)
        u_pool = proj[..., :POOL_WIDTH]
        qkv = proj[..., POOL_WIDTH:].reshape(B, S, 3, SB_HEADS, SB_HEAD_DIM)
        q, k, v = (jnp.transpose(qkv[:, :, i], (0, 2, 1, 3)) for i in range(3))

        y_pool = jnp.einsum('bsc,cd->bsd', multiscale_pool(u_pool, w_pool_mix, pool_scale), w_br_pool)
        o_sb = stick_breaking_attention(q, k, v)
        o_sb = jnp.transpose(o_sb, (0, 2, 1, 3)).reshape(B, S, SB_WIDTH)
        y_sb = jnp.einsum('bsc,cd->bsd', o_sb, w_br_sb)

        gates = jax.nn.sigmoid(jnp.einsum('bsd,de->bse', h, w_gate) + b_gate)
        merged = gates[..., :D_MODEL] * y_pool + gates[..., D_MODEL:] * y_sb
        mix = jnp.einsum('bsd,de->bse', merged, w_out)
        x = x + rmsnorm(mix, g_post_mix)

        h2 = rmsnorm(x, g_pre_mlp)
        a = jnp.square(jax.nn.relu(jnp.einsum('bsd,df->bsf', h2, w_up)))
        ff = jnp.einsum('bsf,fd->bsd', a, w_down)
        x = x + rmsnorm(ff, g_post_mlp)
    return x
```

```python
import os
import numpy as np
import ml_dtypes
from contextlib import ExitStack
import concourse.bass as bass
import concourse.mybir as mybir
from concourse.bass_utils import run_bass_kernel_spmd

F32 = mybir.dt.float32
BF16 = mybir.dt.bfloat16
AF = mybir.ActivationFunctionType
ALU = mybir.AluOpType

S = 2048
D = 1024
NT = S // 128
NEG = -30000.0
EPS = 1e-6


class Res:
    __slots__ = ("name", "w", "rs", "pending", "excl")

    def __init__(self, name, excl=False):
        self.name = name
        self.excl = excl
        self.w = None
        self.rs = {}
        self.pending = []


class Op:
    __slots__ = ("idx", "eng", "fn", "dma_key", "ninst", "deps", "signal", "sigval")

    def __init__(self, idx, eng, fn, dma_key, ninst):
        self.idx = idx
        self.eng = eng
        self.fn = fn
        self.dma_key = dma_key
        self.ninst = ninst
        self.deps = set()
        self.signal = False
        self.sigval = 0


class Prog:
    ENGS = ("pe", "act", "dve", "pool", "sp")

    def __init__(self):
        self.ops = []
        self.dma_cnt = {}
        self.frozen = False

    def add(self, eng, fn, reads=(), writes=(), dma_key=None, ninst=1, barrier=False):
        if self.frozen:
            return None
        op = Op(len(self.ops), eng, fn, dma_key, ninst)
        if barrier:
            last = {}
            for o in self.ops:
                last[(o.eng, o.dma_key)] = o
            for o in last.values():
                op.deps.add(o)
        is_dma = dma_key is not None
        writes = list(writes)
        extra = []
        for w in writes:
            if w.pending:
                extra.extend(w.pending)
                w.pending = []
        deps = []
        for r in reads:
            if r.w is not None:
                deps.append((r.w, "raw"))
            if r.excl:
                for rd in r.rs.values():
                    if rd.eng != eng:
                        deps.append((rd, "raw"))
        for w in writes + extra:
            if w.w is not None:
                deps.append((w.w, "waw"))
            for rd in w.rs.values():
                deps.append((rd, "war"))
        for d, kind in deps:
            if d is op:
                continue
            d_dma = d.dma_key is not None
            if (not d_dma) and (not is_dma) and d.eng == eng and eng == "pe":
                continue
            op.deps.add(d)
        for r in reads:
            r.rs[(eng, dma_key)] = op
        for w in writes:
            w.w = op
            w.rs = {}
        if is_dma:
            self.dma_cnt[dma_key] = self.dma_cnt.get(dma_key, 0) + 16 * ninst
            op.sigval = self.dma_cnt[dma_key]
        self.ops.append(op)
        return op

    def finalize(self):
        for op in self.ops:
            for d in op.deps:
                if d.dma_key is None:
                    d.signal = True
        cnt = {e: 0 for e in self.ENGS}
        for op in self.ops:
            if op.dma_key is None and op.signal:
                cnt[op.eng] += 1
                op.sigval = cnt[op.eng]


def build_nc(debug=None, stop_after=None):
    nc = bass.Bass("TRN2", target_bir_lowering=False)
    P = Prog()

    def dram_in(name, shape, dt=F32):
        return nc.dram_tensor(name, list(shape), dt, kind="ExternalInput").ap()

    x_d = dram_in("x", [S, D])
    w_in_d = dram_in("w_in", [D, 2048])
    w_gate_d = dram_in("w_gate", [D, 2048])
    w_brp_d = dram_in("w_br_pool", [512, D])
    w_brs_d = dram_in("w_br_sb", [512, D])
    w_out_d = dram_in("w_out", [D, D])
    w_up_d = dram_in("w_up", [D, 4096])
    w_down_d = dram_in("w_down", [4096, D])
    wpm_d = dram_in("w_pool_mix", [4, 128, 128])
    gains_d = dram_in("gains", [4, 128, D])
    pscale_d = dram_in("pscale", [128, 4])
    bgate_d = dram_in("bgate", [128, 16])
    invcnt_d = dram_in("invcnt", [128, 64])
    cbf_d = dram_in("cbf", [128, 512], BF16)
    out_d = nc.dram_tensor("out", [S, D], F32, kind="ExternalOutput").ap()
    dbg_d = {}
    if debug:
        for name, shape, dt in debug:
            dbg_d[name] = nc.dram_tensor("dbg_" + name, list(shape), dt, kind="ExternalOutput").ap()

    es = ExitStack()
    with es:
        def sb(name, shape, dt):
            return es.enter_context(nc.sbuf_tensor(name, list(shape), dt))

        RX = sb("RX", [128, 16384], F32)
        R1 = sb("R1", [128, 32768], BF16)
        R2 = sb("R2", [128, 8192], F32)
        WS = sb("WS", [128, 16384], BF16)
        GB = sb("GB", [128, 2048], F32)
        HB = sb("HB", [128, 1024], BF16)
        JK = sb("JK", [128, 1024], BF16)
        CBF = sb("CBF", [128, 512], BF16)
        WPM = sb("WPM", [128, 512], BF16)
        PSC = sb("PSC", [128, 4], F32)
        BGT = sb("BGT", [128, 16], F32)
        ICN = sb("ICN", [128, 64], F32)
        ST = sb("ST", [128, 288], F32)
        PP = [es.enter_context(nc.psum_tensor("pp%d" % i, [128, 1024], F32)) for i in range(4)]

        sem_eng = {e: es.enter_context(nc.semaphore("se_" + e)) for e in Prog.ENGS}
        sem_dma = {}

        def bank(i):
            return PP[i // 2][:, (i % 2) * 512:(i % 2) * 512 + 512]

        rbank = [Res("bank%d" % i, excl=True) for i in range(8)]
        ident = CBF[:, 0:128]
        negtri = CBF[:, 128:256]
        ones = CBF[:, 256:384]
        maskneg = CBF[:, 384:512]

        EPSC = ST[:, 0:1]
        ONEC = ST[:, 1:2]
        st_next = [2]

        def stcol(n=1):
            c = st_next[0]
            st_next[0] += n
            assert st_next[0] <= 272
            return ST[:, c:c + n]

        def dma(eng, out, in_, key, reads=(), writes=()):
            P.add(eng, lambda e: [e.dma_start(out=out, in_=in_)], reads, writes, dma_key=key, ninst=1)

        def dma2(eng, outs_ins, key, reads=(), writes=()):
            def fn(e):
                return [e.dma_start(out=o, in_=i) for (o, i) in outs_ins]
            P.add(eng, fn, reads, writes, dma_key=key, ninst=len(outs_ins))

        def mm(out, lhsT, rhs, start, stop, reads, writes, skip=False):
            if skip:
                P.add("pe", lambda e: e.matmul(out, lhsT, rhs, start=start, stop=stop, skip_group_check=True), reads, writes)
            else:
                P.add("pe", lambda e: e.matmul(out, lhsT, rhs, start=start, stop=stop), reads, writes)

        def act(out, in_, func, reads, writes, bias=None, scale=1.0, accum=None):
            kw = {}
            if bias is not None:
                kw["bias"] = bias
            if accum is not None:
                kw["accum_out"] = accum
            P.add("act", lambda e: e.activation(out, in_, func, scale=scale, **kw), reads, writes)

        def junk():
            return Res("junk")

        r_cbf = Res("cbf")
        r_wpm = Res("wpm")
        r_psc = Res("psc")
        r_bgt = Res("bgt")
        r_icn = Res("icn")
        r_st0 = Res("st0")
        r_gb = [Res("gb0"), Res("gb1")]
        dma("sp", CBF[:], cbf_d, "cbf", writes=[r_cbf])
        dma("sp", GB[:, 0:1024], gains_d[0], "gb0", writes=[r_gb[0]])
        dma("sp", PSC[:], pscale_d, "psc", writes=[r_psc])
        dma("sp", BGT[:], bgate_d, "bgt", writes=[r_bgt])
        dma("sp", ICN[:], invcnt_d, "icn", writes=[r_icn])
        P.add("pool", lambda e: e.memset(EPSC, EPS), (), [r_st0])
        P.add("pool", lambda e: e.memset(ONEC, 1.0), (), [r_st0])
        dma("pool", WPM[:].rearrange("p (g d) -> p g d", g=4), wpm_d.rearrange("g p d -> p g d"), "wpm", writes=[r_wpm])

        r_ws = [Res("ws%d" % i) for i in range(4)]
        ws_n = [0]

        def wslot_view(s):
            return WS[:, s * 4096:(s + 1) * 4096].rearrange("p (k e) -> p k e", k=8)

        def wload(parts, reads=()):
            s = ws_n[0] % 4
            ws_n[0] += 1
            v = wslot_view(s)
            oi = []
            for (k0, src) in parts:
                kk = src.shape[1]
                oi.append((v[:, k0:k0 + kk, :], src))
            dma2("pool", oi, "ws%d" % s, reads=reads, writes=[r_ws[s]])
            return s

        def wcols(wd, c0, k=8):
            v = wd.rearrange("(k p) e -> p k e", p=128)
            return [(0, v[:, 0:k // 2, c0:c0 + 512]), (k // 2, v[:, k // 2:k, c0:c0 + 512])]

        RXb = RX[:].bitcast(BF16)
        R2b = R2[:].bitcast(BF16)
        hT = R2b.rearrange("p (h k t) -> p h k t", h=2, k=8)

        def hT_span(k, s):
            return hT[:, s // 2, k, (s % 2) * 512:(s % 2) * 512 + 512]

        def hT_tile(k, tt):
            return hT[:, tt // 8, k, (tt % 8) * 128:(tt % 8) * 128 + 128]

        r_hT = [Res("hT%d" % t) for t in range(NT)]
        r_x = [Res("x%d" % t) for t in range(NT)]

        def xt(tt):
            return RX[:, tt * 1024:(tt + 1) * 1024]

        QT = R1[:, 0:8192].rearrange("p (c t) -> p c t", c=4)
        KT = R1[:, 8192:16384].rearrange("p (c t) -> p c t", c=4)
        VV = R1[:, 16384:24576].rearrange("p (t e) -> p t e", t=16)
        PLT = R1[:, 24576:32768].rearrange("p (c t) -> p c t", c=4)
        r_QT = [[Res("QT%d_%d" % (c, s)) for s in range(4)] for c in range(4)]
        r_KT = [[Res("KT%d_%d" % (c, s)) for s in range(4)] for c in range(4)]
        r_V = [Res("V%d" % t) for t in range(NT)]
        r_PLT = [Res("PLT%d" % g) for g in range(4)]

        def uT(g):
            return RX[:, g * 2048:(g + 1) * 2048]
        r_uT = [Res("uT%d" % g) for g in range(4)]
        SA = RX[:, 8192:10248]
        SB_ = RX[:, 0:2056]
        r_SA = Res("SA")
        r_SB = Res("SB")
        YPT = RXb[:, 20608:28800].rearrange("p (c t) -> p c t", c=4)
        r_YPT = [[Res("YPT%d_%d" % (g, s)) for s in range(4)] for g in range(4)]
        OT = RXb[:, 0:8192].rearrange("p (c t) -> p c t", c=4)
        r_OT = [[Res("OT%d_%d" % (c, s)) for s in range(4)] for c in range(4)]

        ss1 = stcol(16)
        ln1 = stcol(16)
        rs1 = stcol(16)
        r_ss1 = [Res("ss1_%d" % t) for t in range(NT)]
        r_hb = [Res("hb0"), Res("hb1")]

        HBS = [HB[:, 0:1024], JK[:, 0:1024]]

        def norm_b(tt, xin, r_xin, gb_ap, r_gbx, ssc, lnc, rsc, r_stat, hbs=None, rhbs=None):
            hbs = hbs or HBS
            rhbs = rhbs or r_hb
            hb = hbs[tt % len(hbs)]
            rhb = rhbs[tt % len(hbs)]
            rxl = list(r_xin) if isinstance(r_xin, list) else [r_xin]
            act(hb, xin, AF.Square, rxl, [rhb, r_stat], accum=ssc)
            act(lnc, ssc, AF.Ln, [r_stat, r_st0], [r_stat], bias=EPSC, scale=1.0 / D)
            act(rsc, lnc, AF.Exp, [r_stat], [r_stat], scale=-0.5)
            P.add("dve", lambda e: e.scalar_tensor_tensor(out=hb, in0=xin, scalar=rsc, in1=gb_ap,
                                                          op0=ALU.mult, op1=ALU.mult),
                  rxl + [r_stat, r_gbx], [rhb])

        def norm_c(tt, dst_res, b, hbs=None, rhbs=None, evac="dve"):
            hbs = hbs or HBS
            rhbs = rhbs or r_hb
            hb = hbs[tt % len(hbs)]
            rhb = rhbs[tt % len(hbs)]
            pb = bank(b).bitcast(BF16)
            for k in range(8):
                o = pb[:, k * 128:(k + 1) * 128]
                i_ = hb[:, k * 128:(k + 1) * 128]
                P.add("pe", (lambda o=o, i_=i_: (lambda e: e.transpose(o, i_, ident)))(),
                      [rhb, r_cbf], [rbank[b]])
            dst = hT[:, tt // 8, :, (tt % 8) * 128:(tt % 8) * 128 + 128]
            src = pb.rearrange("p (k t) -> p k t", k=8)
            if evac == "dve":
                P.add("dve", lambda e: e.tensor_copy(dst, src), [rbank[b]], [dst_res])
            elif evac == "act":
                act(dst, src, AF.Copy, [rbank[b]], [dst_res])

        def norm_evac(tt, dst_res, b):
            pb = bank(b).bitcast(BF16)
            dst = hT[:, tt // 8, :, (tt % 8) * 128:(tt % 8) * 128 + 128]
            src = pb.rearrange("p (k t) -> p k t", k=8)
            act(dst, src, AF.Copy, [rbank[b]], [dst_res])

        for tt in range(NT):
            dma("sp", xt(tt), x_d[tt * 128:(tt + 1) * 128, :], "x%d" % tt, writes=[r_x[tt]])
        for i in range(NT + 1):
            if i < NT:
                norm_b(i, xt(i), r_x[i], GB[:, 0:1024], r_gb[0],
                       ss1[:, i:i + 1], ln1[:, i:i + 1], rs1[:, i:i + 1], r_ss1[i])
            if i >= 1:
                norm_c(i - 1, r_hT[i - 1], (i - 1) % 8)

        dma("sp", GB[:, 0:1024], gains_d[1], "gb0", writes=[r_gb[0]])
        dma("sp", GB[:, 1024:2048], gains_d[2], "gb1", writes=[r_gb[1]])
        if stop_after == "P1":
            P.frozen = True
        bk = [0]

        def nextbank():
            b = bk[0] % 8
            bk[0] += 1
            return b

        for g in range(4):
            r_uT[g].pending = list(r_x)
        sl = [wload(wcols(w_in_d, c * 512), reads=([] if c == 0 else [r_x[NT - 1]])) for c in range(4)]

        def proj_fm(c, cc, s, b, ev, lazy=False):
            wv = wslot_view(sl[c])
            ops = []
            for k in range(8):
                ops.append((lambda k=k: mm(bank(b), wv[:, k, cc * 128:(cc + 1) * 128], hT_span(k, s), k == 0, k == 7,
                                           [r_ws[sl[c]]] + r_hT[4 * s:4 * s + 4], [rbank[b]])))
            if c == 0:
                dst, rd, sc = uT(cc)[:, s * 512:(s + 1) * 512], r_uT[cc], 1.0
            elif c == 1:
                dst, rd, sc = QT[:, cc, s * 512:(s + 1) * 512], r_QT[cc][s], 0.125
            else:
                dst, rd, sc = KT[:, cc, s * 512:(s + 1) * 512], r_KT[cc][s], 1.0

            def evac():
                if ev == "act":
                    act(dst, bank(b), AF.Copy, [rbank[b]], [rd], scale=sc)
                elif sc != 1.0:
                    P.add("dve", lambda e: e.tensor_scalar(out=dst, in0=bank(b), scalar1=sc, scalar2=None, op0=ALU.mult),
                          [rbank[b]], [rd])
                else:
                    P.add("dve", lambda e: e.tensor_copy(dst, bank(b)), [rbank[b]], [rd])
            ops.append(evac)
            if lazy:
                return ops
            for o in ops:
                o()

        def proj_v(tt, b, ev, lazy=False):
            wv = wslot_view(sl[3])
            ops = []
            for k in range(8):
                ops.append((lambda k=k: mm(bank(b), hT_tile(k, tt), wv[:, k, :], k == 0, k == 7,
                                           [r_ws[sl[3]], r_hT[tt]], [rbank[b]])))
            dst = VV[:, tt, :]

            def evac():
                if ev == "act":
                    act(dst, bank(b), AF.Copy, [rbank[b]], [r_V[tt]])
                else:
                    P.add("dve", lambda e: e.tensor_copy(dst, bank(b)), [rbank[b]], [r_V[tt]])
            ops.append(evac)
            if lazy:
                return ops
            for o in ops:
                o()

        for s in range(4):
            for cc in range(4):
                proj_fm(0, cc, s, nextbank(), "act")
        NPRE = 2
        for s in range(NPRE):
            for c in (1, 2):
                for cc in range(4):
                    proj_fm(c, cc, s, nextbank(), "act")
            for tt in range(4 * s, 4 * s + 4):
                proj_v(tt, nextbank(), "act")
        deferred = {}
        for s in range(NPRE, 4):
            micro = []
            for c in (1, 2):
                for cc in range(4):
                    micro += proj_fm(c, cc, s, 7, "dve", lazy=True)
            for tt in range(4 * s, 4 * s + 4):
                micro += proj_v(tt, 7, "dve", lazy=True)
            deferred[s] = micro

        if stop_after == "P2":
            P.frozen = True
        def dve_tt(out, a, b_, op, reads, writes):
            P.add("dve", lambda e: e.tensor_tensor(out=out, in0=a, in1=b_, op=op), reads, writes)

        P.add("dve", lambda e: e.memset(SA[:, 0:8], 0.0), (), [r_SA])
        WIN = (2, 4, 8, 16)
        r_tmp = Res("tmp16")
        for g in range(4):
            u = uT(g)
            dve_tt(SA[:, 9:2056], u[:, 1:2048], u[:, 0:2047], ALU.add, [r_uT[g]], [r_SA])
            P.add("dve", (lambda u=u: (lambda e: e.tensor_copy(SA[:, 8:9], u[:, 0:1])))(), [r_uT[g]], [r_SA])
            cur, rcur = SA, r_SA
            if g == 1:
                S4 = RX[:, 0:2048]
                dve_tt(S4, SA[:, 8:2056], SA[:, 6:2054], ALU.add, [r_SA], [r_SB, r_uT[0]])
                cur, rcur = None, r_SB
                cv1 = S4
            if g >= 2:
                if g == 2:
                    P.add("dve", lambda e: e.memset(SB_[:, 0:8], 0.0), (), [r_SB, r_uT[0], r_uT[1]])
                dve_tt(SB_[:, 8:2056], SA[:, 8:2056], SA[:, 6:2054], ALU.add, [r_SA], [r_SB, r_uT[0], r_uT[1]])
                cur, rcur = SB_, r_SB
            if g >= 2:
                dve_tt(SA[:, 8:2056], SB_[:, 8:2056], SB_[:, 4:2052], ALU.add, [r_SB], [r_SA])
                cur, rcur = SA, r_SA
            if g >= 3:
                dve_tt(SB_[:, 8:2056], SA[:, 8:2056], SA[:, 0:2048], ALU.add, [r_SA], [r_SB])
                cur, rcur = SB_, r_SB
            w = WIN[g]
            cv = cv1 if g == 1 else cur[:, 8:2056]
            P.add("dve", (lambda cv=cv, u=u, g=g, w=w: (lambda e: e.scalar_tensor_tensor(
                out=PLT[:, g, :], in0=cv, scalar=1.0 / w, in1=u, op0=ALU.mult, op1=ALU.subtract)))(),
                [rcur, r_uT[g]], [r_PLT[g]])
            tmp = ST[:, 272:288]
            P.add("dve", (lambda cv=cv, g=g, tmp=tmp: (lambda e: e.tensor_tensor(
                out=tmp, in0=cv[:, 0:16], in1=ICN[:, g * 16:(g + 1) * 16], op=ALU.mult)))(),
                [rcur, r_icn], [r_tmp])
            P.add("dve", (lambda u=u, g=g, tmp=tmp: (lambda e: e.tensor_tensor(
                out=PLT[:, g, 0:16], in0=tmp, in1=u[:, 0:16], op=ALU.subtract)))(),
                [r_tmp, r_uT[g]], [r_PLT[g]])
        for g in range(4):
            for s in range(4):
                b = nextbank()
                mm(bank(b), WPM[:, g * 128:(g + 1) * 128], PLT[:, g, s * 512:(s + 1) * 512], True, True,
                   [r_wpm, r_PLT[g]], [rbank[b]])
                act(YPT[:, g, s * 512:(s + 1) * 512], bank(b), AF.Copy, [rbank[b], r_psc], [r_YPT[g][s]],
                    scale=PSC[:, g:g + 1])

        r_hs = [Res("hs%d" % i) for i in range(8)]
        for i in range(8):
            r_hs[i].pending = [r_ws[i // 2]]
        hs_n = [0]

        def hslot_view(i):
            return WS[:, i * 2048:(i + 1) * 2048].rearrange("p (k e) -> p k e", k=8)

        def hload(parts):
            i = hs_n[0] % 8
            hs_n[0] += 1
            v = hslot_view(i)
            oi = []
            for (k0, src) in parts:
                oi.append((v[:, k0:k0 + src.shape[1], :], src))
            dma2("pool", oi, "hs%d" % i, writes=[r_hs[i]])
            return i

        def p5_load(gq, which):
            c0 = gq * 256
            if which == "a":
                v = w_gate_d.rearrange("(k p) e -> p k e", p=128)
                return hload([(0, v[:, 0:4, c0:c0 + 256]), (4, v[:, 4:8, c0:c0 + 256])])
            if which == "b":
                v = w_gate_d.rearrange("(k p) e -> p k e", p=128)
                return hload([(0, v[:, 0:4, 1024 + c0:1024 + c0 + 256]), (4, v[:, 4:8, 1024 + c0:1024 + c0 + 256])])
            vp = w_brp_d.rearrange("(k p) e -> p k e", p=128)
            vs = w_brs_d.rearrange("(k p) e -> p k e", p=128)
            return hload([(0, vp[:, :, c0:c0 + 256]), (4, vs[:, :, c0:c0 + 256])])
        p5hs = {}

        def p5_prefetch():
            for gq in (0, 1):
                for wh in "abc":
                    p5hs[(gq, wh)] = p5_load(gq, wh)
            p5hs[(2, "a")] = p5_load(2, "a")
            p5hs[(2, "b")] = p5_load(2, "b")
        p5w0 = [None]

        if stop_after == "P3":
            P.frozen = True
        def f32v(off):
            return RX[:, off:off + 1024].rearrange("p (h n) -> p h n", h=2)

        def bf16v(off32):
            return RXb[:, 2 * off32:2 * off32 + 1024].rearrange("p (h n) -> p h n", h=2)
        E_ = [f32v(4096), f32v(5120)]
        ARG = [f32v(6144), f32v(7168)]
        R1f_a = R1[:].bitcast(F32)
        CCS = [f32v(8192), R1f_a[:, 12288:13312].rearrange("p (h n) -> p h n", h=2)]
        SP_ = [bf16v(9216), bf16v(9728)]
        ATT = [bf16v(14400), bf16v(14912)]
        r_E = [Res("E0"), Res("E1")]
        r_ARG = [Res("ARG0"), Res("ARG1")]
        r_CCS = [Res("CC0"), Res("CC1")]
        r_CC = r_CCS[0]
        r_CCS[1].pending = list(r_PLT)
        r_SP = [Res("SP0"), Res("SP1")]
        r_ATT = [Res("ATT0"), Res("ATT1"), Res("ATT2")]
        ARG.append(R1f_a[:, 13312:14336].rearrange("p (h n) -> p h n", h=2))
        ATT.append(R1[:, 28672:29696].rearrange("p (h n) -> p h n", h=2))
        r_ARG.append(Res("ARG2"))
        r_ARG[2].pending = list(r_PLT)
        r_ATT[2].pending = list(r_PLT)
        LAGA = 3
        old = r_uT + [r_SA, r_SB]
        for r in r_E + r_ARG[:2] + [r_CC] + r_SP:
            r.pending = list(old)
        for c in range(4):
            for s in range(4):
                r_OT[c][s].pending = list(old)

        ZP = [PP[0], PP[1]]
        r_Z = [[rbank[0], rbank[1]], [rbank[2], rbank[3]]]
        TP = PP[2]
        r_T = [rbank[4], rbank[5]]
        OACC = [bank(6), bank(6)]
        r_OACC = [rbank[6], rbank[6]]

        chains = []
        chain_id = 0
        for j in range(4):
            for hp in range(4):
                kbs = [4 * j + 3, 4 * j + 2, 4 * j + 1, 4 * j] + list(range(4 * j - 1, -1, -1))
                ch = []
                for i, kb in enumerate(kbs):
                    c0 = 128 * (kb - 4 * j) if kb >= 4 * j else 0
                    ch.append(dict(j=j, hp=hp, kb=kb, c0=c0, first=(i == 0), last=(i == len(kbs) - 1),
                                   diag=(kb >= 4 * j), chain=chain_id))
                chains.append(ch)
                chain_id += 1
        tiles = []
        for ch in chains:
            tiles.extend(ch)
        sched = {}
        for s_ in range(NPRE, 4):
            lo = 0 if s_ == NPRE else min(n for n, t in enumerate(tiles) if t["j"] == s_ - 1)
            hi = min(n for n, t in enumerate(tiles) if t["j"] == s_)
            micro = deferred[s_]
            for g, op_ in enumerate(micro):
                n = lo + (g * (hi - lo)) // len(micro)
                sched.setdefault(n, []).append(op_)
        last_sched = max(sched)
        NTL = len(tiles)

        def pv(t3, c0):
            return t3[:, :, c0:512]

        def z3(zb):
            return ZP[zb][:].rearrange("p (h n) -> p h n", h=2)

        def emit_qk(n):
            t = tiles[n]
            zb = n % 2
            j, hp, kb, c0 = t["j"], t["hp"], t["kb"], t["c0"]
            for hd in range(2):
                rows = slice(64 * hd, 64 * hd + 64)
                out = ZP[zb][:, hd * 512 + c0:hd * 512 + 512]
                lhsT = KT[rows, hp, kb * 128:(kb + 1) * 128]
                rhs = QT[rows, hp, j * 512 + c0:(j + 1) * 512]
                mm(out, lhsT, rhs, True, not t["diag"], [r_KT[hp][kb // 4], r_QT[hp][j]], [r_Z[zb][hd]])
            if t["diag"]:
                for hd in range(2):
                    out = ZP[zb][:, hd * 512 + c0:hd * 512 + c0 + 128]
                    mm(out, ident, maskneg, False, True, [r_cbf], [r_Z[zb][hd]])

        def emit_exp1(n):
            t = tiles[n]
            zb = n % 2
            act(pv(E_[zb], t["c0"]), pv(z3(zb), t["c0"]), AF.Exp, r_Z[zb], [r_E[zb]])

        def emit_ln(n):
            t = tiles[n]
            zb = n % 2
            act(pv(SP_[zb], t["c0"]), pv(E_[zb], t["c0"]), AF.Ln, [r_E[zb], r_st0], [r_SP[zb]], bias=ONEC)

        def emit_tri(n):
            t = tiles[n]
            zb = n % 2
            c0 = t["c0"]
            for hd in range(2):
                out = ZP[zb][:, hd * 512 + c0:hd * 512 + 512]
                mm(out, negtri, SP_[zb][:, hd, c0:512], False, True, [r_cbf, r_SP[zb]], [r_Z[zb][hd]], skip=True)
            for hd in range(2):
                out = TP[:, hd * 512 + c0:hd * 512 + 512]
                mm(out, ones, SP_[zb][:, hd, c0:512], True, True, [r_cbf, r_SP[zb]], [r_T[hd]])

        def emit_dve(n):
            t = tiles[n]
            zb = n % 2
            c0 = t["c0"]
            CC = CCS[t["chain"] % 2]
            rcc = r_CCS[t["chain"] % 2]
            if t["first"]:
                P.add("pool", (lambda CC=CC: (lambda e: e.memset(CC, 0.0)))(), (), [rcc])
            a = pv(ARG[n % 3], c0)
            z = pv(z3(zb), c0)
            c = pv(CC, c0)
            tp = pv(TP[:].rearrange("p (h n) -> p h n", h=2), c0)
            dve_tt(a, z, c, ALU.subtract, r_Z[zb] + [rcc], [r_ARG[n % 3]])
            if not t["last"]:
                dve_tt(c, tp, c, ALU.add, r_T + [rcc], [rcc])

        def emit_exp2(n):
            t = tiles[n]
            zb = n % 2
            act(pv(ATT[n % 3], t["c0"]), pv(ARG[n % 3], t["c0"]), AF.Exp, [r_ARG[n % 3]], [r_ATT[n % 3]])

        def emit_av(n):
            t = tiles[n]
            zb = n % 2
            c0, hp, kb, j = t["c0"], t["hp"], t["kb"], t["j"]
            ob = t["chain"] % 2
            for hd in range(2):
                out = OACC[ob][64 * hd:64 * hd + 64, c0:512]
                lhsT = VV[:, kb, (2 * hp + hd) * 64:(2 * hp + hd) * 64 + 64]
                mm(out, lhsT, ATT[n % 3][:, hd, c0:512], t["first"], t["last"], [r_V[kb], r_ATT[n % 3]], [r_OACC[ob]], skip=True)
            if t["last"]:
                dst = OT[:, hp, j * 512:(j + 1) * 512]
                src = OACC[ob]
                P.add("dve", lambda e: e.tensor_copy(dst, src), [r_OACC[ob]], [r_OT[hp][j]])

        emit_qk(0)
        for n in range(NTL + LAGA):
            if n + 1 < NTL:
                emit_qk(n + 1)
            if n < NTL:
                emit_exp1(n)
            if LAGA <= n:
                emit_exp2(n - LAGA)
            if n < NTL:
                emit_ln(n)
                emit_tri(n)
                emit_dve(n)
            if LAGA <= n:
                emit_av(n - LAGA)
            for op_ in sched.get(n, []):
                op_()
            if n == last_sched:
                p5_prefetch()

        if stop_after == "P4":
            P.frozen = True
        NXS = 6
        R1f = R1[:].bitcast(F32)
        XS = [R1f[:, 8192 + j * 1024:8192 + (j + 1) * 1024] for j in range(NXS)]
        r_xs = [Res("xs%d" % j) for j in range(NXS)]
        for j in range(NXS):
            r_xs[j].pending = r_V + r_PLT + [r_CCS[1], r_ARG[2], r_ATT[2]]
            dma("sp", XS[j], x_d[j * 128:(j + 1) * 128, :], "xs%d" % j, writes=[r_xs[j]])
        MT = R1[:, 0:16384].rearrange("p (c t) -> p c t", c=8)
        r_MT = [[Res("MT%d_%d" % (c, s)) for s in range(4)] for c in range(8)]
        oldq = [r for row in r_QT for r in row] + [r for row in r_KT for r in row]
        for c in range(8):
            for s in range(4):
                r_MT[c][s].pending = list(oldq)
        TM = [[RX[:, 4096 + (i * 4 + q) * 512:4096 + (i * 4 + q + 1) * 512] for q in range(4)] for i in range(2)]
        r_TM = [[Res("TM%d_%d" % (i, q)) for q in range(4)] for i in range(2)]
        olda = r_E + r_ARG + [r_CC] + r_SP
        for i in range(2):
            for q in range(4):
                r_TM[i][q].pending = list(olda)
        it = 0
        so = [None, None]
        for gq in range(4):
            ha, hb_, hc = p5hs[(gq, "a")], p5hs[(gq, "b")], p5hs[(gq, "c")]
            wa, wb, wc = hslot_view(ha), hslot_view(hb_), hslot_view(hc)
            for dcl in range(2):
                dc = 2 * gq + dcl
                cs = slice(dcl * 128, dcl * 128 + 128)
                for s in range(4):
                    pb = 4 * (it % 2)
                    ti = it % 2
                    it += 1
                    bgp, bgs, byp, bys = pb, pb + 1, pb + 2, pb + 3
                    for k in range(8):
                        mm(bank(bgp), wa[:, k, cs], hT_span(k, s), k == 0, k == 7,
                           [r_hs[ha]] + r_hT[4 * s:4 * s + 4], [rbank[bgp]])
                    for k in range(8):
                        mm(bank(bgs), wb[:, k, cs], hT_span(k, s), k == 0, k == 7,
                           [r_hs[hb_]] + r_hT[4 * s:4 * s + 4], [rbank[bgs]])
                    for k in range(4):
                        mm(bank(byp), wc[:, k, cs], YPT[:, k, s * 512:(s + 1) * 512], k == 0, k == 3,
                           [r_hs[hc], r_YPT[k][s]], [rbank[byp]])
                    for k in range(4):
                        mm(bank(bys), wc[:, 4 + k, cs], OT[:, k, s * 512:(s + 1) * 512], k == 0, k == 3,
                           [r_hs[hc], r_OT[k][s]], [rbank[bys]])
                    act(TM[ti][0], bank(bgp), AF.Sigmoid, [rbank[bgp], r_bgt], [r_TM[ti][0]], bias=BGT[:, dc:dc + 1])
                    act(TM[ti][1], bank(bgs), AF.Sigmoid, [rbank[bgs], r_bgt], [r_TM[ti][1]], bias=BGT[:, 8 + dc:9 + dc])
                    dve_tt(TM[ti][2], bank(byp), TM[ti][0], ALU.mult, [rbank[byp], r_TM[ti][0]], [r_TM[ti][2]])
                    dve_tt(TM[ti][3], bank(bys), TM[ti][1], ALU.mult, [rbank[bys], r_TM[ti][1]], [r_TM[ti][3]])
                    dve_tt(MT[:, dc, s * 512:(s + 1) * 512], TM[ti][2], TM[ti][3], ALU.add,
                           [r_TM[ti][2], r_TM[ti][3]], [r_MT[dc][s]])
            if gq == 0:
                p5hs[(2, "c")] = p5_load(2, "c")
                p5hs[(3, "a")] = p5_load(3, "a")
                p5hs[(3, "b")] = p5_load(3, "b")
            elif gq == 1:
                p5hs[(3, "c")] = p5_load(3, "c")
                assert hs_n[0] == 12
                for s_ in range(4):
                    r_ws[s_].pending = [r_hs[2 * s_], r_hs[2 * s_ + 1]]
                ws_n[0] = 6
                so[0] = wload(wcols(w_out_d, 0))
            elif gq == 2:
                so[1] = wload(wcols(w_out_d, 512))

        if stop_after == "P5":
            P.frozen = True
        oldrx = [r for row in r_YPT for r in row] + [r for row in r_OT for r in row] + \
                [r for row in r_TM for r in row] + r_ATT
        r_x1h = [[Res("x1_%d_0" % t), Res("x1_%d_1" % t)] for t in range(NT)]
        for tt in range(NT):
            r_x1h[tt][0].pending = list(oldrx)
            r_x1h[tt][1].pending = list(oldrx)
            if tt >= NXS:
                dma("sp", xt(tt), x_d[tt * 128:(tt + 1) * 128, :], "x%d" % tt, reads=[r_ws[so[1]]], writes=r_x1h[tt])
        r_h2T = [Res("h2T%d" % t) for t in range(NT)]
        for tt in range(NT):
            r_h2T[tt].pending = list(r_hT)
        ssA = stcol(32)
        ssS = stcol(16)
        lnA = stcol(16)
        rsA = stcol(16)
        ssB = stcol(16)
        lnB = stcol(16)
        rsB = stcol(16)
        r_stA = [Res("stA%d" % t) for t in range(NT)]
        r_stB = [Res("stB%d" % t) for t in range(NT)]
        HBS4 = HBS + [R1[:, 28672:29696], R1[:, 29696:30720]]
        r_hb4 = r_hb + [Res("hb2"), Res("hb3")]
        JK6 = R1[:, 30720:31232]
        r_jk6 = Res("jk6")
        for r in r_hb4[2:] + [r_jk6]:
            r.pending = r_V + r_PLT + [r_CCS[1], r_ARG[2], r_ATT[2]]

        JK6w = R1[:, 30720:31744]

        def p6_mm(T):
            b0 = 2 * (T % 3)
            for dh in range(2):
                wv = wslot_view(so[dh])
                for k in range(8):
                    mm(bank(b0 + dh), MT[:, k, T * 128:(T + 1) * 128], wv[:, k, :], k == 0, k == 7,
                       [r_ws[so[dh]], r_MT[k][T // 4]], [rbank[b0 + dh]])

        def p6_sqA(T):
            b0 = 2 * (T % 3)
            act(JK6w, PP[T % 3][:], AF.Square, [rbank[b0], rbank[b0 + 1]], [r_jk6, r_stA[T]], accum=ssS[:, T:T + 1])

        def p6_lnA(T):
            act(lnA[:, T:T + 1], ssS[:, T:T + 1], AF.Ln, [r_stA[T], r_st0], [r_stA[T]], bias=EPSC, scale=1.0 / D)

        def p6_expA(T):
            act(rsA[:, T:T + 1], lnA[:, T:T + 1], AF.Exp, [r_stA[T]], [r_stA[T]], scale=-0.5)

        def p6_res(T):
            b0 = 2 * (T % 3)
            for dh in range(2):
                P.add("dve", (lambda b=b0 + dh, T=T, dh=dh: (lambda e: e.scalar_tensor_tensor(
                    out=bank(b), in0=bank(b), scalar=rsA[:, T:T + 1], in1=GB[:, dh * 512:(dh + 1) * 512],
                    op0=ALU.mult, op1=ALU.mult)))(),
                    [rbank[b0 + dh], r_stA[T], r_gb[0]], [rbank[b0 + dh]])
            for dh in range(2):
                xo = xt(T)[:, dh * 512:(dh + 1) * 512]
                if T < NXS:
                    xi = XS[T][:, dh * 512:(dh + 1) * 512]
                    dve_tt(xo, bank(b0 + dh), xi, ALU.add, [rbank[b0 + dh], r_xs[T]], [r_x1h[T][dh]])
                else:
                    dve_tt(xo, bank(b0 + dh), xo, ALU.add, [rbank[b0 + dh], r_x1h[T][dh]], [r_x1h[T][dh]])

        def p6_sqB(T):
            act(HBS4[T % 4], xt(T), AF.Square, r_x1h[T], [r_hb4[T % 4], r_stB[T]], accum=ssB[:, T:T + 1])

        def p6_lnB(T):
            act(lnB[:, T:T + 1], ssB[:, T:T + 1], AF.Ln, [r_stB[T], r_st0], [r_stB[T]], bias=EPSC, scale=1.0 / D)

        def p6_expB(T):
            act(rsB[:, T:T + 1], lnB[:, T:T + 1], AF.Exp, [r_stB[T]], [r_stB[T]], scale=-0.5)

        def p6_h(T):
            hb = HBS4[T % 4]
            P.add("dve", (lambda hb=hb, T=T: (lambda e: e.scalar_tensor_tensor(
                out=hb, in0=xt(T), scalar=rsB[:, T:T + 1], in1=GB[:, 1024:2048], op0=ALU.mult, op1=ALU.mult)))(),
                r_x1h[T] + [r_stB[T], r_gb[1]], [r_hb4[T % 4]])

        def ok(T):
            return 0 <= T < NT

        for i in range(NT + 6):
            if ok(i - 5):
                norm_c(i - 5, r_h2T[i - 5], 6 + (i - 5) % 2, HBS4, r_hb4, evac="none")
            if ok(i):
                p6_mm(i)
            if ok(i - 1):
                p6_sqA(i - 1)
            if ok(i - 3):
                p6_sqB(i - 3)
            if ok(i - 1):
                p6_lnA(i - 1)
            if ok(i - 3):
                p6_lnB(i - 3)
            if ok(i - 1):
                p6_expA(i - 1)
            if ok(i - 3):
                p6_expB(i - 3)
            if ok(i - 6):
                norm_evac(i - 6, r_h2T[i - 6], 6 + (i - 6) % 2)
            if ok(i - 2):
                p6_res(i - 2)
            if ok(i - 4):
                p6_h(i - 4)

        if stop_after == "P6":
            P.frozen = True
        AT = R1[:].rearrange("p (c t) -> p c t", c=32)
        r_AT = [[Res("AT%d_%d" % (c, s)) for s in range(2)] for c in range(32)]
        oldm = [r for row in r_MT for r in row] + r_V + r_PLT + r_xs + r_hb4[2:] + [r_jk6, r_CCS[1], r_ARG[2], r_ATT[2]]
        dma("sp", GB[:, 0:1024], gains_d[3], "gb0", writes=[r_gb[0]])
        FFS = [R2[:, h * 4096:(h + 1) * 4096].rearrange("p (t e) -> p t e", t=8) for h in range(2)]
        ssC = stcol(32)
        ssT = stcol(16)
        lnC = stcol(16)
        rsC = stcol(16)
        r_stC = [Res("stC%d" % t) for t in range(NT)]
        r_rt = [Res("rt0"), Res("rt1")]
        for r in r_rt:
            r.pending = [r_gb[1]]
        for half in range(2):
            for c in range(32):
                for s in range(2):
                    r_AT[c][s].pending = list(oldm)
            oldm = []
            for f4 in range(8):
                su = wload(wcols(w_up_d, f4 * 512))
                wv = wslot_view(su)
                for fl in range(4):
                    fc = 4 * f4 + fl
                    for s in range(2):
                        b = nextbank()
                        for k in range(8):
                            mm(bank(b), wv[:, k, fl * 128:(fl + 1) * 128],
                               hT[:, half, k, s * 512:(s + 1) * 512], k == 0, k == 7,
                               [r_ws[su]] + r_h2T[8 * half + 4 * s:8 * half + 4 * s + 4], [rbank[b]])
                        dst = AT[:, fc, s * 512:(s + 1) * 512]
                        ti = b % 2
                        tmpr = GB[:, 1024 + ti * 512:1024 + (ti + 1) * 512]
                        act(tmpr, bank(b), AF.Relu, [rbank[b]], [r_rt[ti]])
                        dve_tt(dst, bank(b), tmpr, ALU.mult, [rbank[b], r_rt[ti]], [r_AT[fc][s]])
            if stop_after == "P7a":
                P.frozen = True
            r_ffs = [Res("ffs%d_%d" % (half, t)) for t in range(8)]
            for t in range(8):
                r_ffs[t].pending = list(r_h2T[8 * half:8 * half + 8])
            for dh in range(2):
                for f8 in range(4):
                    vd = w_down_d.rearrange("(c p) d -> p c d", p=128)
                    sd = wload([(0, vd[:, 8 * f8:8 * f8 + 4, dh * 512:(dh + 1) * 512]),
                                (4, vd[:, 8 * f8 + 4:8 * f8 + 8, dh * 512:(dh + 1) * 512])])
                    wv = wslot_view(sd)
                    if f8 == 3:
                        for t in range(8):
                            for fl in range(8):
                                fc = 8 * f8 + fl
                                mm(bank(t), AT[:, fc, t * 128:(t + 1) * 128], wv[:, fl, :], fc == 0, fc == 31,
                                   [r_ws[sd], r_AT[fc][t // 4]], [rbank[t]])
                    else:
                        for fl in range(8):
                            fc = 8 * f8 + fl
                            for t in range(8):
                                mm(bank(t), AT[:, fc, t * 128:(t + 1) * 128], wv[:, fl, :], fc == 0, fc == 31,
                                   [r_ws[sd], r_AT[fc][t // 4]], [rbank[t]])
                if stop_after == "P7b":
                    P.frozen = True
                def ev_stats(t, half=half, dh=dh):
                    tt = 8 * half + t
                    act(HBS[t % 2][:, 0:512], bank(t), AF.Square, [rbank[t]], [r_hb[t % 2], r_stC[tt]],
                        accum=ssC[:, 2 * tt + dh:2 * tt + dh + 1])
                    if dh == 0:
                        P.add("dve", (lambda t=t, half=half: (lambda e: e.tensor_copy(FFS[half][:, t, :], bank(t))))(),
                              [rbank[t]], [r_ffs[t]])
                    else:
                        dve_tt(ssT[:, tt:tt + 1], ssC[:, 2 * tt:2 * tt + 1], ssC[:, 2 * tt + 1:2 * tt + 2], ALU.add,
                               [r_stC[tt]], [r_stC[tt]])
                        act(lnC[:, tt:tt + 1], ssT[:, tt:tt + 1], AF.Ln, [r_stC[tt], r_st0], [r_stC[tt]],
                            bias=EPSC, scale=1.0 / D)
                        act(rsC[:, tt:tt + 1], lnC[:, tt:tt + 1], AF.Exp, [r_stC[tt]], [r_stC[tt]], scale=-0.5)

                def ev_big(t, half=half):
                    tt = 8 * half + t
                    for d2 in range(2):
                        src = FFS[half][:, t, :] if d2 == 0 else bank(t)
                        rsrc = r_ffs[t] if d2 == 0 else rbank[t]
                        P.add("dve", (lambda src=src, tt=tt, d2=d2: (lambda e: e.scalar_tensor_tensor(
                            out=src, in0=src, scalar=rsC[:, tt:tt + 1], in1=GB[:, d2 * 512:(d2 + 1) * 512],
                            op0=ALU.mult, op1=ALU.mult)))(),
                            [rsrc, r_stC[tt], r_gb[0]], [rsrc])
                    for d2 in range(2):
                        src = FFS[half][:, t, :] if d2 == 0 else bank(t)
                        rsrc = r_ffs[t] if d2 == 0 else rbank[t]
                        xs = xt(tt)[:, d2 * 512:(d2 + 1) * 512]
                        P.add("dve", (lambda xs=xs, src=src: (lambda e: e.tensor_tensor(out=xs, in0=src, in1=xs, op=ALU.add)))(),
                              [rsrc, r_x1h[tt][d2]], [r_x1h[tt][d2]])
                    if stop_after != "P7d":
                        dma("sp", out_d[tt * 128:(tt + 1) * 128, :], xt(tt), "st%d" % tt, reads=r_x1h[tt])

                for t in range(9):
                    if t < 8:
                        ev_stats(t)
                    if dh == 1 and t >= 1:
                        ev_big(t - 1)
                if stop_after == "P7c" and dh == 0:
                    P.frozen = True

        P.frozen = False
        if debug:
            dbg_src = {
                "hT": (R2b, []), "R1": (R1[:], []), "RX": (RX[:], []),
            }
            for name, _, _ in debug:
                src, _r = dbg_src[name]
                P.add("sp", (lambda src=src, name=name: (lambda e: [e.dma_start(out=dbg_d[name], in_=src)]))(),
                      [], [], dma_key="dbg_" + name, ninst=1, barrier=True)

        P.finalize()
        for key in P.dma_cnt:
            sem_dma[key] = es.enter_context(nc.semaphore("sd_" + key))
        final_waits = [(k, v) for k, v in P.dma_cnt.items() if k.startswith("st") or k.startswith("dbg_")]

        def emit_engine(name, e):
            waited = {}
            for op in P.ops:
                if op.eng != name:
                    continue
                need = {}
                for d in op.deps:
                    key = ("d", d.dma_key) if d.dma_key is not None else ("e", d.eng)
                    if d.sigval > need.get(key, 0):
                        need[key] = d.sigval
                for key, val in need.items():
                    if waited.get(key, 0) >= val:
                        continue
                    waited[key] = val
                    sem = sem_dma[key[1]] if key[0] == "d" else sem_eng[key[1]]
                    e.wait_ge(sem, val)
                res = op.fn(e)
                if op.dma_key is not None:
                    for inst in res:
                        inst.then_inc(sem_dma[op.dma_key], 16)
                elif op.signal:
                    res.then_inc(sem_eng[name], 1)
            if name == "sp":
                if debug:
                    for en in ("pe", "act", "dve", "pool"):
                        pass
                for k, v in final_waits:
                    e.wait_ge(sem_dma[k], v)

        with nc.Block() as block:
            @block.tensor
            def _(e):
                emit_engine("pe", e)

            @block.scalar
            def _(e):
                emit_engine("act", e)

            @block.vector
            def _(e):
                emit_engine("dve", e)

            @block.gpsimd
            def _(e):
                emit_engine("pool", e)

            @block.sync
            def _(e):
                emit_engine("sp", e)
    return nc


_NC_CACHE = {}


def _consts():
    bf = ml_dtypes.bfloat16
    p = np.arange(128)[:, None]
    c = np.arange(128)[None, :]
    ident = (p == c).astype(np.float32)
    negtri = -(p >= c).astype(np.float32)
    ones = np.ones((128, 128), np.float32)
    maskneg = np.where(p < c, 0.0, NEG).astype(np.float32)
    cbf = np.concatenate([ident, negtri, ones, maskneg], axis=1).astype(bf)
    invcnt = np.zeros((128, 64), np.float32)
    for g, w in enumerate((2, 4, 8, 16)):
        t = np.arange(16)
        invcnt[:, g * 16:(g + 1) * 16] = (1.0 / np.minimum(t + 1, w))[None, :]
    return cbf, invcnt


def kernel(x, g_pre_mix, w_in, w_pool_mix, pool_scale, w_br_pool, w_br_sb, w_gate, b_gate,
           w_out, g_post_mix, g_pre_mlp, w_up, w_down, g_post_mlp, _debug=None, _stop=None):
    f = lambda a: np.ascontiguousarray(np.asarray(a, dtype=np.float32))
    x = f(x)
    B = x.shape[0]
    key = "dbg" if _debug else "main"
    if key not in _NC_CACHE:
        _NC_CACHE[key] = build_nc(_debug, _stop)
    nc = _NC_CACHE[key]
    cbf, invcnt = _consts()
    gains = np.stack([np.broadcast_to(f(g)[None, :], (128, D)) for g in
                      (g_pre_mix, g_post_mix, g_pre_mlp, g_post_mlp)]).copy()
    pscale = np.ascontiguousarray(f(pool_scale).reshape(4, 128).T)
    bgate = np.ascontiguousarray(f(b_gate).reshape(16, 128).T)
    shared = {
        "w_in": f(w_in), "w_gate": f(w_gate), "w_br_pool": f(w_br_pool), "w_br_sb": f(w_br_sb),
        "w_out": f(w_out), "w_up": f(w_up), "w_down": f(w_down), "w_pool_mix": f(w_pool_mix),
        "gains": gains, "pscale": pscale, "bgate": bgate, "invcnt": invcnt, "cbf": cbf,
    }
    in_maps = [dict(shared, x=x[b]) for b in range(B)]
    res = run_bass_kernel_spmd(nc, in_maps, core_ids=list(range(B)))
    out = np.stack([np.asarray(r["out"], dtype=np.float32) for r in res.results], axis=0)
    if _debug:
        return out, [{k: np.asarray(v) for k, v in r.items()} for r in res.results]
    return out
```

```python
import os
import numpy as np
import ml_dtypes
from contextlib import ExitStack
import concourse.bass as bass
import concourse.mybir as mybir
from concourse.bass_utils import run_bass_kernel_spmd

F32 = mybir.dt.float32
BF16 = mybir.dt.bfloat16
AF = mybir.ActivationFunctionType
ALU = mybir.AluOpType

S = 2048
D = 1024
NT = S // 128
NEG = -30000.0
EPS = 1e-6


class Res:
    __slots__ = ("name", "w", "rs", "pending", "excl")

    def __init__(self, name, excl=False):
        self.name = name
        self.excl = excl
        self.w = None
        self.rs = {}
        self.pending = []


class Op:
    __slots__ = ("idx", "eng", "fn", "dma_key", "ninst", "deps", "signal", "sigval", "multi")

    def __init__(self, idx, eng, fn, dma_key, ninst):
        self.idx = idx
        self.eng = eng
        self.fn = fn
        self.dma_key = dma_key
        self.ninst = ninst
        self.deps = set()
        self.signal = False
        self.sigval = 0
        self.multi = False


class Prog:
    ENGS = ("pe", "act", "dve", "pool", "sp")

    def __init__(self):
        self.ops = []
        self.dma_cnt = {}
        self.frozen = False

    def add(self, eng, fn, reads=(), writes=(), dma_key=None, ninst=1, barrier=False):
        if self.frozen:
            return None
        op = Op(len(self.ops), eng, fn, dma_key, ninst)
        if barrier:
            last = {}
            for o in self.ops:
                last[(o.eng, o.dma_key)] = o
            for o in last.values():
                op.deps.add(o)
        is_dma = dma_key is not None
        writes = list(writes)
        extra = []
        for w in writes:
            if w.pending:
                extra.extend(w.pending)
                w.pending = []
        deps = []
        for r in reads:
            if r.w is not None:
                deps.append((r.w, "raw"))
            if r.excl:
                for rd in r.rs.values():
                    if rd.eng != eng:
                        deps.append((rd, "raw"))
        for w in writes + extra:
            if w.w is not None:
                deps.append((w.w, "waw"))
            for rd in w.rs.values():
                deps.append((rd, "war"))
        for d, kind in deps:
            if d is op:
                continue
            d_dma = d.dma_key is not None
            if (not d_dma) and (not is_dma) and d.eng == eng and eng == "pe":
                continue
            op.deps.add(d)
        for r in reads:
            r.rs[(eng, dma_key)] = op
        for w in writes:
            w.w = op
            w.rs = {}
        if is_dma:
            self.dma_cnt[dma_key] = self.dma_cnt.get(dma_key, 0) + 16 * ninst
            op.sigval = self.dma_cnt[dma_key]
        self.ops.append(op)
        return op

    def finalize(self):
        for op in self.ops:
            for d in op.deps:
                if d.dma_key is None:
                    d.signal = True
        cnt = {e: 0 for e in self.ENGS}
        for op in self.ops:
            if op.dma_key is None and op.signal:
                cnt[op.eng] += 1
                op.sigval = cnt[op.eng]


def build_nc(debug=None, stop_after=None):
    nc = bass.Bass("TRN2", target_bir_lowering=False)
    P = Prog()

    def dram_in(name, shape, dt=F32):
        return nc.dram_tensor(name, list(shape), dt, kind="ExternalInput").ap()

    x_d = dram_in("x", [S, D])
    w_in_d = dram_in("w_in", [D, 2048])
    w_gate_d = dram_in("w_gate", [D, 2048])
    w_brp_d = dram_in("w_br_pool", [512, D])
    w_brs_d = dram_in("w_br_sb", [512, D])
    w_out_d = dram_in("w_out", [D, D])
    w_up_d = dram_in("w_up", [D, 4096])
    w_down_d = dram_in("w_down", [4096, D])
    wpm_d = dram_in("w_pool_mix", [4, 128, 128])
    gains_d = dram_in("gains", [4, 128, D])
    pscale_d = dram_in("pscale", [128, 4])
    bgate_d = dram_in("bgate", [128, 16])
    invcnt_d = dram_in("invcnt", [128, 64])
    cbf_d = dram_in("cbf", [128, 512], BF16)
    out_d = nc.dram_tensor("out", [S, D], F32, kind="ExternalOutput").ap()
    dbg_d = {}
    if debug:
        for name, shape, dt in debug:
            dbg_d[name] = nc.dram_tensor("dbg_" + name, list(shape), dt, kind="ExternalOutput").ap()

    es = ExitStack()
    with es:
        def sb(name, shape, dt):
            return es.enter_context(nc.sbuf_tensor(name, list(shape), dt))

        RX = sb("RX", [128, 16384], F32)
        R1 = sb("R1", [128, 32768], BF16)
        R2 = sb("R2", [128, 8192], F32)
        WS = sb("WS", [128, 16384], BF16)
        GB = sb("GB", [128, 2048], F32)
        HB = sb("HB", [128, 1024], BF16)
        JK = sb("JK", [128, 1024], BF16)
        CBF = sb("CBF", [128, 512], BF16)
        WPM = sb("WPM", [128, 512], BF16)
        PSC = sb("PSC", [128, 4], F32)
        BGT = sb("BGT", [128, 16], F32)
        ICN = sb("ICN", [128, 64], F32)
        ST = sb("ST", [128, 288], F32)
        PP = [es.enter_context(nc.psum_tensor("pp%d" % i, [128, 1024], F32)) for i in range(4)]

        sem_eng = {e: es.enter_context(nc.semaphore("se_" + e)) for e in Prog.ENGS}
        sem_dma = {}

        def bank(i):
            return PP[i // 2][:, (i % 2) * 512:(i % 2) * 512 + 512]

        rbank = [Res("bank%d" % i, excl=True) for i in range(8)]
        ident = CBF[:, 0:128]
        negtri = CBF[:, 128:256]
        ones = CBF[:, 256:384]
        maskneg = CBF[:, 384:512]

        EPSC = ST[:, 0:1]
        ONEC = ST[:, 1:2]
        st_next = [2]

        def stcol(n=1):
            c = st_next[0]
            st_next[0] += n
            assert st_next[0] <= 272
            return ST[:, c:c + n]

        def dma(eng, out, in_, key, reads=(), writes=()):
            P.add(eng, lambda e: [e.dma_start(out=out, in_=in_)], reads, writes, dma_key=key, ninst=1)

        def dma2(eng, outs_ins, key, reads=(), writes=()):
            def fn(e):
                return [e.dma_start(out=o, in_=i) for (o, i) in outs_ins]
            P.add(eng, fn, reads, writes, dma_key=key, ninst=len(outs_ins))

        def mm(out, lhsT, rhs, start, stop, reads, writes, skip=False):
            if skip:
                P.add("pe", lambda e: e.matmul(out, lhsT, rhs, start=start, stop=stop, skip_group_check=True), reads, writes)
            else:
                P.add("pe", lambda e: e.matmul(out, lhsT, rhs, start=start, stop=stop), reads, writes)

        def act(out, in_, func, reads, writes, bias=None, scale=1.0, accum=None):
            kw = {}
            if bias is not None:
                kw["bias"] = bias
            if accum is not None:
                kw["accum_out"] = accum
            o_ = P.add("act", lambda e: e.activation(out, in_, func, scale=scale, **kw), reads, writes)
            if o_ is not None and accum is not None:
                o_.multi = True

        def junk():
            return Res("junk")

        r_cbf = Res("cbf")
        r_wpm = Res("wpm")
        r_psc = Res("psc")
        r_bgt = Res("bgt")
        r_icn = Res("icn")
        r_st0 = Res("st0")
        r_gb = [Res("gb0"), Res("gb1")]
        dma("sp", CBF[:], cbf_d, "cbf", writes=[r_cbf])
        dma("sp", GB[:, 0:1024], gains_d[0], "gb0", writes=[r_gb[0]])
        dma("sp", PSC[:], pscale_d, "psc", writes=[r_psc])
        dma("sp", BGT[:], bgate_d, "bgt", writes=[r_bgt])
        dma("sp", ICN[:], invcnt_d, "icn", writes=[r_icn])
        P.add("pool", lambda e: e.memset(EPSC, EPS), (), [r_st0])
        P.add("pool", lambda e: e.memset(ONEC, 1.0), (), [r_st0])
        dma("pool", WPM[:].rearrange("p (g d) -> p g d", g=4), wpm_d.rearrange("g p d -> p g d"), "wpm", writes=[r_wpm])

        r_ws = [Res("ws%d" % i) for i in range(4)]
        ws_n = [0]

        def wslot_view(s):
            return WS[:, s * 4096:(s + 1) * 4096].rearrange("p (k e) -> p k e", k=8)

        def wload(parts, reads=()):
            s = ws_n[0] % 4
            ws_n[0] += 1
            v = wslot_view(s)
            oi = []
            for (k0, src) in parts:
                kk = src.shape[1]
                oi.append((v[:, k0:k0 + kk, :], src))
            dma2("pool", oi, "ws%d" % s, reads=reads, writes=[r_ws[s]])
            return s

        def wcols(wd, c0, k=8):
            v = wd.rearrange("(k p) e -> p k e", p=128)
            return [(0, v[:, 0:k // 2, c0:c0 + 512]), (k // 2, v[:, k // 2:k, c0:c0 + 512])]

        RXb = RX[:].bitcast(BF16)
        R2b = R2[:].bitcast(BF16)
        hT = R2b.rearrange("p (h k t) -> p h k t", h=2, k=8)

        def hT_span(k, s):
            return hT[:, s // 2, k, (s % 2) * 512:(s % 2) * 512 + 512]

        def hT_tile(k, tt):
            return hT[:, tt // 8, k, (tt % 8) * 128:(tt % 8) * 128 + 128]

        r_hT = [Res("hT%d" % t) for t in range(NT)]
        r_x = [Res("x%d" % t) for t in range(NT)]

        def xt(tt):
            return RX[:, tt * 1024:(tt + 1) * 1024]

        QT = R1[:, 0:8192].rearrange("p (c t) -> p c t", c=4)
        KT = R1[:, 8192:16384].rearrange("p (c t) -> p c t", c=4)
        VV = R1[:, 16384:24576].rearrange("p (t e) -> p t e", t=16)
        PLT = R1[:, 24576:32768].rearrange("p (c t) -> p c t", c=4)
        r_QT = [[Res("QT%d_%d" % (c, s)) for s in range(4)] for c in range(4)]
        r_KT = [[Res("KT%d_%d" % (c, s)) for s in range(4)] for c in range(4)]
        r_V = [Res("V%d" % t) for t in range(NT)]
        r_PLT = [Res("PLT%d" % g) for g in range(4)]

        def uT(g):
            return RX[:, g * 2048:(g + 1) * 2048]
        r_uT = [Res("uT%d" % g) for g in range(4)]
        SA = RX[:, 8192:10248]
        SB_ = RX[:, 0:2056]
        r_SA = Res("SA")
        r_SB = Res("SB")
        YPT = RXb[:, 20608:28800].rearrange("p (c t) -> p c t", c=4)
        r_YPT = [[Res("YPT%d_%d" % (g, s)) for s in range(4)] for g in range(4)]
        OT = RXb[:, 0:8192].rearrange("p (c t) -> p c t", c=4)
        r_OT = [[Res("OT%d_%d" % (c, s)) for s in range(4)] for c in range(4)]

        ss1 = stcol(16)
        ln1 = stcol(16)
        rs1 = stcol(16)
        r_ss1 = [Res("ss1_%d" % t) for t in range(NT)]
        r_hb = [Res("hb0"), Res("hb1")]

        HBS = [HB[:, 0:1024], JK[:, 0:1024]]

        def norm_b(tt, xin, r_xin, gb_ap, r_gbx, ssc, lnc, rsc, r_stat, hbs=None, rhbs=None):
            hbs = hbs or HBS
            rhbs = rhbs or r_hb
            hb = hbs[tt % len(hbs)]
            rhb = rhbs[tt % len(hbs)]
            rxl = list(r_xin) if isinstance(r_xin, list) else [r_xin]
            act(hb, xin, AF.Square, rxl, [rhb, r_stat], accum=ssc)
            act(lnc, ssc, AF.Ln, [r_stat, r_st0], [r_stat], bias=EPSC, scale=1.0 / D)
            act(rsc, lnc, AF.Exp, [r_stat], [r_stat], scale=-0.5)
            P.add("dve", lambda e: e.scalar_tensor_tensor(out=hb, in0=xin, scalar=rsc, in1=gb_ap,
                                                          op0=ALU.mult, op1=ALU.mult),
                  rxl + [r_stat, r_gbx], [rhb])

        def norm_c(tt, dst_res, b, hbs=None, rhbs=None, evac="dve"):
            hbs = hbs or HBS
            rhbs = rhbs or r_hb
            hb = hbs[tt % len(hbs)]
            rhb = rhbs[tt % len(hbs)]
            pb = bank(b).bitcast(BF16)
            for k in range(8):
                o = pb[:, k * 128:(k + 1) * 128]
                i_ = hb[:, k * 128:(k + 1) * 128]
                P.add("pe", (lambda o=o, i_=i_: (lambda e: e.transpose(o, i_, ident)))(),
                      [rhb, r_cbf], [rbank[b]])
            dst = hT[:, tt // 8, :, (tt % 8) * 128:(tt % 8) * 128 + 128]
            src = pb.rearrange("p (k t) -> p k t", k=8)
            if evac == "dve":
                P.add("dve", lambda e: e.tensor_copy(dst, src), [rbank[b]], [dst_res])
            elif evac == "act":
                act(dst, src, AF.Copy, [rbank[b]], [dst_res])

        def norm_evac(tt, dst_res, b):
            pb = bank(b).bitcast(BF16)
            dst = hT[:, tt // 8, :, (tt % 8) * 128:(tt % 8) * 128 + 128]
            src = pb.rearrange("p (k t) -> p k t", k=8)
            act(dst, src, AF.Copy, [rbank[b]], [dst_res])

        for tt in range(NT):
            dma("sp", xt(tt), x_d[tt * 128:(tt + 1) * 128, :], "x%d" % tt, writes=[r_x[tt]])
        for i in range(NT + 1):
            if i < NT:
                norm_b(i, xt(i), r_x[i], GB[:, 0:1024], r_gb[0],
                       ss1[:, i:i + 1], ln1[:, i:i + 1], rs1[:, i:i + 1], r_ss1[i])
            if i >= 1:
                norm_c(i - 1, r_hT[i - 1], (i - 1) % 8)

        dma("sp", GB[:, 0:1024], gains_d[1], "gb0", writes=[r_gb[0]])
        dma("sp", GB[:, 1024:2048], gains_d[2], "gb1", writes=[r_gb[1]])
        if stop_after == "P1":
            P.frozen = True
        bk = [0]

        def nextbank():
            b = bk[0] % 8
            bk[0] += 1
            return b

        for g in range(4):
            r_uT[g].pending = list(r_x)
        sl = [wload(wcols(w_in_d, c * 512), reads=([] if c == 0 else [r_x[NT - 1]])) for c in range(4)]

        def proj_fm(c, cc, s, b, ev, lazy=False):
            wv = wslot_view(sl[c])
            ops = []
            for k in range(8):
                ops.append((lambda k=k: mm(bank(b), wv[:, k, cc * 128:(cc + 1) * 128], hT_span(k, s), k == 0, k == 7,
                                           [r_ws[sl[c]]] + r_hT[4 * s:4 * s + 4], [rbank[b]])))
            if c == 0:
                dst, rd, sc = uT(cc)[:, s * 512:(s + 1) * 512], r_uT[cc], 1.0
            elif c == 1:
                dst, rd, sc = QT[:, cc, s * 512:(s + 1) * 512], r_QT[cc][s], 0.125
            else:
                dst, rd, sc = KT[:, cc, s * 512:(s + 1) * 512], r_KT[cc][s], 1.0

            def evac():
                if ev == "act":
                    act(dst, bank(b), AF.Copy, [rbank[b]], [rd], scale=sc)
                elif sc != 1.0:
                    P.add("dve", lambda e: e.tensor_scalar(out=dst, in0=bank(b), scalar1=sc, scalar2=None, op0=ALU.mult),
                          [rbank[b]], [rd])
                else:
                    P.add("dve", lambda e: e.tensor_copy(dst, bank(b)), [rbank[b]], [rd])
            ops.append(evac)
            if lazy:
                return ops
            for o in ops:
                o()

        def proj_v(tt, b, ev, lazy=False):
            wv = wslot_view(sl[3])
            ops = []
            for k in range(8):
                ops.append((lambda k=k: mm(bank(b), hT_tile(k, tt), wv[:, k, :], k == 0, k == 7,
                                           [r_ws[sl[3]], r_hT[tt]], [rbank[b]])))
            dst = VV[:, tt, :]

            def evac():
                if ev == "act":
                    act(dst, bank(b), AF.Copy, [rbank[b]], [r_V[tt]])
                else:
                    P.add("dve", lambda e: e.tensor_copy(dst, bank(b)), [rbank[b]], [r_V[tt]])
            ops.append(evac)
            if lazy:
                return ops
            for o in ops:
                o()

        for s in range(4):
            for cc in range(4):
                proj_fm(0, cc, s, nextbank(), "act")
        NPRE = 2
        for s in range(NPRE):
            for c in (1, 2):
                for cc in range(4):
                    proj_fm(c, cc, s, nextbank(), "act")
            for tt in range(4 * s, 4 * s + 4):
                proj_v(tt, nextbank(), "act")
        deferred = {}
        for s in range(NPRE, 4):
            micro = []
            for c in (1, 2):
                for cc in range(4):
                    micro += proj_fm(c, cc, s, 7, "dve", lazy=True)
            for tt in range(4 * s, 4 * s + 4):
                micro += proj_v(tt, 7, "dve", lazy=True)
            deferred[s] = micro

        if stop_after == "P2":
            P.frozen = True
        def dve_tt(out, a, b_, op, reads, writes):
            P.add("dve", lambda e: e.tensor_tensor(out=out, in0=a, in1=b_, op=op), reads, writes)

        P.add("dve", lambda e: e.memset(SA[:, 0:8], 0.0), (), [r_SA])
        WIN = (2, 4, 8, 16)
        r_tmp = Res("tmp16")
        for g in range(4):
            u = uT(g)
            dve_tt(SA[:, 9:2056], u[:, 1:2048], u[:, 0:2047], ALU.add, [r_uT[g]], [r_SA])
            P.add("dve", (lambda u=u: (lambda e: e.tensor_copy(SA[:, 8:9], u[:, 0:1])))(), [r_uT[g]], [r_SA])
            cur, rcur = SA, r_SA
            if g == 1:
                S4 = RX[:, 0:2048]
                dve_tt(S4, SA[:, 8:2056], SA[:, 6:2054], ALU.add, [r_SA], [r_SB, r_uT[0]])
                cur, rcur = None, r_SB
                cv1 = S4
            if g >= 2:
                if g == 2:
                    P.add("dve", lambda e: e.memset(SB_[:, 0:8], 0.0), (), [r_SB, r_uT[0], r_uT[1]])
                dve_tt(SB_[:, 8:2056], SA[:, 8:2056], SA[:, 6:2054], ALU.add, [r_SA], [r_SB, r_uT[0], r_uT[1]])
                cur, rcur = SB_, r_SB
            if g >= 2:
                dve_tt(SA[:, 8:2056], SB_[:, 8:2056], SB_[:, 4:2052], ALU.add, [r_SB], [r_SA])
                cur, rcur = SA, r_SA
            if g >= 3:
                dve_tt(SB_[:, 8:2056], SA[:, 8:2056], SA[:, 0:2048], ALU.add, [r_SA], [r_SB])
                cur, rcur = SB_, r_SB
            w = WIN[g]
            cv = cv1 if g == 1 else cur[:, 8:2056]
            P.add("dve", (lambda cv=cv, u=u, g=g, w=w: (lambda e: e.scalar_tensor_tensor(
                out=PLT[:, g, :], in0=cv, scalar=1.0 / w, in1=u, op0=ALU.mult, op1=ALU.subtract)))(),
                [rcur, r_uT[g]], [r_PLT[g]])
            tmp = ST[:, 272:288]
            P.add("dve", (lambda cv=cv, g=g, tmp=tmp: (lambda e: e.tensor_tensor(
                out=tmp, in0=cv[:, 0:16], in1=ICN[:, g * 16:(g + 1) * 16], op=ALU.mult)))(),
                [rcur, r_icn], [r_tmp])
            P.add("dve", (lambda u=u, g=g, tmp=tmp: (lambda e: e.tensor_tensor(
                out=PLT[:, g, 0:16], in0=tmp, in1=u[:, 0:16], op=ALU.subtract)))(),
                [r_tmp, r_uT[g]], [r_PLT[g]])
        for g in range(4):
            for s in range(4):
                b = nextbank()
                mm(bank(b), WPM[:, g * 128:(g + 1) * 128], PLT[:, g, s * 512:(s + 1) * 512], True, True,
                   [r_wpm, r_PLT[g]], [rbank[b]])
                act(YPT[:, g, s * 512:(s + 1) * 512], bank(b), AF.Copy, [rbank[b], r_psc], [r_YPT[g][s]],
                    scale=PSC[:, g:g + 1])

        r_hs = [Res("hs%d" % i) for i in range(8)]
        for i in range(8):
            r_hs[i].pending = [r_ws[i // 2]]
        hs_n = [0]

        def hslot_view(i):
            return WS[:, i * 2048:(i + 1) * 2048].rearrange("p (k e) -> p k e", k=8)

        def hload(parts):
            i = hs_n[0] % 8
            hs_n[0] += 1
            v = hslot_view(i)
            oi = []
            for (k0, src) in parts:
                oi.append((v[:, k0:k0 + src.shape[1], :], src))
            dma2("pool", oi, "hs%d" % i, writes=[r_hs[i]])
            return i

        def p5_load(gq, which):
            c0 = gq * 256
            if which == "a":
                v = w_gate_d.rearrange("(k p) e -> p k e", p=128)
                return hload([(0, v[:, 0:4, c0:c0 + 256]), (4, v[:, 4:8, c0:c0 + 256])])
            if which == "b":
                v = w_gate_d.rearrange("(k p) e -> p k e", p=128)
                return hload([(0, v[:, 0:4, 1024 + c0:1024 + c0 + 256]), (4, v[:, 4:8, 1024 + c0:1024 + c0 + 256])])
            vp = w_brp_d.rearrange("(k p) e -> p k e", p=128)
            vs = w_brs_d.rearrange("(k p) e -> p k e", p=128)
            return hload([(0, vp[:, :, c0:c0 + 256]), (4, vs[:, :, c0:c0 + 256])])
        p5hs = {}

        def p5_prefetch():
            for gq in (0, 1):
                for wh in "abc":
                    p5hs[(gq, wh)] = p5_load(gq, wh)
            p5hs[(2, "a")] = p5_load(2, "a")
            p5hs[(2, "b")] = p5_load(2, "b")
        p5w0 = [None]

        if stop_after == "P3":
            P.frozen = True
        def f32v(off):
            return RX[:, off:off + 1024].rearrange("p (h n) -> p h n", h=2)

        def bf16v(off32):
            return RXb[:, 2 * off32:2 * off32 + 1024].rearrange("p (h n) -> p h n", h=2)
        E_ = [f32v(4096), f32v(5120)]
        ARG = [f32v(6144), f32v(7168)]
        R1f_a = R1[:].bitcast(F32)
        CCS = [f32v(8192), R1f_a[:, 12288:13312].rearrange("p (h n) -> p h n", h=2)]
        SP_ = [bf16v(9216), bf16v(9728)]
        ATT = [bf16v(14400), bf16v(14912)]
        r_E = [Res("E0"), Res("E1")]
        r_ARG = [Res("ARG0"), Res("ARG1")]
        r_CCS = [Res("CC0"), Res("CC1")]
        r_CC = r_CCS[0]
        r_CCS[1].pending = list(r_PLT)
        r_SP = [Res("SP0"), Res("SP1")]
        r_ATT = [Res("ATT0"), Res("ATT1")]
        old = r_uT + [r_SA, r_SB]
        for r in r_E + r_ARG + [r_CC] + r_SP:
            r.pending = list(old)
        for c in range(4):
            for s in range(4):
                r_OT[c][s].pending = list(old)

        ZP = [PP[0], PP[1]]
        r_Z = [[rbank[0], rbank[1]], [rbank[2], rbank[3]]]
        TP = PP[2]
        r_T = [rbank[4], rbank[5]]
        OACC = [bank(6), bank(6)]
        r_OACC = [rbank[6], rbank[6]]

        chains = []
        chain_id = 0
        for j in range(4):
            for hp in range(4):
                kbs = [4 * j + 3, 4 * j + 2, 4 * j + 1, 4 * j] + list(range(4 * j - 1, -1, -1))
                ch = []
                for i, kb in enumerate(kbs):
                    c0 = 128 * (kb - 4 * j) if kb >= 4 * j else 0
                    ch.append(dict(j=j, hp=hp, kb=kb, c0=c0, first=(i == 0), last=(i == len(kbs) - 1),
                                   diag=(kb >= 4 * j), chain=chain_id))
                chains.append(ch)
                chain_id += 1
        tiles = []
        for ch in chains:
            tiles.extend(ch)
        sched = {}
        for s_ in range(NPRE, 4):
            lo = 0 if s_ == NPRE else min(n for n, t in enumerate(tiles) if t["j"] == s_ - 1)
            hi = min(n for n, t in enumerate(tiles) if t["j"] == s_)
            micro = deferred[s_]
            for g, op_ in enumerate(micro):
                n = lo + (g * (hi - lo)) // len(micro)
                sched.setdefault(n, []).append(op_)
        last_sched = max(sched)
        NTL = len(tiles)

        def pv(t3, c0):
            return t3[:, :, c0:512]

        def z3(zb):
            return ZP[zb][:].rearrange("p (h n) -> p h n", h=2)

        def emit_qk(n):
            t = tiles[n]
            zb = n % 2
            j, hp, kb, c0 = t["j"], t["hp"], t["kb"], t["c0"]
            for hd in range(2):
                rows = slice(64 * hd, 64 * hd + 64)
                out = ZP[zb][:, hd * 512 + c0:hd * 512 + 512]
                lhsT = KT[rows, hp, kb * 128:(kb + 1) * 128]
                rhs = QT[rows, hp, j * 512 + c0:(j + 1) * 512]
                mm(out, lhsT, rhs, True, not t["diag"], [r_KT[hp][kb // 4], r_QT[hp][j]], [r_Z[zb][hd]])
            if t["diag"]:
                for hd in range(2):
                    out = ZP[zb][:, hd * 512 + c0:hd * 512 + c0 + 128]
                    mm(out, ident, maskneg, False, True, [r_cbf], [r_Z[zb][hd]])

        def emit_exp1(n):
            t = tiles[n]
            zb = n % 2
            act(pv(E_[zb], t["c0"]), pv(z3(zb), t["c0"]), AF.Exp, r_Z[zb], [r_E[zb]])

        def emit_ln(n):
            t = tiles[n]
            zb = n % 2
            act(pv(SP_[zb], t["c0"]), pv(E_[zb], t["c0"]), AF.Ln, [r_E[zb], r_st0], [r_SP[zb]], bias=ONEC)

        def emit_tri(n):
            t = tiles[n]
            zb = n % 2
            c0 = t["c0"]
            for hd in range(2):
                out = ZP[zb][:, hd * 512 + c0:hd * 512 + 512]
                mm(out, negtri, SP_[zb][:, hd, c0:512], False, True, [r_cbf, r_SP[zb]], [r_Z[zb][hd]], skip=True)
            for hd in range(2):
                out = TP[:, hd * 512 + c0:hd * 512 + 512]
                mm(out, ones, SP_[zb][:, hd, c0:512], True, True, [r_cbf, r_SP[zb]], [r_T[hd]])

        def emit_dve(n):
            t = tiles[n]
            zb = n % 2
            c0 = t["c0"]
            CC = CCS[t["chain"] % 2]
            rcc = r_CCS[t["chain"] % 2]
            if t["first"]:
                P.add("pool", (lambda CC=CC: (lambda e: e.memset(CC, 0.0)))(), (), [rcc])
            a = pv(ARG[zb], c0)
            z = pv(z3(zb), c0)
            c = pv(CC, c0)
            tp = pv(TP[:].rearrange("p (h n) -> p h n", h=2), c0)
            dve_tt(a, z, c, ALU.subtract, r_Z[zb] + [rcc], [r_ARG[zb]])
            if not t["last"]:
                dve_tt(c, tp, c, ALU.add, r_T + [rcc], [rcc])

        def emit_exp2(n):
            t = tiles[n]
            zb = n % 2
            act(pv(ATT[zb], t["c0"]), pv(ARG[zb], t["c0"]), AF.Exp, [r_ARG[zb]], [r_ATT[zb]])

        def emit_av(n):
            t = tiles[n]
            zb = n % 2
            c0, hp, kb, j = t["c0"], t["hp"], t["kb"], t["j"]
            ob = t["chain"] % 2
            for hd in range(2):
                out = OACC[ob][64 * hd:64 * hd + 64, c0:512]
                lhsT = VV[:, kb, (2 * hp + hd) * 64:(2 * hp + hd) * 64 + 64]
                mm(out, lhsT, ATT[zb][:, hd, c0:512], t["first"], t["last"], [r_V[kb], r_ATT[zb]], [r_OACC[ob]], skip=True)
            if t["last"]:
                dst = OT[:, hp, j * 512:(j + 1) * 512]
                src = OACC[ob]
                P.add("dve", lambda e: e.tensor_copy(dst, src), [r_OACC[ob]], [r_OT[hp][j]])

        emit_qk(0)
        for n in range(NTL + 2):
            if n + 1 < NTL:
                emit_qk(n + 1)
            if n < NTL:
                emit_exp1(n)
            if 2 <= n:
                emit_exp2(n - 2)
            if n < NTL:
                emit_ln(n)
                emit_tri(n)
                emit_dve(n)
            if 2 <= n:
                emit_av(n - 2)
            for op_ in sched.get(n, []):
                op_()
            if n == last_sched:
                p5_prefetch()

        if stop_after == "P4":
            P.frozen = True
        NXS = 6
        R1f = R1[:].bitcast(F32)
        XS = [R1f[:, 8192 + j * 1024:8192 + (j + 1) * 1024] for j in range(NXS)]
        r_xs = [Res("xs%d" % j) for j in range(NXS)]
        for j in range(NXS):
            r_xs[j].pending = r_V + r_PLT + [r_CCS[1]]
            dma("sp", XS[j], x_d[j * 128:(j + 1) * 128, :], "xs%d" % j, writes=[r_xs[j]])
        MT = R1[:, 0:16384].rearrange("p (c t) -> p c t", c=8)
        r_MT = [[Res("MT%d_%d" % (c, s)) for s in range(4)] for c in range(8)]
        oldq = [r for row in r_QT for r in row] + [r for row in r_KT for r in row]
        for c in range(8):
            for s in range(4):
                r_MT[c][s].pending = list(oldq)
        TM = [[RX[:, 4096 + (i * 4 + q) * 512:4096 + (i * 4 + q + 1) * 512] for q in range(4)] for i in range(2)]
        r_TM = [[Res("TM%d_%d" % (i, q)) for q in range(4)] for i in range(2)]
        olda = r_E + r_ARG + [r_CC] + r_SP
        for i in range(2):
            for q in range(4):
                r_TM[i][q].pending = list(olda)
        it = 0
        so = [None, None]
        for gq in range(4):
            ha, hb_, hc = p5hs[(gq, "a")], p5hs[(gq, "b")], p5hs[(gq, "c")]
            wa, wb, wc = hslot_view(ha), hslot_view(hb_), hslot_view(hc)
            for dcl in range(2):
                dc = 2 * gq + dcl
                cs = slice(dcl * 128, dcl * 128 + 128)
                for s in range(4):
                    pb = 4 * (it % 2)
                    ti = it % 2
                    it += 1
                    bgp, bgs, byp, bys = pb, pb + 1, pb + 2, pb + 3
                    for k in range(8):
                        mm(bank(bgp), wa[:, k, cs], hT_span(k, s), k == 0, k == 7,
                           [r_hs[ha]] + r_hT[4 * s:4 * s + 4], [rbank[bgp]])
                    for k in range(8):
                        mm(bank(bgs), wb[:, k, cs], hT_span(k, s), k == 0, k == 7,
                           [r_hs[hb_]] + r_hT[4 * s:4 * s + 4], [rbank[bgs]])
                    for k in range(4):
                        mm(bank(byp), wc[:, k, cs], YPT[:, k, s * 512:(s + 1) * 512], k == 0, k == 3,
                           [r_hs[hc], r_YPT[k][s]], [rbank[byp]])
                    for k in range(4):
                        mm(bank(bys), wc[:, 4 + k, cs], OT[:, k, s * 512:(s + 1) * 512], k == 0, k == 3,
                           [r_hs[hc], r_OT[k][s]], [rbank[bys]])
                    act(TM[ti][0], bank(bgp), AF.Sigmoid, [rbank[bgp], r_bgt], [r_TM[ti][0]], bias=BGT[:, dc:dc + 1])
                    act(TM[ti][1], bank(bgs), AF.Sigmoid, [rbank[bgs], r_bgt], [r_TM[ti][1]], bias=BGT[:, 8 + dc:9 + dc])
                    dve_tt(TM[ti][2], bank(byp), TM[ti][0], ALU.mult, [rbank[byp], r_TM[ti][0]], [r_TM[ti][2]])
                    dve_tt(TM[ti][3], bank(bys), TM[ti][1], ALU.mult, [rbank[bys], r_TM[ti][1]], [r_TM[ti][3]])
                    dve_tt(MT[:, dc, s * 512:(s + 1) * 512], TM[ti][2], TM[ti][3], ALU.add,
                           [r_TM[ti][2], r_TM[ti][3]], [r_MT[dc][s]])
            if gq == 0:
                p5hs[(2, "c")] = p5_load(2, "c")
                p5hs[(3, "a")] = p5_load(3, "a")
                p5hs[(3, "b")] = p5_load(3, "b")
            elif gq == 1:
                p5hs[(3, "c")] = p5_load(3, "c")
                assert hs_n[0] == 12
                for s_ in range(4):
                    r_ws[s_].pending = [r_hs[2 * s_], r_hs[2 * s_ + 1]]
                ws_n[0] = 6
                so[0] = wload(wcols(w_out_d, 0))
            elif gq == 2:
                so[1] = wload(wcols(w_out_d, 512))

        if stop_after == "P5":
            P.frozen = True
        oldrx = [r for row in r_YPT for r in row] + [r for row in r_OT for r in row] + \
                [r for row in r_TM for r in row] + r_ATT
        r_x1h = [[Res("x1_%d_0" % t), Res("x1_%d_1" % t)] for t in range(NT)]
        for tt in range(NT):
            r_x1h[tt][0].pending = list(oldrx)
            r_x1h[tt][1].pending = list(oldrx)
            if tt >= NXS:
                dma("sp", xt(tt), x_d[tt * 128:(tt + 1) * 128, :], "x%d" % tt, reads=[r_ws[so[1]]], writes=r_x1h[tt])
        r_h2T = [Res("h2T%d" % t) for t in range(NT)]
        for tt in range(NT):
            r_h2T[tt].pending = list(r_hT)
        ssA = stcol(32)
        ssS = stcol(16)
        lnA = stcol(16)
        rsA = stcol(16)
        ssB = stcol(16)
        lnB = stcol(16)
        rsB = stcol(16)
        r_stA = [Res("stA%d" % t) for t in range(NT)]
        r_stB = [Res("stB%d" % t) for t in range(NT)]
        HBS4 = HBS + [R1[:, 28672:29696], R1[:, 29696:30720]]
        r_hb4 = r_hb + [Res("hb2"), Res("hb3")]
        JK6 = R1[:, 30720:31232]
        r_jk6 = Res("jk6")
        for r in r_hb4[2:] + [r_jk6]:
            r.pending = r_V + r_PLT + [r_CCS[1]]

        JK6w = R1[:, 30720:31744]

        def p6_mm(T):
            b0 = 2 * (T % 3)
            for dh in range(2):
                wv = wslot_view(so[dh])
                for k in range(8):
                    mm(bank(b0 + dh), MT[:, k, T * 128:(T + 1) * 128], wv[:, k, :], k == 0, k == 7,
                       [r_ws[so[dh]], r_MT[k][T // 4]], [rbank[b0 + dh]])

        def p6_sqA(T):
            b0 = 2 * (T % 3)
            act(JK6w, PP[T % 3][:], AF.Square, [rbank[b0], rbank[b0 + 1]], [r_jk6, r_stA[T]], accum=ssS[:, T:T + 1])

        def p6_lnA(T):
            act(lnA[:, T:T + 1], ssS[:, T:T + 1], AF.Ln, [r_stA[T], r_st0], [r_stA[T]], bias=EPSC, scale=1.0 / D)

        def p6_expA(T):
            act(rsA[:, T:T + 1], lnA[:, T:T + 1], AF.Exp, [r_stA[T]], [r_stA[T]], scale=-0.5)

        def p6_res(T):
            b0 = 2 * (T % 3)
            for dh in range(2):
                P.add("dve", (lambda b=b0 + dh, T=T, dh=dh: (lambda e: e.scalar_tensor_tensor(
                    out=bank(b), in0=bank(b), scalar=rsA[:, T:T + 1], in1=GB[:, dh * 512:(dh + 1) * 512],
                    op0=ALU.mult, op1=ALU.mult)))(),
                    [rbank[b0 + dh], r_stA[T], r_gb[0]], [rbank[b0 + dh]])
            for dh in range(2):
                xo = xt(T)[:, dh * 512:(dh + 1) * 512]
                if T < NXS:
                    xi = XS[T][:, dh * 512:(dh + 1) * 512]
                    dve_tt(xo, bank(b0 + dh), xi, ALU.add, [rbank[b0 + dh], r_xs[T]], [r_x1h[T][dh]])
                else:
                    dve_tt(xo, bank(b0 + dh), xo, ALU.add, [rbank[b0 + dh], r_x1h[T][dh]], [r_x1h[T][dh]])

        def p6_sqB(T):
            act(HBS4[T % 4], xt(T), AF.Square, r_x1h[T], [r_hb4[T % 4], r_stB[T]], accum=ssB[:, T:T + 1])

        def p6_lnB(T):
            act(lnB[:, T:T + 1], ssB[:, T:T + 1], AF.Ln, [r_stB[T], r_st0], [r_stB[T]], bias=EPSC, scale=1.0 / D)

        def p6_expB(T):
            act(rsB[:, T:T + 1], lnB[:, T:T + 1], AF.Exp, [r_stB[T]], [r_stB[T]], scale=-0.5)

        def p6_h(T):
            hb = HBS4[T % 4]
            P.add("dve", (lambda hb=hb, T=T: (lambda e: e.scalar_tensor_tensor(
                out=hb, in0=xt(T), scalar=rsB[:, T:T + 1], in1=GB[:, 1024:2048], op0=ALU.mult, op1=ALU.mult)))(),
                r_x1h[T] + [r_stB[T], r_gb[1]], [r_hb4[T % 4]])

        def ok(T):
            return 0 <= T < NT

        for i in range(NT + 6):
            if ok(i - 5):
                norm_c(i - 5, r_h2T[i - 5], 6 + (i - 5) % 2, HBS4, r_hb4, evac="none")
            if ok(i):
                p6_mm(i)
            if ok(i - 1):
                p6_sqA(i - 1)
            if ok(i - 3):
                p6_sqB(i - 3)
            if ok(i - 1):
                p6_lnA(i - 1)
            if ok(i - 3):
                p6_lnB(i - 3)
            if ok(i - 1):
                p6_expA(i - 1)
            if ok(i - 3):
                p6_expB(i - 3)
            if ok(i - 6):
                norm_evac(i - 6, r_h2T[i - 6], 6 + (i - 6) % 2)
            if ok(i - 2):
                p6_res(i - 2)
            if ok(i - 4):
                p6_h(i - 4)

        if stop_after == "P6":
            P.frozen = True
        AT = R1[:].rearrange("p (c t) -> p c t", c=32)
        r_AT = [[Res("AT%d_%d" % (c, s)) for s in range(2)] for c in range(32)]
        oldm = [r for row in r_MT for r in row] + r_V + r_PLT + r_xs + r_hb4[2:] + [r_jk6, r_CCS[1]]
        dma("sp", GB[:, 0:1024], gains_d[3], "gb0", writes=[r_gb[0]])
        FFS = [R2[:, h * 4096:(h + 1) * 4096].rearrange("p (t e) -> p t e", t=8) for h in range(2)]
        ssC = stcol(32)
        ssT = stcol(16)
        lnC = stcol(16)
        rsC = stcol(16)
        r_stC = [Res("stC%d" % t) for t in range(NT)]
        r_rt = [Res("rt0"), Res("rt1")]
        for r in r_rt:
            r.pending = [r_gb[1]]
        for half in range(2):
            for c in range(32):
                for s in range(2):
                    r_AT[c][s].pending = list(oldm)
            oldm = []
            for f4 in range(8):
                su = wload(wcols(w_up_d, f4 * 512))
                wv = wslot_view(su)
                for fl in range(4):
                    fc = 4 * f4 + fl
                    for s in range(2):
                        b = nextbank()
                        for k in range(8):
                            mm(bank(b), wv[:, k, fl * 128:(fl + 1) * 128],
                               hT[:, half, k, s * 512:(s + 1) * 512], k == 0, k == 7,
                               [r_ws[su]] + r_h2T[8 * half + 4 * s:8 * half + 4 * s + 4], [rbank[b]])
                        dst = AT[:, fc, s * 512:(s + 1) * 512]
                        ti = b % 2
                        tmpr = GB[:, 1024 + ti * 512:1024 + (ti + 1) * 512]
                        act(tmpr, bank(b), AF.Relu, [rbank[b]], [r_rt[ti]])
                        dve_tt(dst, bank(b), tmpr, ALU.mult, [rbank[b], r_rt[ti]], [r_AT[fc][s]])
            if stop_after == "P7a":
                P.frozen = True
            r_ffs = [Res("ffs%d_%d" % (half, t)) for t in range(8)]
            for t in range(8):
                r_ffs[t].pending = list(r_h2T[8 * half:8 * half + 8])
            for dh in range(2):
                for f8 in range(4):
                    vd = w_down_d.rearrange("(c p) d -> p c d", p=128)
                    sd = wload([(0, vd[:, 8 * f8:8 * f8 + 4, dh * 512:(dh + 1) * 512]),
                                (4, vd[:, 8 * f8 + 4:8 * f8 + 8, dh * 512:(dh + 1) * 512])])
                    wv = wslot_view(sd)
                    if f8 == 3:
                        for t in range(8):
                            for fl in range(8):
                                fc = 8 * f8 + fl
                                mm(bank(t), AT[:, fc, t * 128:(t + 1) * 128], wv[:, fl, :], fc == 0, fc == 31,
                                   [r_ws[sd], r_AT[fc][t // 4]], [rbank[t]])
                    else:
                        for fl in range(8):
                            fc = 8 * f8 + fl
                            for t in range(8):
                                mm(bank(t), AT[:, fc, t * 128:(t + 1) * 128], wv[:, fl, :], fc == 0, fc == 31,
                                   [r_ws[sd], r_AT[fc][t // 4]], [rbank[t]])
                if stop_after == "P7b":
                    P.frozen = True
                def ev_stats(t, half=half, dh=dh):
                    tt = 8 * half + t
                    act(HBS[t % 2][:, 0:512], bank(t), AF.Square, [rbank[t]], [r_hb[t % 2], r_stC[tt]],
                        accum=ssC[:, 2 * tt + dh:2 * tt + dh + 1])
                    if dh == 0:
                        P.add("dve", (lambda t=t, half=half: (lambda e: e.tensor_copy(FFS[half][:, t, :], bank(t))))(),
                              [rbank[t]], [r_ffs[t]])
                    else:
                        dve_tt(ssT[:, tt:tt + 1], ssC[:, 2 * tt:2 * tt + 1], ssC[:, 2 * tt + 1:2 * tt + 2], ALU.add,
                               [r_stC[tt]], [r_stC[tt]])
                        act(lnC[:, tt:tt + 1], ssT[:, tt:tt + 1], AF.Ln, [r_stC[tt], r_st0], [r_stC[tt]],
                            bias=EPSC, scale=1.0 / D)
                        act(rsC[:, tt:tt + 1], lnC[:, tt:tt + 1], AF.Exp, [r_stC[tt]], [r_stC[tt]], scale=-0.5)

                def ev_big(t, half=half):
                    tt = 8 * half + t
                    for d2 in range(2):
                        src = FFS[half][:, t, :] if d2 == 0 else bank(t)
                        rsrc = r_ffs[t] if d2 == 0 else rbank[t]
                        P.add("dve", (lambda src=src, tt=tt, d2=d2: (lambda e: e.scalar_tensor_tensor(
                            out=src, in0=src, scalar=rsC[:, tt:tt + 1], in1=GB[:, d2 * 512:(d2 + 1) * 512],
                            op0=ALU.mult, op1=ALU.mult)))(),
                            [rsrc, r_stC[tt], r_gb[0]], [rsrc])
                    for d2 in range(2):
                        src = FFS[half][:, t, :] if d2 == 0 else bank(t)
                        rsrc = r_ffs[t] if d2 == 0 else rbank[t]
                        xs = xt(tt)[:, d2 * 512:(d2 + 1) * 512]
                        P.add("dve", (lambda xs=xs, src=src: (lambda e: e.tensor_tensor(out=xs, in0=src, in1=xs, op=ALU.add)))(),
                              [rsrc, r_x1h[tt][d2]], [r_x1h[tt][d2]])
                    if stop_after != "P7d":
                        dma("sp", out_d[tt * 128:(tt + 1) * 128, :], xt(tt), "st%d" % tt, reads=r_x1h[tt])

                for t in range(9):
                    if t < 8:
                        ev_stats(t)
                    if dh == 1 and t >= 1:
                        ev_big(t - 1)
                if stop_after == "P7c" and dh == 0:
                    P.frozen = True

        P.frozen = False
        if debug:
            dbg_src = {
                "hT": (R2b, []), "R1": (R1[:], []), "RX": (RX[:], []),
            }
            for name, _, _ in debug:
                src, _r = dbg_src[name]
                P.add("sp", (lambda src=src, name=name: (lambda e: [e.dma_start(out=dbg_d[name], in_=src)]))(),
                      [], [], dma_key="dbg_" + name, ninst=1, barrier=True)

        P.finalize()
        for key in P.dma_cnt:
            sem_dma[key] = es.enter_context(nc.semaphore("sd_" + key))
        final_waits = [(k, v) for k, v in P.dma_cnt.items() if k.startswith("st") or k.startswith("dbg_")]

        def emit_engine(name, e):
            waited = {}
            for op in P.ops:
                if op.eng != name:
                    continue
                need = {}
                for d in op.deps:
                    key = ("d", d.dma_key) if d.dma_key is not None else ("e", d.eng)
                    if d.sigval > need.get(key, 0):
                        need[key] = d.sigval
                todo = []
                for key, val in need.items():
                    if waited.get(key, 0) >= val:
                        continue
                    waited[key] = val
                    todo.append((sem_dma[key[1]] if key[0] == "d" else sem_eng[key[1]], val))
                embed = None
                if todo and op.dma_key is None and name in ("act", "dve", "pool") and not op.multi:
                    embed = todo.pop()
                for sem, val in todo:
                    e.wait_ge(sem, val)
                res = op.fn(e)
                if embed is not None:
                    res._wait_ge(embed[0], embed[1])
                if op.dma_key is not None:
                    for inst in res:
                        inst.then_inc(sem_dma[op.dma_key], 16)
                elif op.signal:
                    res.then_inc(sem_eng[name], 1)
            if name == "sp":
                if debug:
                    for en in ("pe", "act", "dve", "pool"):
                        pass
                for k, v in final_waits:
                    e.wait_ge(sem_dma[k], v)

        with nc.Block() as block:
            @block.tensor
            def _(e):
                emit_engine("pe", e)

            @block.scalar
            def _(e):
                emit_engine("act", e)

            @block.vector
            def _(e):
                emit_engine("dve", e)

            @block.gpsimd
            def _(e):
                emit_engine("pool", e)

            @block.sync
            def _(e):
                emit_engine("sp", e)
    return nc


_NC_CACHE = {}


def _consts():
    bf = ml_dtypes.bfloat16
    p = np.arange(128)[:, None]
    c = np.arange(128)[None, :]
    ident = (p == c).astype(np.float32)
    negtri = -(p >= c).astype(np.float32)
    ones = np.ones((128, 128), np.float32)
    maskneg = np.where(p < c, 0.0, NEG).astype(np.float32)
    cbf = np.concatenate([ident, negtri, ones, maskneg], axis=1).astype(bf)
    invcnt = np.zeros((128, 64), np.float32)
    for g, w in enumerate((2, 4, 8, 16)):
        t = np.arange(16)
        invcnt[:, g * 16:(g + 1) * 16] = (1.0 / np.minimum(t + 1, w))[None, :]
    return cbf, invcnt


def kernel(x, g_pre_mix, w_in, w_pool_mix, pool_scale, w_br_pool, w_br_sb, w_gate, b_gate,
           w_out, g_post_mix, g_pre_mlp, w_up, w_down, g_post_mlp, _debug=None, _stop=None):
    f = lambda a: np.ascontiguousarray(np.asarray(a, dtype=np.float32))
    x = f(x)
    B = x.shape[0]
    key = "dbg" if _debug else "main"
    if key not in _NC_CACHE:
        _NC_CACHE[key] = build_nc(_debug, _stop)
    nc = _NC_CACHE[key]
    cbf, invcnt = _consts()
    gains = np.stack([np.broadcast_to(f(g)[None, :], (128, D)) for g in
                      (g_pre_mix, g_post_mix, g_pre_mlp, g_post_mlp)]).copy()
    pscale = np.ascontiguousarray(f(pool_scale).reshape(4, 128).T)
    bgate = np.ascontiguousarray(f(b_gate).reshape(16, 128).T)
    shared = {
        "w_in": f(w_in), "w_gate": f(w_gate), "w_br_pool": f(w_br_pool), "w_br_sb": f(w_br_sb),
        "w_out": f(w_out), "w_up": f(w_up), "w_down": f(w_down), "w_pool_mix": f(w_pool_mix),
        "gains": gains, "pscale": pscale, "bgate": bgate, "invcnt": invcnt, "cbf": cbf,
    }
    in_maps = [dict(shared, x=x[b]) for b in range(B)]
    res = run_bass_kernel_spmd(nc, in_maps, core_ids=list(range(B)))
    out = np.stack([np.asarray(r["out"], dtype=np.float32) for r in res.results], axis=0)
    if _debug:
        return out, [{k: np.asarray(v) for k, v in r.items()} for r in res.results]
    return out
```

```python
import os
import numpy as np
import ml_dtypes
from contextlib import ExitStack
import concourse.bass as bass
import concourse.mybir as mybir
from concourse.bass_utils import run_bass_kernel_spmd

F32 = mybir.dt.float32
BF16 = mybir.dt.bfloat16
AF = mybir.ActivationFunctionType
ALU = mybir.AluOpType

S = 2048
D = 1024
NT = S // 128
NEG = -30000.0
EPS = 1e-6


class Res:
    __slots__ = ("name", "w", "rs", "pending", "excl")

    def __init__(self, name, excl=False):
        self.name = name
        self.excl = excl
        self.w = None
        self.rs = {}
        self.pending = []


class Op:
    __slots__ = ("idx", "eng", "fn", "dma_key", "ninst", "deps", "signal", "sigval", "multi", "waits")

    def __init__(self, idx, eng, fn, dma_key, ninst):
        self.idx = idx
        self.eng = eng
        self.fn = fn
        self.dma_key = dma_key
        self.ninst = ninst
        self.deps = set()
        self.signal = False
        self.sigval = 0
        self.multi = False
        self.waits = []


class Prog:
    ENGS = ("pe", "act", "dve", "pool", "sp")

    def __init__(self):
        self.ops = []
        self.dma_cnt = {}
        self.frozen = False

    def add(self, eng, fn, reads=(), writes=(), dma_key=None, ninst=1, barrier=False):
        if self.frozen:
            return None
        op = Op(len(self.ops), eng, fn, dma_key, ninst)
        if barrier:
            last = {}
            for o in self.ops:
                last[(o.eng, o.dma_key)] = o
            for o in last.values():
                op.deps.add(o)
        is_dma = dma_key is not None
        writes = list(writes)
        extra = []
        for w in writes:
            if w.pending:
                extra.extend(w.pending)
                w.pending = []
        deps = []
        for r in reads:
            if r.w is not None:
                deps.append((r.w, "raw"))
            if r.excl:
                for rd in r.rs.values():
                    if rd.eng != eng:
                        deps.append((rd, "raw"))
        for w in writes + extra:
            if w.w is not None:
                deps.append((w.w, "waw"))
            for rd in w.rs.values():
                deps.append((rd, "war"))
        for d, kind in deps:
            if d is op:
                continue
            d_dma = d.dma_key is not None
            if (not d_dma) and (not is_dma) and d.eng == eng and eng == "pe":
                continue
            op.deps.add(d)
        for r in reads:
            r.rs[(eng, dma_key)] = op
        for w in writes:
            w.w = op
            w.rs = {}
        if is_dma:
            self.dma_cnt[dma_key] = self.dma_cnt.get(dma_key, 0) + 16 * ninst
            op.sigval = self.dma_cnt[dma_key]
        self.ops.append(op)
        return op

    def finalize(self):
        for op in self.ops:
            for d in op.deps:
                if d.dma_key is None:
                    d.signal = True
        cnt = {e: 0 for e in self.ENGS}
        for op in self.ops:
            if op.dma_key is None and op.signal:
                cnt[op.eng] += 1
                op.sigval = cnt[op.eng]


    def plan_waits(self):
        waited = {e: {} for e in self.ENGS}
        know = {}
        for op in self.ops:
            w = waited[op.eng]
            need = {}
            for d in op.deps:
                key = ("d", d.dma_key) if d.dma_key is not None else ("e", d.eng)
                if d.sigval > need.get(key, 0):
                    need[key] = d.sigval
            op.waits = []
            for key, val in sorted(need.items(), key=lambda kv: str(kv[0])):
                if w.get(key, 0) >= val:
                    continue
                op.waits.append((key, val))
                w[key] = val
                for k2, v2 in know.get((key, val), {}).items():
                    if v2 > w.get(k2, 0):
                        w[k2] = v2
            if op.dma_key is not None:
                know[(("d", op.dma_key), op.sigval)] = dict(w)
            elif op.signal:
                kk = dict(w)
                kk[("e", op.eng)] = max(kk.get(("e", op.eng), 0), op.sigval)
                know[(("e", op.eng), op.sigval)] = kk


def build_nc(debug=None, stop_after=None):
    nc = bass.Bass("TRN2", target_bir_lowering=False)
    P = Prog()

    def dram_in(name, shape, dt=F32):
        return nc.dram_tensor(name, list(shape), dt, kind="ExternalInput").ap()

    x_d = dram_in("x", [S, D])
    w_in_d = dram_in("w_in", [D, 2048])
    w_gate_d = dram_in("w_gate", [D, 2048])
    w_brp_d = dram_in("w_br_pool", [512, D])
    w_brs_d = dram_in("w_br_sb", [512, D])
    w_out_d = dram_in("w_out", [D, D])
    w_up_d = dram_in("w_up", [D, 4096])
    w_down_d = dram_in("w_down", [4096, D])
    wpm_d = dram_in("w_pool_mix", [4, 128, 128])
    gains_d = dram_in("gains", [4, 128, D])
    pscale_d = dram_in("pscale", [128, 4])
    bgate_d = dram_in("bgate", [128, 16])
    invcnt_d = dram_in("invcnt", [128, 64])
    cbf_d = dram_in("cbf", [128, 512], BF16)
    out_d = nc.dram_tensor("out", [S, D], F32, kind="ExternalOutput").ap()
    dbg_d = {}
    if debug:
        for name, shape, dt in debug:
            dbg_d[name] = nc.dram_tensor("dbg_" + name, list(shape), dt, kind="ExternalOutput").ap()

    es = ExitStack()
    with es:
        def sb(name, shape, dt):
            return es.enter_context(nc.sbuf_tensor(name, list(shape), dt))

        RX = sb("RX", [128, 16384], F32)
        R1 = sb("R1", [128, 32768], BF16)
        R2 = sb("R2", [128, 8192], F32)
        WS = sb("WS", [128, 16384], BF16)
        GB = sb("GB", [128, 2048], F32)
        HB = sb("HB", [128, 1024], BF16)
        JK = sb("JK", [128, 1024], BF16)
        CBF = sb("CBF", [128, 512], BF16)
        WPM = sb("WPM", [128, 512], BF16)
        PSC = sb("PSC", [128, 4], F32)
        BGT = sb("BGT", [128, 16], F32)
        ICN = sb("ICN", [128, 64], F32)
        ST = sb("ST", [128, 288], F32)
        PP = [es.enter_context(nc.psum_tensor("pp%d" % i, [128, 1024], F32)) for i in range(4)]

        sem_eng = {e: es.enter_context(nc.semaphore("se_" + e)) for e in Prog.ENGS}
        sem_dma = {}

        def bank(i):
            return PP[i // 2][:, (i % 2) * 512:(i % 2) * 512 + 512]

        rbank = [Res("bank%d" % i, excl=True) for i in range(8)]
        ident = CBF[:, 0:128]
        negtri = CBF[:, 128:256]
        ones = CBF[:, 256:384]
        maskneg = CBF[:, 384:512]

        EPSC = ST[:, 0:1]
        ONEC = ST[:, 1:2]
        st_next = [2]

        def stcol(n=1):
            c = st_next[0]
            st_next[0] += n
            assert st_next[0] <= 272
            return ST[:, c:c + n]

        def dma(eng, out, in_, key, reads=(), writes=()):
            P.add(eng, lambda e: [e.dma_start(out=out, in_=in_)], reads, writes, dma_key=key, ninst=1)

        def dma2(eng, outs_ins, key, reads=(), writes=()):
            def fn(e):
                return [e.dma_start(out=o, in_=i) for (o, i) in outs_ins]
            P.add(eng, fn, reads, writes, dma_key=key, ninst=len(outs_ins))

        def mm(out, lhsT, rhs, start, stop, reads, writes, skip=False):
            if skip:
                P.add("pe", lambda e: e.matmul(out, lhsT, rhs, start=start, stop=stop, skip_group_check=True), reads, writes)
            else:
                P.add("pe", lambda e: e.matmul(out, lhsT, rhs, start=start, stop=stop), reads, writes)

        def act(out, in_, func, reads, writes, bias=None, scale=1.0, accum=None):
            kw = {}
            if bias is not None:
                kw["bias"] = bias
            if accum is not None:
                kw["accum_out"] = accum
            o_ = P.add("act", lambda e: e.activation(out, in_, func, scale=scale, **kw), reads, writes)
            if o_ is not None and accum is not None:
                o_.multi = True

        def junk():
            return Res("junk")

        r_cbf = Res("cbf")
        r_wpm = Res("wpm")
        r_psc = Res("psc")
        r_bgt = Res("bgt")
        r_icn = Res("icn")
        r_st0 = Res("st0")
        r_gb = [Res("gb0"), Res("gb1")]
        dma("sp", CBF[:], cbf_d, "cbf", writes=[r_cbf])
        dma("sp", GB[:, 0:1024], gains_d[0], "gb0", writes=[r_gb[0]])
        dma("sp", PSC[:], pscale_d, "psc", writes=[r_psc])
        dma("sp", BGT[:], bgate_d, "bgt", writes=[r_bgt])
        dma("sp", ICN[:], invcnt_d, "icn", writes=[r_icn])
        P.add("pool", lambda e: e.memset(EPSC, EPS), (), [r_st0])
        P.add("pool", lambda e: e.memset(ONEC, 1.0), (), [r_st0])
        dma("pool", WPM[:].rearrange("p (g d) -> p g d", g=4), wpm_d.rearrange("g p d -> p g d"), "wpm", writes=[r_wpm])

        r_ws = [Res("ws%d" % i) for i in range(4)]
        ws_n = [0]

        def wslot_view(s):
            return WS[:, s * 4096:(s + 1) * 4096].rearrange("p (k e) -> p k e", k=8)

        def wload(parts, reads=()):
            s = ws_n[0] % 4
            ws_n[0] += 1
            v = wslot_view(s)
            oi = []
            for (k0, src) in parts:
                kk = src.shape[1]
                oi.append((v[:, k0:k0 + kk, :], src))
            dma2("pool", oi, "ws%d" % s, reads=reads, writes=[r_ws[s]])
            return s

        def wcols(wd, c0, k=8):
            v = wd.rearrange("(k p) e -> p k e", p=128)
            return [(0, v[:, 0:k // 2, c0:c0 + 512]), (k // 2, v[:, k // 2:k, c0:c0 + 512])]

        RXb = RX[:].bitcast(BF16)
        R2b = R2[:].bitcast(BF16)
        hT = R2b.rearrange("p (h k t) -> p h k t", h=2, k=8)

        def hT_span(k, s):
            return hT[:, s // 2, k, (s % 2) * 512:(s % 2) * 512 + 512]

        def hT_tile(k, tt):
            return hT[:, tt // 8, k, (tt % 8) * 128:(tt % 8) * 128 + 128]

        r_hT = [Res("hT%d" % t) for t in range(NT)]
        r_x = [Res("x%d" % t) for t in range(NT)]

        def xt(tt):
            return RX[:, tt * 1024:(tt + 1) * 1024]

        QT = R1[:, 0:8192].rearrange("p (c t) -> p c t", c=4)
        KT = R1[:, 8192:16384].rearrange("p (c t) -> p c t", c=4)
        VV = R1[:, 16384:24576].rearrange("p (t e) -> p t e", t=16)
        PLT = R1[:, 24576:32768].rearrange("p (c t) -> p c t", c=4)
        r_QT = [[Res("QT%d_%d" % (c, s)) for s in range(4)] for c in range(4)]
        r_KT = [[Res("KT%d_%d" % (c, s)) for s in range(4)] for c in range(4)]
        r_V = [Res("V%d" % t) for t in range(NT)]
        r_PLT = [Res("PLT%d" % g) for g in range(4)]

        def uT(g):
            return RX[:, g * 2048:(g + 1) * 2048]
        r_uT = [Res("uT%d" % g) for g in range(4)]
        SA = RX[:, 8192:10248]
        SB_ = RX[:, 0:2056]
        r_SA = Res("SA")
        r_SB = Res("SB")
        YPT = RXb[:, 20608:28800].rearrange("p (c t) -> p c t", c=4)
        r_YPT = [[Res("YPT%d_%d" % (g, s)) for s in range(4)] for g in range(4)]
        OT = RXb[:, 0:8192].rearrange("p (c t) -> p c t", c=4)
        r_OT = [[Res("OT%d_%d" % (c, s)) for s in range(4)] for c in range(4)]

        ss1 = stcol(16)
        ln1 = stcol(16)
        rs1 = stcol(16)
        r_ss1 = [Res("ss1_%d" % t) for t in range(NT)]
        r_hb = [Res("hb0"), Res("hb1")]

        HBS = [HB[:, 0:1024], JK[:, 0:1024]]

        def norm_b(tt, xin, r_xin, gb_ap, r_gbx, ssc, lnc, rsc, r_stat, hbs=None, rhbs=None):
            hbs = hbs or HBS
            rhbs = rhbs or r_hb
            hb = hbs[tt % len(hbs)]
            rhb = rhbs[tt % len(hbs)]
            rxl = list(r_xin) if isinstance(r_xin, list) else [r_xin]
            act(hb, xin, AF.Square, rxl, [rhb, r_stat], accum=ssc)
            act(lnc, ssc, AF.Ln, [r_stat, r_st0], [r_stat], bias=EPSC, scale=1.0 / D)
            act(rsc, lnc, AF.Exp, [r_stat], [r_stat], scale=-0.5)
            P.add("dve", lambda e: e.scalar_tensor_tensor(out=hb, in0=xin, scalar=rsc, in1=gb_ap,
                                                          op0=ALU.mult, op1=ALU.mult),
                  rxl + [r_stat, r_gbx], [rhb])

        def norm_c(tt, dst_res, b, hbs=None, rhbs=None, evac="dve"):
            hbs = hbs or HBS
            rhbs = rhbs or r_hb
            hb = hbs[tt % len(hbs)]
            rhb = rhbs[tt % len(hbs)]
            pb = bank(b).bitcast(BF16)
            for k in range(8):
                o = pb[:, k * 128:(k + 1) * 128]
                i_ = hb[:, k * 128:(k + 1) * 128]
                P.add("pe", (lambda o=o, i_=i_: (lambda e: e.transpose(o, i_, ident)))(),
                      [rhb, r_cbf], [rbank[b]])
            dst = hT[:, tt // 8, :, (tt % 8) * 128:(tt % 8) * 128 + 128]
            src = pb.rearrange("p (k t) -> p k t", k=8)
            if evac == "dve":
                P.add("dve", lambda e: e.tensor_copy(dst, src), [rbank[b]], [dst_res])
            elif evac == "act":
                act(dst, src, AF.Copy, [rbank[b]], [dst_res])

        def norm_evac(tt, dst_res, b):
            pb = bank(b).bitcast(BF16)
            dst = hT[:, tt // 8, :, (tt % 8) * 128:(tt % 8) * 128 + 128]
            src = pb.rearrange("p (k t) -> p k t", k=8)
            act(dst, src, AF.Copy, [rbank[b]], [dst_res])

        for tt in range(NT):
            dma("sp", xt(tt), x_d[tt * 128:(tt + 1) * 128, :], "x%d" % tt, writes=[r_x[tt]])
        for i in range(NT + 1):
            if i < NT:
                norm_b(i, xt(i), r_x[i], GB[:, 0:1024], r_gb[0],
                       ss1[:, i:i + 1], ln1[:, i:i + 1], rs1[:, i:i + 1], r_ss1[i])
            if i >= 1:
                norm_c(i - 1, r_hT[i - 1], (i - 1) % 8)

        dma("sp", GB[:, 0:1024], gains_d[1], "gb0", writes=[r_gb[0]])
        dma("sp", GB[:, 1024:2048], gains_d[2], "gb1", writes=[r_gb[1]])
        if stop_after == "P1":
            P.frozen = True
        bk = [0]

        def nextbank():
            b = bk[0] % 8
            bk[0] += 1
            return b

        for g in range(4):
            r_uT[g].pending = list(r_x)
        sl = [wload(wcols(w_in_d, c * 512), reads=([] if c == 0 else [r_x[NT - 1]])) for c in range(4)]

        def proj_fm(c, cc, s, b, ev, lazy=False):
            wv = wslot_view(sl[c])
            ops = []
            for k in range(8):
                ops.append((lambda k=k: mm(bank(b), wv[:, k, cc * 128:(cc + 1) * 128], hT_span(k, s), k == 0, k == 7,
                                           [r_ws[sl[c]]] + r_hT[4 * s:4 * s + 4], [rbank[b]])))
            if c == 0:
                dst, rd, sc = uT(cc)[:, s * 512:(s + 1) * 512], r_uT[cc], 1.0
            elif c == 1:
                dst, rd, sc = QT[:, cc, s * 512:(s + 1) * 512], r_QT[cc][s], 0.125
            else:
                dst, rd, sc = KT[:, cc, s * 512:(s + 1) * 512], r_KT[cc][s], 1.0

            def evac():
                if ev == "act":
                    act(dst, bank(b), AF.Copy, [rbank[b]], [rd], scale=sc)
                elif sc != 1.0:
                    P.add("dve", lambda e: e.tensor_scalar(out=dst, in0=bank(b), scalar1=sc, scalar2=None, op0=ALU.mult),
                          [rbank[b]], [rd])
                else:
                    P.add("dve", lambda e: e.tensor_copy(dst, bank(b)), [rbank[b]], [rd])
            ops.append(evac)
            if lazy:
                return ops
            for o in ops:
                o()

        def proj_v(tt, b, ev, lazy=False):
            wv = wslot_view(sl[3])
            ops = []
            for k in range(8):
                ops.append((lambda k=k: mm(bank(b), hT_tile(k, tt), wv[:, k, :], k == 0, k == 7,
                                           [r_ws[sl[3]], r_hT[tt]], [rbank[b]])))
            dst = VV[:, tt, :]

            def evac():
                if ev == "act":
                    act(dst, bank(b), AF.Copy, [rbank[b]], [r_V[tt]])
                else:
                    P.add("dve", lambda e: e.tensor_copy(dst, bank(b)), [rbank[b]], [r_V[tt]])
            ops.append(evac)
            if lazy:
                return ops
            for o in ops:
                o()

        for s in range(4):
            for cc in range(4):
                proj_fm(0, cc, s, nextbank(), "act")
        NPRE = 2
        for s in range(NPRE):
            for c in (1, 2):
                for cc in range(4):
                    proj_fm(c, cc, s, nextbank(), "act")
            for tt in range(4 * s, 4 * s + 4):
                proj_v(tt, nextbank(), "act")
        deferred = {}
        for s in range(NPRE, 4):
            micro = []
            for c in (1, 2):
                for cc in range(4):
                    micro += proj_fm(c, cc, s, 7, "dve", lazy=True)
            for tt in range(4 * s, 4 * s + 4):
                micro += proj_v(tt, 7, "dve", lazy=True)
            deferred[s] = micro

        if stop_after == "P2":
            P.frozen = True
        def dve_tt(out, a, b_, op, reads, writes):
            P.add("dve", lambda e: e.tensor_tensor(out=out, in0=a, in1=b_, op=op), reads, writes)

        P.add("dve", lambda e: e.memset(SA[:, 0:8], 0.0), (), [r_SA])
        WIN = (2, 4, 8, 16)
        r_tmp = Res("tmp16")
        for g in range(4):
            u = uT(g)
            dve_tt(SA[:, 9:2056], u[:, 1:2048], u[:, 0:2047], ALU.add, [r_uT[g]], [r_SA])
            P.add("dve", (lambda u=u: (lambda e: e.tensor_copy(SA[:, 8:9], u[:, 0:1])))(), [r_uT[g]], [r_SA])
            cur, rcur = SA, r_SA
            if g == 1:
                S4 = RX[:, 0:2048]
                dve_tt(S4, SA[:, 8:2056], SA[:, 6:2054], ALU.add, [r_SA], [r_SB, r_uT[0]])
                cur, rcur = None, r_SB
                cv1 = S4
            if g >= 2:
                if g == 2:
                    P.add("dve", lambda e: e.memset(SB_[:, 0:8], 0.0), (), [r_SB, r_uT[0], r_uT[1]])
                dve_tt(SB_[:, 8:2056], SA[:, 8:2056], SA[:, 6:2054], ALU.add, [r_SA], [r_SB, r_uT[0], r_uT[1]])
                cur, rcur = SB_, r_SB
            if g >= 2:
                dve_tt(SA[:, 8:2056], SB_[:, 8:2056], SB_[:, 4:2052], ALU.add, [r_SB], [r_SA])
                cur, rcur = SA, r_SA
            if g >= 3:
                dve_tt(SB_[:, 8:2056], SA[:, 8:2056], SA[:, 0:2048], ALU.add, [r_SA], [r_SB])
                cur, rcur = SB_, r_SB
            w = WIN[g]
            cv = cv1 if g == 1 else cur[:, 8:2056]
            P.add("dve", (lambda cv=cv, u=u, g=g, w=w: (lambda e: e.scalar_tensor_tensor(
                out=PLT[:, g, :], in0=cv, scalar=1.0 / w, in1=u, op0=ALU.mult, op1=ALU.subtract)))(),
                [rcur, r_uT[g]], [r_PLT[g]])
            tmp = ST[:, 272:288]
            P.add("dve", (lambda cv=cv, g=g, tmp=tmp: (lambda e: e.tensor_tensor(
                out=tmp, in0=cv[:, 0:16], in1=ICN[:, g * 16:(g + 1) * 16], op=ALU.mult)))(),
                [rcur, r_icn], [r_tmp])
            P.add("dve", (lambda u=u, g=g, tmp=tmp: (lambda e: e.tensor_tensor(
                out=PLT[:, g, 0:16], in0=tmp, in1=u[:, 0:16], op=ALU.subtract)))(),
                [r_tmp, r_uT[g]], [r_PLT[g]])
        for g in range(4):
            for s in range(4):
                b = nextbank()
                mm(bank(b), WPM[:, g * 128:(g + 1) * 128], PLT[:, g, s * 512:(s + 1) * 512], True, True,
                   [r_wpm, r_PLT[g]], [rbank[b]])
                act(YPT[:, g, s * 512:(s + 1) * 512], bank(b), AF.Copy, [rbank[b], r_psc], [r_YPT[g][s]],
                    scale=PSC[:, g:g + 1])

        r_hs = [Res("hs%d" % i) for i in range(8)]
        for i in range(8):
            r_hs[i].pending = [r_ws[i // 2]]
        hs_n = [0]

        def hslot_view(i):
            return WS[:, i * 2048:(i + 1) * 2048].rearrange("p (k e) -> p k e", k=8)

        def hload(parts):
            i = hs_n[0] % 8
            hs_n[0] += 1
            v = hslot_view(i)
            oi = []
            for (k0, src) in parts:
                oi.append((v[:, k0:k0 + src.shape[1], :], src))
            dma2("pool", oi, "hs%d" % i, writes=[r_hs[i]])
            return i

        def p5_load(gq, which):
            c0 = gq * 256
            if which == "a":
                v = w_gate_d.rearrange("(k p) e -> p k e", p=128)
                return hload([(0, v[:, 0:4, c0:c0 + 256]), (4, v[:, 4:8, c0:c0 + 256])])
            if which == "b":
                v = w_gate_d.rearrange("(k p) e -> p k e", p=128)
                return hload([(0, v[:, 0:4, 1024 + c0:1024 + c0 + 256]), (4, v[:, 4:8, 1024 + c0:1024 + c0 + 256])])
            vp = w_brp_d.rearrange("(k p) e -> p k e", p=128)
            vs = w_brs_d.rearrange("(k p) e -> p k e", p=128)
            return hload([(0, vp[:, :, c0:c0 + 256]), (4, vs[:, :, c0:c0 + 256])])
        p5hs = {}

        def p5_prefetch():
            for gq in (0, 1):
                for wh in "abc":
                    p5hs[(gq, wh)] = p5_load(gq, wh)
            p5hs[(2, "a")] = p5_load(2, "a")
            p5hs[(2, "b")] = p5_load(2, "b")
        p5w0 = [None]

        if stop_after == "P3":
            P.frozen = True
        def f32v(off):
            return RX[:, off:off + 1024].rearrange("p (h n) -> p h n", h=2)

        def bf16v(off32):
            return RXb[:, 2 * off32:2 * off32 + 1024].rearrange("p (h n) -> p h n", h=2)
        E_ = [f32v(4096), f32v(5120)]
        ARG = [f32v(6144), f32v(7168)]
        R1f_a = R1[:].bitcast(F32)
        CCS = [f32v(8192), R1f_a[:, 12288:13312].rearrange("p (h n) -> p h n", h=2)]
        SP_ = [bf16v(9216), bf16v(9728)]
        ATT = [bf16v(14400), bf16v(14912)]
        r_E = [Res("E0"), Res("E1")]
        r_ARG = [Res("ARG0"), Res("ARG1")]
        r_CCS = [Res("CC0"), Res("CC1")]
        r_CC = r_CCS[0]
        r_CCS[1].pending = list(r_PLT)
        r_SP = [Res("SP0"), Res("SP1")]
        r_ATT = [Res("ATT0"), Res("ATT1")]
        old = r_uT + [r_SA, r_SB]
        for r in r_E + r_ARG + [r_CC] + r_SP:
            r.pending = list(old)
        for c in range(4):
            for s in range(4):
                r_OT[c][s].pending = list(old)

        ZP = [PP[0], PP[1]]
        r_Z = [[rbank[0], rbank[1]], [rbank[2], rbank[3]]]
        TP = PP[2]
        r_T = [rbank[4], rbank[5]]
        OACC = [bank(6), bank(6)]
        r_OACC = [rbank[6], rbank[6]]

        chains = []
        chain_id = 0
        for j in range(4):
            for hp in range(4):
                kbs = [4 * j + 3, 4 * j + 2, 4 * j + 1, 4 * j] + list(range(4 * j - 1, -1, -1))
                ch = []
                for i, kb in enumerate(kbs):
                    c0 = 128 * (kb - 4 * j) if kb >= 4 * j else 0
                    ch.append(dict(j=j, hp=hp, kb=kb, c0=c0, first=(i == 0), last=(i == len(kbs) - 1),
                                   diag=(kb >= 4 * j), chain=chain_id))
                chains.append(ch)
                chain_id += 1
        tiles = []
        for ch in chains:
            tiles.extend(ch)
        sched = {}
        for s_ in range(NPRE, 4):
            lo = 0 if s_ == NPRE else min(n for n, t in enumerate(tiles) if t["j"] == s_ - 1)
            hi = min(n for n, t in enumerate(tiles) if t["j"] == s_)
            micro = deferred[s_]
            for g, op_ in enumerate(micro):
                n = lo + (g * (hi - lo)) // len(micro)
                sched.setdefault(n, []).append(op_)
        last_sched = max(sched)
        NTL = len(tiles)

        def pv(t3, c0):
            return t3[:, :, c0:512]

        def z3(zb):
            return ZP[zb][:].rearrange("p (h n) -> p h n", h=2)

        def emit_qk(n):
            t = tiles[n]
            zb = n % 2
            j, hp, kb, c0 = t["j"], t["hp"], t["kb"], t["c0"]
            for hd in range(2):
                rows = slice(64 * hd, 64 * hd + 64)
                out = ZP[zb][:, hd * 512 + c0:hd * 512 + 512]
                lhsT = KT[rows, hp, kb * 128:(kb + 1) * 128]
                rhs = QT[rows, hp, j * 512 + c0:(j + 1) * 512]
                mm(out, lhsT, rhs, True, not t["diag"], [r_KT[hp][kb // 4], r_QT[hp][j]], [r_Z[zb][hd]])
            if t["diag"]:
                for hd in range(2):
                    out = ZP[zb][:, hd * 512 + c0:hd * 512 + c0 + 128]
                    mm(out, ident, maskneg, False, True, [r_cbf], [r_Z[zb][hd]])

        def emit_exp1(n):
            t = tiles[n]
            zb = n % 2
            act(pv(E_[zb], t["c0"]), pv(z3(zb), t["c0"]), AF.Exp, r_Z[zb], [r_E[zb]])

        def emit_ln(n):
            t = tiles[n]
            zb = n % 2
            act(pv(SP_[zb], t["c0"]), pv(E_[zb], t["c0"]), AF.Ln, [r_E[zb], r_st0], [r_SP[zb]], bias=ONEC)

        def emit_tri(n):
            t = tiles[n]
            zb = n % 2
            c0 = t["c0"]
            for hd in range(2):
                out = ZP[zb][:, hd * 512 + c0:hd * 512 + 512]
                mm(out, negtri, SP_[zb][:, hd, c0:512], False, True, [r_cbf, r_SP[zb]], [r_Z[zb][hd]], skip=True)
            for hd in range(2):
                out = TP[:, hd * 512 + c0:hd * 512 + 512]
                mm(out, ones, SP_[zb][:, hd, c0:512], True, True, [r_cbf, r_SP[zb]], [r_T[hd]])

        def emit_dve(n):
            t = tiles[n]
            zb = n % 2
            c0 = t["c0"]
            CC = CCS[t["chain"] % 2]
            rcc = r_CCS[t["chain"] % 2]
            if t["first"]:
                P.add("pool", (lambda CC=CC: (lambda e: e.memset(CC, 0.0)))(), (), [rcc])
            a = pv(ARG[zb], c0)
            z = pv(z3(zb), c0)
            c = pv(CC, c0)
            tp = pv(TP[:].rearrange("p (h n) -> p h n", h=2), c0)
            dve_tt(a, z, c, ALU.subtract, r_Z[zb] + [rcc], [r_ARG[zb]])
            if not t["last"]:
                dve_tt(c, tp, c, ALU.add, r_T + [rcc], [rcc])

        def emit_exp2(n):
            t = tiles[n]
            zb = n % 2
            act(pv(ATT[zb], t["c0"]), pv(ARG[zb], t["c0"]), AF.Exp, [r_ARG[zb]], [r_ATT[zb]])

        def emit_av(n):
            t = tiles[n]
            zb = n % 2
            c0, hp, kb, j = t["c0"], t["hp"], t["kb"], t["j"]
            ob = t["chain"] % 2
            for hd in range(2):
                out = OACC[ob][64 * hd:64 * hd + 64, c0:512]
                lhsT = VV[:, kb, (2 * hp + hd) * 64:(2 * hp + hd) * 64 + 64]
                mm(out, lhsT, ATT[zb][:, hd, c0:512], t["first"], t["last"], [r_V[kb], r_ATT[zb]], [r_OACC[ob]], skip=True)
            if t["last"]:
                dst = OT[:, hp, j * 512:(j + 1) * 512]
                src = OACC[ob]
                P.add("dve", lambda e: e.tensor_copy(dst, src), [r_OACC[ob]], [r_OT[hp][j]])

        emit_qk(0)
        for n in range(NTL + 2):
            if n + 1 < NTL:
                emit_qk(n + 1)
            if n < NTL:
                emit_exp1(n)
            if 2 <= n:
                emit_exp2(n - 2)
            if n < NTL:
                emit_ln(n)
                emit_tri(n)
                emit_dve(n)
            if 2 <= n:
                emit_av(n - 2)
            for op_ in sched.get(n, []):
                op_()
            if n == last_sched:
                p5_prefetch()

        if stop_after == "P4":
            P.frozen = True
        NXS = 6
        R1f = R1[:].bitcast(F32)
        XS = [R1f[:, 8192 + j * 1024:8192 + (j + 1) * 1024] for j in range(NXS)]
        r_xs = [Res("xs%d" % j) for j in range(NXS)]
        for j in range(NXS):
            r_xs[j].pending = r_V + r_PLT + [r_CCS[1]]
            dma("sp", XS[j], x_d[j * 128:(j + 1) * 128, :], "xs%d" % j, writes=[r_xs[j]])
        MT = R1[:, 0:16384].rearrange("p (c t) -> p c t", c=8)
        r_MT = [[Res("MT%d_%d" % (c, s)) for s in range(4)] for c in range(8)]
        oldq = [r for row in r_QT for r in row] + [r for row in r_KT for r in row]
        for c in range(8):
            for s in range(4):
                r_MT[c][s].pending = list(oldq)
        TM = [[RX[:, 4096 + (i * 4 + q) * 512:4096 + (i * 4 + q + 1) * 512] for q in range(4)] for i in range(2)]
        r_TM = [[Res("TM%d_%d" % (i, q)) for q in range(4)] for i in range(2)]
        olda = r_E + r_ARG + [r_CC] + r_SP
        for i in range(2):
            for q in range(4):
                r_TM[i][q].pending = list(olda)
        it = 0
        so = [None, None]
        for gq in range(4):
            ha, hb_, hc = p5hs[(gq, "a")], p5hs[(gq, "b")], p5hs[(gq, "c")]
            wa, wb, wc = hslot_view(ha), hslot_view(hb_), hslot_view(hc)
            for dcl in range(2):
                dc = 2 * gq + dcl
                cs = slice(dcl * 128, dcl * 128 + 128)
                for s in range(4):
                    pb = 4 * (it % 2)
                    ti = it % 2
                    it += 1
                    bgp, bgs, byp, bys = pb, pb + 1, pb + 2, pb + 3
                    for k in range(8):
                        mm(bank(bgp), wa[:, k, cs], hT_span(k, s), k == 0, k == 7,
                           [r_hs[ha]] + r_hT[4 * s:4 * s + 4], [rbank[bgp]])
                    for k in range(8):
                        mm(bank(bgs), wb[:, k, cs], hT_span(k, s), k == 0, k == 7,
                           [r_hs[hb_]] + r_hT[4 * s:4 * s + 4], [rbank[bgs]])
                    for k in range(4):
                        mm(bank(byp), wc[:, k, cs], YPT[:, k, s * 512:(s + 1) * 512], k == 0, k == 3,
                           [r_hs[hc], r_YPT[k][s]], [rbank[byp]])
                    for k in range(4):
                        mm(bank(bys), wc[:, 4 + k, cs], OT[:, k, s * 512:(s + 1) * 512], k == 0, k == 3,
                           [r_hs[hc], r_OT[k][s]], [rbank[bys]])
                    act(TM[ti][0], bank(bgp), AF.Sigmoid, [rbank[bgp], r_bgt], [r_TM[ti][0]], bias=BGT[:, dc:dc + 1])
                    act(TM[ti][1], bank(bgs), AF.Sigmoid, [rbank[bgs], r_bgt], [r_TM[ti][1]], bias=BGT[:, 8 + dc:9 + dc])
                    dve_tt(TM[ti][2], bank(byp), TM[ti][0], ALU.mult, [rbank[byp], r_TM[ti][0]], [r_TM[ti][2]])
                    dve_tt(TM[ti][3], bank(bys), TM[ti][1], ALU.mult, [rbank[bys], r_TM[ti][1]], [r_TM[ti][3]])
                    dve_tt(MT[:, dc, s * 512:(s + 1) * 512], TM[ti][2], TM[ti][3], ALU.add,
                           [r_TM[ti][2], r_TM[ti][3]], [r_MT[dc][s]])
            if gq == 0:
                p5hs[(2, "c")] = p5_load(2, "c")
                p5hs[(3, "a")] = p5_load(3, "a")
                p5hs[(3, "b")] = p5_load(3, "b")
            elif gq == 1:
                p5hs[(3, "c")] = p5_load(3, "c")
                assert hs_n[0] == 12
                for s_ in range(4):
                    r_ws[s_].pending = [r_hs[2 * s_], r_hs[2 * s_ + 1]]
                ws_n[0] = 6
                so[0] = wload(wcols(w_out_d, 0))
            elif gq == 2:
                so[1] = wload(wcols(w_out_d, 512))

        if stop_after == "P5":
            P.frozen = True
        oldrx = [r for row in r_YPT for r in row] + [r for row in r_OT for r in row] + \
                [r for row in r_TM for r in row] + r_ATT
        r_x1h = [[Res("x1_%d_0" % t), Res("x1_%d_1" % t)] for t in range(NT)]
        for tt in range(NT):
            r_x1h[tt][0].pending = list(oldrx)
            r_x1h[tt][1].pending = list(oldrx)
            if tt >= NXS:
                dma("sp", xt(tt), x_d[tt * 128:(tt + 1) * 128, :], "x%d" % tt, reads=[r_ws[so[1]]], writes=r_x1h[tt])
        r_h2T = [Res("h2T%d" % t) for t in range(NT)]
        for tt in range(NT):
            r_h2T[tt].pending = list(r_hT)
        ssA = stcol(32)
        ssS = stcol(16)
        lnA = stcol(16)
        rsA = stcol(16)
        ssB = stcol(16)
        lnB = stcol(16)
        rsB = stcol(16)
        r_stA = [Res("stA%d" % t) for t in range(NT)]
        r_stB = [Res("stB%d" % t) for t in range(NT)]
        HBS4 = HBS + [R1[:, 28672:29696], R1[:, 29696:30720]]
        r_hb4 = r_hb + [Res("hb2"), Res("hb3")]
        JK6 = R1[:, 30720:31232]
        r_jk6 = Res("jk6")
        for r in r_hb4[2:] + [r_jk6]:
            r.pending = r_V + r_PLT + [r_CCS[1]]

        JK6w = R1[:, 30720:31744]

        def p6_mm(T):
            b0 = 2 * (T % 3)
            for dh in range(2):
                wv = wslot_view(so[dh])
                for k in range(8):
                    mm(bank(b0 + dh), MT[:, k, T * 128:(T + 1) * 128], wv[:, k, :], k == 0, k == 7,
                       [r_ws[so[dh]], r_MT[k][T // 4]], [rbank[b0 + dh]])

        def p6_sqA(T):
            b0 = 2 * (T % 3)
            act(JK6w, PP[T % 3][:], AF.Square, [rbank[b0], rbank[b0 + 1]], [r_jk6, r_stA[T]], accum=ssS[:, T:T + 1])

        def p6_lnA(T):
            act(lnA[:, T:T + 1], ssS[:, T:T + 1], AF.Ln, [r_stA[T], r_st0], [r_stA[T]], bias=EPSC, scale=1.0 / D)

        def p6_expA(T):
            act(rsA[:, T:T + 1], lnA[:, T:T + 1], AF.Exp, [r_stA[T]], [r_stA[T]], scale=-0.5)

        def p6_res(T):
            b0 = 2 * (T % 3)
            for dh in range(2):
                P.add("dve", (lambda b=b0 + dh, T=T, dh=dh: (lambda e: e.scalar_tensor_tensor(
                    out=bank(b), in0=bank(b), scalar=rsA[:, T:T + 1], in1=GB[:, dh * 512:(dh + 1) * 512],
                    op0=ALU.mult, op1=ALU.mult)))(),
                    [rbank[b0 + dh], r_stA[T], r_gb[0]], [rbank[b0 + dh]])
            for dh in range(2):
                xo = xt(T)[:, dh * 512:(dh + 1) * 512]
                if T < NXS:
                    xi = XS[T][:, dh * 512:(dh + 1) * 512]
                    dve_tt(xo, bank(b0 + dh), xi, ALU.add, [rbank[b0 + dh], r_xs[T]], [r_x1h[T][dh]])
                else:
                    dve_tt(xo, bank(b0 + dh), xo, ALU.add, [rbank[b0 + dh], r_x1h[T][dh]], [r_x1h[T][dh]])

        def p6_sqB(T):
            act(HBS4[T % 4], xt(T), AF.Square, r_x1h[T], [r_hb4[T % 4], r_stB[T]], accum=ssB[:, T:T + 1])

        def p6_lnB(T):
            act(lnB[:, T:T + 1], ssB[:, T:T + 1], AF.Ln, [r_stB[T], r_st0], [r_stB[T]], bias=EPSC, scale=1.0 / D)

        def p6_expB(T):
            act(rsB[:, T:T + 1], lnB[:, T:T + 1], AF.Exp, [r_stB[T]], [r_stB[T]], scale=-0.5)

        def p6_h(T):
            hb = HBS4[T % 4]
            P.add("dve", (lambda hb=hb, T=T: (lambda e: e.scalar_tensor_tensor(
                out=hb, in0=xt(T), scalar=rsB[:, T:T + 1], in1=GB[:, 1024:2048], op0=ALU.mult, op1=ALU.mult)))(),
                r_x1h[T] + [r_stB[T], r_gb[1]], [r_hb4[T % 4]])

        def ok(T):
            return 0 <= T < NT

        for i in range(NT + 6):
            if ok(i - 5):
                norm_c(i - 5, r_h2T[i - 5], 6 + (i - 5) % 2, HBS4, r_hb4, evac="none")
            if ok(i):
                p6_mm(i)
            if ok(i - 1):
                p6_sqA(i - 1)
            if ok(i - 3):
                p6_sqB(i - 3)
            if ok(i - 1):
                p6_lnA(i - 1)
            if ok(i - 3):
                p6_lnB(i - 3)
            if ok(i - 1):
                p6_expA(i - 1)
            if ok(i - 3):
                p6_expB(i - 3)
            if ok(i - 6):
                norm_evac(i - 6, r_h2T[i - 6], 6 + (i - 6) % 2)
            if ok(i - 2):
                p6_res(i - 2)
            if ok(i - 4):
                p6_h(i - 4)

        if stop_after == "P6":
            P.frozen = True
        AT = R1[:].rearrange("p (c t) -> p c t", c=32)
        r_AT = [[Res("AT%d_%d" % (c, s)) for s in range(2)] for c in range(32)]
        oldm = [r for row in r_MT for r in row] + r_V + r_PLT + r_xs + r_hb4[2:] + [r_jk6, r_CCS[1]]
        dma("sp", GB[:, 0:1024], gains_d[3], "gb0", writes=[r_gb[0]])
        FFS = [R2[:, h * 4096:(h + 1) * 4096].rearrange("p (t e) -> p t e", t=8) for h in range(2)]
        ssC = stcol(32)
        ssT = stcol(16)
        lnC = stcol(16)
        rsC = stcol(16)
        r_stC = [Res("stC%d" % t) for t in range(NT)]
        r_rt = [Res("rt0"), Res("rt1")]
        for r in r_rt:
            r.pending = [r_gb[1]]
        for half in range(2):
            for c in range(32):
                for s in range(2):
                    r_AT[c][s].pending = list(oldm)
            oldm = []
            for f4 in range(8):
                su = wload(wcols(w_up_d, f4 * 512))
                wv = wslot_view(su)
                for fl in range(4):
                    fc = 4 * f4 + fl
                    for s in range(2):
                        b = nextbank()
                        for k in range(8):
                            mm(bank(b), wv[:, k, fl * 128:(fl + 1) * 128],
                               hT[:, half, k, s * 512:(s + 1) * 512], k == 0, k == 7,
                               [r_ws[su]] + r_h2T[8 * half + 4 * s:8 * half + 4 * s + 4], [rbank[b]])
                        dst = AT[:, fc, s * 512:(s + 1) * 512]
                        ti = b % 2
                        tmpr = GB[:, 1024 + ti * 512:1024 + (ti + 1) * 512]
                        act(tmpr, bank(b), AF.Relu, [rbank[b]], [r_rt[ti]])
                        dve_tt(dst, bank(b), tmpr, ALU.mult, [rbank[b], r_rt[ti]], [r_AT[fc][s]])
            if stop_after == "P7a":
                P.frozen = True
            r_ffs = [Res("ffs%d_%d" % (half, t)) for t in range(8)]
            for t in range(8):
                r_ffs[t].pending = list(r_h2T[8 * half:8 * half + 8])
            for dh in range(2):
                for f8 in range(4):
                    vd = w_down_d.rearrange("(c p) d -> p c d", p=128)
                    sd = wload([(0, vd[:, 8 * f8:8 * f8 + 4, dh * 512:(dh + 1) * 512]),
                                (4, vd[:, 8 * f8 + 4:8 * f8 + 8, dh * 512:(dh + 1) * 512])])
                    wv = wslot_view(sd)
                    if f8 == 3:
                        for t in range(8):
                            for fl in range(8):
                                fc = 8 * f8 + fl
                                mm(bank(t), AT[:, fc, t * 128:(t + 1) * 128], wv[:, fl, :], fc == 0, fc == 31,
                                   [r_ws[sd], r_AT[fc][t // 4]], [rbank[t]])
                    else:
                        for fl in range(8):
                            fc = 8 * f8 + fl
                            for t in range(8):
                                mm(bank(t), AT[:, fc, t * 128:(t + 1) * 128], wv[:, fl, :], fc == 0, fc == 31,
                                   [r_ws[sd], r_AT[fc][t // 4]], [rbank[t]])
                if stop_after == "P7b":
                    P.frozen = True
                def ev_stats(t, half=half, dh=dh):
                    tt = 8 * half + t
                    act(HBS[t % 2][:, 0:512], bank(t), AF.Square, [rbank[t]], [r_hb[t % 2], r_stC[tt]],
                        accum=ssC[:, 2 * tt + dh:2 * tt + dh + 1])
                    if dh == 0:
                        P.add("dve", (lambda t=t, half=half: (lambda e: e.tensor_copy(FFS[half][:, t, :], bank(t))))(),
                              [rbank[t]], [r_ffs[t]])
                    else:
                        dve_tt(ssT[:, tt:tt + 1], ssC[:, 2 * tt:2 * tt + 1], ssC[:, 2 * tt + 1:2 * tt + 2], ALU.add,
                               [r_stC[tt]], [r_stC[tt]])
                        act(lnC[:, tt:tt + 1], ssT[:, tt:tt + 1], AF.Ln, [r_stC[tt], r_st0], [r_stC[tt]],
                            bias=EPSC, scale=1.0 / D)
                        act(rsC[:, tt:tt + 1], lnC[:, tt:tt + 1], AF.Exp, [r_stC[tt]], [r_stC[tt]], scale=-0.5)

                def ev_big(t, half=half):
                    tt = 8 * half + t
                    for d2 in range(2):
                        src = FFS[half][:, t, :] if d2 == 0 else bank(t)
                        rsrc = r_ffs[t] if d2 == 0 else rbank[t]
                        P.add("dve", (lambda src=src, tt=tt, d2=d2: (lambda e: e.scalar_tensor_tensor(
                            out=src, in0=src, scalar=rsC[:, tt:tt + 1], in1=GB[:, d2 * 512:(d2 + 1) * 512],
                            op0=ALU.mult, op1=ALU.mult)))(),
                            [rsrc, r_stC[tt], r_gb[0]], [rsrc])
                    for d2 in range(2):
                        src = FFS[half][:, t, :] if d2 == 0 else bank(t)
                        rsrc = r_ffs[t] if d2 == 0 else rbank[t]
                        xs = xt(tt)[:, d2 * 512:(d2 + 1) * 512]
                        P.add("dve", (lambda xs=xs, src=src: (lambda e: e.tensor_tensor(out=xs, in0=src, in1=xs, op=ALU.add)))(),
                              [rsrc, r_x1h[tt][d2]], [r_x1h[tt][d2]])
                    if stop_after != "P7d":
                        dma("sp", out_d[tt * 128:(tt + 1) * 128, :], xt(tt), "st%d" % tt, reads=r_x1h[tt])

                for t in range(9):
                    if t < 8:
                        ev_stats(t)
                    if dh == 1 and t >= 1:
                        ev_big(t - 1)
                if stop_after == "P7c" and dh == 0:
                    P.frozen = True

        P.frozen = False
        if debug:
            dbg_src = {
                "hT": (R2b, []), "R1": (R1[:], []), "RX": (RX[:], []),
            }
            for name, _, _ in debug:
                src, _r = dbg_src[name]
                P.add("sp", (lambda src=src, name=name: (lambda e: [e.dma_start(out=dbg_d[name], in_=src)]))(),
                      [], [], dma_key="dbg_" + name, ninst=1, barrier=True)

        P.finalize()
        P.plan_waits()
        for key in P.dma_cnt:
            sem_dma[key] = es.enter_context(nc.semaphore("sd_" + key))
        final_waits = [(k, v) for k, v in P.dma_cnt.items() if k.startswith("st") or k.startswith("dbg_")]

        def emit_engine(name, e):
            waited = {}
            for op in P.ops:
                if op.eng != name:
                    continue
                todo = [(sem_dma[key[1]] if key[0] == "d" else sem_eng[key[1]], val) for key, val in op.waits]
                embed = None
                if todo and op.dma_key is None and name in ("act", "dve", "pool") and not op.multi:
                    embed = todo.pop()
                for sem, val in todo:
                    e.wait_ge(sem, val)
                res = op.fn(e)
                if embed is not None:
                    res._wait_ge(embed[0], embed[1])
                if op.dma_key is not None:
                    for inst in res:
                        inst.then_inc(sem_dma[op.dma_key], 16)
                elif op.signal:
                    res.then_inc(sem_eng[name], 1)
            if name == "sp":
                if debug:
                    for en in ("pe", "act", "dve", "pool"):
                        pass
                for k, v in final_waits:
                    e.wait_ge(sem_dma[k], v)

        with nc.Block() as block:
            @block.tensor
            def _(e):
                emit_engine("pe", e)

            @block.scalar
            def _(e):
                emit_engine("act", e)

            @block.vector
            def _(e):
                emit_engine("dve", e)

            @block.gpsimd
            def _(e):
                emit_engine("pool", e)

            @block.sync
            def _(e):
                emit_engine("sp", e)
    return nc


_NC_CACHE = {}


def _consts():
    bf = ml_dtypes.bfloat16
    p = np.arange(128)[:, None]
    c = np.arange(128)[None, :]
    ident = (p == c).astype(np.float32)
    negtri = -(p >= c).astype(np.float32)
    ones = np.ones((128, 128), np.float32)
    maskneg = np.where(p < c, 0.0, NEG).astype(np.float32)
    cbf = np.concatenate([ident, negtri, ones, maskneg], axis=1).astype(bf)
    invcnt = np.zeros((128, 64), np.float32)
    for g, w in enumerate((2, 4, 8, 16)):
        t = np.arange(16)
        invcnt[:, g * 16:(g + 1) * 16] = (1.0 / np.minimum(t + 1, w))[None, :]
    return cbf, invcnt


def kernel(x, g_pre_mix, w_in, w_pool_mix, pool_scale, w_br_pool, w_br_sb, w_gate, b_gate,
           w_out, g_post_mix, g_pre_mlp, w_up, w_down, g_post_mlp, _debug=None, _stop=None):
    f = lambda a: np.ascontiguousarray(np.asarray(a, dtype=np.float32))
    x = f(x)
    B = x.shape[0]
    key = "dbg" if _debug else "main"
    if key not in _NC_CACHE:
        _NC_CACHE[key] = build_nc(_debug, _stop)
    nc = _NC_CACHE[key]
    cbf, invcnt = _consts()
    gains = np.stack([np.broadcast_to(f(g)[None, :], (128, D)) for g in
                      (g_pre_mix, g_post_mix, g_pre_mlp, g_post_mlp)]).copy()
    pscale = np.ascontiguousarray(f(pool_scale).reshape(4, 128).T)
    bgate = np.ascontiguousarray(f(b_gate).reshape(16, 128).T)
    shared = {
        "w_in": f(w_in), "w_gate": f(w_gate), "w_br_pool": f(w_br_pool), "w_br_sb": f(w_br_sb),
        "w_out": f(w_out), "w_up": f(w_up), "w_down": f(w_down), "w_pool_mix": f(w_pool_mix),
        "gains": gains, "pscale": pscale, "bgate": bgate, "invcnt": invcnt, "cbf": cbf,
    }
    in_maps = [dict(shared, x=x[b]) for b in range(B)]
    res = run_bass_kernel_spmd(nc, in_maps, core_ids=list(range(B)))
    out = np.stack([np.asarray(r["out"], dtype=np.float32) for r in res.results], axis=0)
    if _debug:
        return out, [{k: np.asarray(v) for k, v in r.items()} for r in res.results]
    return out
```

```python
import os
import numpy as np
import ml_dtypes
from contextlib import ExitStack
import concourse.bass as bass
import concourse.mybir as mybir
from concourse.bass_utils import run_bass_kernel_spmd

F32 = mybir.dt.float32
BF16 = mybir.dt.bfloat16
AF = mybir.ActivationFunctionType
ALU = mybir.AluOpType

S = 2048
D = 1024
NT = S // 128
NEG = -30000.0
EPS = 1e-6


class Res:
    __slots__ = ("name", "w", "rs", "pending", "excl")

    def __init__(self, name, excl=False):
        self.name = name
        self.excl = excl
        self.w = None
        self.rs = {}
        self.pending = []


class Op:
    __slots__ = ("idx", "eng", "fn", "dma_key", "ninst", "deps", "signal", "sigval", "multi", "waits")

    def __init__(self, idx, eng, fn, dma_key, ninst):
        self.idx = idx
        self.eng = eng
        self.fn = fn
        self.dma_key = dma_key
        self.ninst = ninst
        self.deps = set()
        self.signal = False
        self.sigval = 0
        self.multi = False
        self.waits = []


class Prog:
    ENGS = ("pe", "act", "dve", "pool", "sp")

    def __init__(self):
        self.ops = []
        self.dma_cnt = {}
        self.frozen = False

    def add(self, eng, fn, reads=(), writes=(), dma_key=None, ninst=1, barrier=False):
        if self.frozen:
            return None
        op = Op(len(self.ops), eng, fn, dma_key, ninst)
        if barrier:
            last = {}
            for o in self.ops:
                last[(o.eng, o.dma_key)] = o
            for o in last.values():
                op.deps.add(o)
        is_dma = dma_key is not None
        writes = list(writes)
        extra = []
        for w in writes:
            if w.pending:
                extra.extend(w.pending)
                w.pending = []
        deps = []
        for r in reads:
            if r.w is not None:
                deps.append((r.w, "raw"))
            if r.excl:
                for rd in r.rs.values():
                    if rd.eng != eng:
                        deps.append((rd, "raw"))
        for w in writes + extra:
            if w.w is not None:
                deps.append((w.w, "waw"))
            for rd in w.rs.values():
                deps.append((rd, "war"))
        for d, kind in deps:
            if d is op:
                continue
            d_dma = d.dma_key is not None
            if (not d_dma) and (not is_dma) and d.eng == eng and eng == "pe":
                continue
            op.deps.add(d)
        for r in reads:
            r.rs[(eng, dma_key)] = op
        for w in writes:
            w.w = op
            w.rs = {}
        if is_dma:
            self.dma_cnt[dma_key] = self.dma_cnt.get(dma_key, 0) + 16 * ninst
            op.sigval = self.dma_cnt[dma_key]
        self.ops.append(op)
        return op

    def finalize(self):
        for op in self.ops:
            for d in op.deps:
                if d.dma_key is None:
                    d.signal = True
        cnt = {e: 0 for e in self.ENGS}
        for op in self.ops:
            if op.dma_key is None and op.signal:
                cnt[op.eng] += 1
                op.sigval = cnt[op.eng]


    def plan_waits(self):
        waited = {e: {} for e in self.ENGS}
        know = {}
        for op in self.ops:
            w = waited[op.eng]
            need = {}
            for d in op.deps:
                key = ("d", d.dma_key) if d.dma_key is not None else ("e", d.eng)
                if d.sigval > need.get(key, 0):
                    need[key] = d.sigval
            op.waits = []
            for key, val in sorted(need.items(), key=lambda kv: str(kv[0])):
                if w.get(key, 0) >= val:
                    continue
                op.waits.append((key, val))
                w[key] = val
                for k2, v2 in know.get((key, val), {}).items():
                    if v2 > w.get(k2, 0):
                        w[k2] = v2
            if op.dma_key is not None:
                know[(("d", op.dma_key), op.sigval)] = dict(w)
            elif op.signal:
                kk = dict(w)
                kk[("e", op.eng)] = max(kk.get(("e", op.eng), 0), op.sigval)
                know[(("e", op.eng), op.sigval)] = kk


def build_nc(debug=None, stop_after=None):
    nc = bass.Bass("TRN2", target_bir_lowering=False)
    P = Prog()

    def dram_in(name, shape, dt=F32):
        return nc.dram_tensor(name, list(shape), dt, kind="ExternalInput").ap()

    x_d = dram_in("x", [S, D])
    w_in_d = dram_in("w_in", [D, 2048])
    w_gate_d = dram_in("w_gate", [D, 2048])
    w_brp_d = dram_in("w_br_pool", [512, D])
    w_brs_d = dram_in("w_br_sb", [512, D])
    w_out_d = dram_in("w_out", [D, D])
    w_up_d = dram_in("w_up", [D, 4096])
    w_down_d = dram_in("w_down", [4096, D])
    wpm_d = dram_in("w_pool_mix", [4, 128, 128])
    gains_d = dram_in("gains", [4, 128, D])
    pscale_d = dram_in("pscale", [128, 4])
    bgate_d = dram_in("bgate", [128, 16])
    invcnt_d = dram_in("invcnt", [128, 64])
    cbf_d = dram_in("cbf", [128, 512], BF16)
    out_d = nc.dram_tensor("out", [S, D], F32, kind="ExternalOutput").ap()
    dbg_d = {}
    if debug:
        for name, shape, dt in debug:
            dbg_d[name] = nc.dram_tensor("dbg_" + name, list(shape), dt, kind="ExternalOutput").ap()

    es = ExitStack()
    with es:
        def sb(name, shape, dt):
            return es.enter_context(nc.sbuf_tensor(name, list(shape), dt))

        RX = sb("RX", [128, 16384], F32)
        R1 = sb("R1", [128, 32768], BF16)
        R2 = sb("R2", [128, 8192], F32)
        WS = sb("WS", [128, 16384], BF16)
        GB = sb("GB", [128, 2048], F32)
        HB = sb("HB", [128, 1024], BF16)
        JK = sb("JK", [128, 1024], BF16)
        CBF = sb("CBF", [128, 512], BF16)
        WPM = sb("WPM", [128, 512], BF16)
        PSC = sb("PSC", [128, 4], F32)
        BGT = sb("BGT", [128, 16], F32)
        ICN = sb("ICN", [128, 64], F32)
        ST = sb("ST", [128, 288], F32)
        PP = [es.enter_context(nc.psum_tensor("pp%d" % i, [128, 1024], F32)) for i in range(4)]

        sem_eng = {e: es.enter_context(nc.semaphore("se_" + e)) for e in Prog.ENGS}
        sem_dma = {}

        def bank(i):
            return PP[i // 2][:, (i % 2) * 512:(i % 2) * 512 + 512]

        rbank = [Res("bank%d" % i, excl=True) for i in range(8)]
        ident = CBF[:, 0:128]
        negtri = CBF[:, 128:256]
        ones = CBF[:, 256:384]
        maskneg = CBF[:, 384:512]

        EPSC = ST[:, 0:1]
        ONEC = ST[:, 1:2]
        st_next = [2]

        def stcol(n=1):
            c = st_next[0]
            st_next[0] += n
            assert st_next[0] <= 272
            return ST[:, c:c + n]

        def dma(eng, out, in_, key, reads=(), writes=()):
            P.add(eng, lambda e: [e.dma_start(out=out, in_=in_)], reads, writes, dma_key=key, ninst=1)

        def dma2(eng, outs_ins, key, reads=(), writes=()):
            def fn(e):
                return [e.dma_start(out=o, in_=i) for (o, i) in outs_ins]
            P.add(eng, fn, reads, writes, dma_key=key, ninst=len(outs_ins))

        def mm(out, lhsT, rhs, start, stop, reads, writes, skip=False):
            if skip:
                P.add("pe", lambda e: e.matmul(out, lhsT, rhs, start=start, stop=stop, skip_group_check=True), reads, writes)
            else:
                P.add("pe", lambda e: e.matmul(out, lhsT, rhs, start=start, stop=stop), reads, writes)

        def act(out, in_, func, reads, writes, bias=None, scale=1.0, accum=None):
            kw = {}
            if bias is not None:
                kw["bias"] = bias
            if accum is not None:
                kw["accum_out"] = accum
            o_ = P.add("act", lambda e: e.activation(out, in_, func, scale=scale, **kw), reads, writes)
            if o_ is not None and accum is not None:
                o_.multi = True

        def junk():
            return Res("junk")

        r_cbf = Res("cbf")
        r_wpm = Res("wpm")
        r_psc = Res("psc")
        r_bgt = Res("bgt")
        r_icn = Res("icn")
        r_st0 = Res("st0")
        r_gb = [Res("gb0"), Res("gb1")]
        dma("sp", CBF[:], cbf_d, "cbf", writes=[r_cbf])
        dma("sp", GB[:, 0:1024], gains_d[0], "gb0", writes=[r_gb[0]])
        dma("sp", PSC[:], pscale_d, "psc", writes=[r_psc])
        dma("sp", BGT[:], bgate_d, "bgt", writes=[r_bgt])
        dma("sp", ICN[:], invcnt_d, "icn", writes=[r_icn])
        P.add("pool", lambda e: e.memset(EPSC, EPS), (), [r_st0])
        P.add("pool", lambda e: e.memset(ONEC, 1.0), (), [r_st0])
        dma("pool", WPM[:].rearrange("p (g d) -> p g d", g=4), wpm_d.rearrange("g p d -> p g d"), "wpm", writes=[r_wpm])

        r_ws = [Res("ws%d" % i) for i in range(4)]
        ws_n = [0]

        def wslot_view(s):
            return WS[:, s * 4096:(s + 1) * 4096].rearrange("p (k e) -> p k e", k=8)

        def wload(parts, reads=()):
            s = ws_n[0] % 4
            ws_n[0] += 1
            v = wslot_view(s)
            oi = []
            for (k0, src) in parts:
                kk = src.shape[1]
                oi.append((v[:, k0:k0 + kk, :], src))
            dma2("pool", oi, "ws%d" % s, reads=reads, writes=[r_ws[s]])
            return s

        def wcols(wd, c0, k=8):
            v = wd.rearrange("(k p) e -> p k e", p=128)
            return [(0, v[:, 0:k // 2, c0:c0 + 512]), (k // 2, v[:, k // 2:k, c0:c0 + 512])]

        RXb = RX[:].bitcast(BF16)
        R2b = R2[:].bitcast(BF16)
        hT = R2b.rearrange("p (h k t) -> p h k t", h=2, k=8)

        def hT_span(k, s):
            return hT[:, s // 2, k, (s % 2) * 512:(s % 2) * 512 + 512]

        def hT_tile(k, tt):
            return hT[:, tt // 8, k, (tt % 8) * 128:(tt % 8) * 128 + 128]

        r_hT = [Res("hT%d" % t) for t in range(NT)]
        r_x = [Res("x%d" % t) for t in range(NT)]

        def xt(tt):
            return RX[:, tt * 1024:(tt + 1) * 1024]

        QT = R1[:, 0:8192].rearrange("p (c t) -> p c t", c=4)
        KT = R1[:, 8192:16384].rearrange("p (c t) -> p c t", c=4)
        VV = R1[:, 16384:24576].rearrange("p (t e) -> p t e", t=16)
        PLT = R1[:, 24576:32768].rearrange("p (c t) -> p c t", c=4)
        r_QT = [[Res("QT%d_%d" % (c, s)) for s in range(4)] for c in range(4)]
        r_KT = [[Res("KT%d_%d" % (c, s)) for s in range(4)] for c in range(4)]
        r_V = [Res("V%d" % t) for t in range(NT)]
        r_PLT = [Res("PLT%d" % g) for g in range(4)]

        def uT(g):
            return RX[:, g * 2048:(g + 1) * 2048]
        r_uT = [Res("uT%d" % g) for g in range(4)]
        SA = RX[:, 8192:10248]
        SB_ = RX[:, 0:2056]
        r_SA = Res("SA")
        r_SB = Res("SB")
        YPT = RXb[:, 20608:28800].rearrange("p (c t) -> p c t", c=4)
        r_YPT = [[Res("YPT%d_%d" % (g, s)) for s in range(4)] for g in range(4)]
        OT = RXb[:, 0:8192].rearrange("p (c t) -> p c t", c=4)
        r_OT = [[Res("OT%d_%d" % (c, s)) for s in range(4)] for c in range(4)]

        ss1 = stcol(16)
        ln1 = stcol(16)
        rs1 = stcol(16)
        r_ss1 = [Res("ss1_%d" % t) for t in range(NT)]
        r_hb = [Res("hb0"), Res("hb1")]

        HBS = [HB[:, 0:1024], JK[:, 0:1024]]

        def norm_b(tt, xin, r_xin, gb_ap, r_gbx, ssc, lnc, rsc, r_stat, hbs=None, rhbs=None):
            hbs = hbs or HBS
            rhbs = rhbs or r_hb
            hb = hbs[tt % len(hbs)]
            rhb = rhbs[tt % len(hbs)]
            rxl = list(r_xin) if isinstance(r_xin, list) else [r_xin]
            act(hb, xin, AF.Square, rxl, [rhb, r_stat], accum=ssc)
            act(lnc, ssc, AF.Ln, [r_stat, r_st0], [r_stat], bias=EPS, scale=1.0 / D)
            act(rsc, lnc, AF.Exp, [r_stat], [r_stat], scale=-0.5)
            P.add("dve", lambda e: e.scalar_tensor_tensor(out=hb, in0=xin, scalar=rsc, in1=gb_ap,
                                                          op0=ALU.mult, op1=ALU.mult),
                  rxl + [r_stat, r_gbx], [rhb])

        def norm_c(tt, dst_res, b, hbs=None, rhbs=None, evac="dve"):
            hbs = hbs or HBS
            rhbs = rhbs or r_hb
            hb = hbs[tt % len(hbs)]
            rhb = rhbs[tt % len(hbs)]
            pb = bank(b).bitcast(BF16)
            for k in range(8):
                o = pb[:, k * 128:(k + 1) * 128]
                i_ = hb[:, k * 128:(k + 1) * 128]
                P.add("pe", (lambda o=o, i_=i_: (lambda e: e.transpose(o, i_, ident)))(),
                      [rhb, r_cbf], [rbank[b]])
            dst = hT[:, tt // 8, :, (tt % 8) * 128:(tt % 8) * 128 + 128]
            src = pb.rearrange("p (k t) -> p k t", k=8)
            if evac == "dve":
                P.add("dve", lambda e: e.tensor_copy(dst, src), [rbank[b]], [dst_res])
            elif evac == "act":
                act(dst, src, AF.Copy, [rbank[b]], [dst_res])

        def norm_evac(tt, dst_res, b):
            pb = bank(b).bitcast(BF16)
            dst = hT[:, tt // 8, :, (tt % 8) * 128:(tt % 8) * 128 + 128]
            src = pb.rearrange("p (k t) -> p k t", k=8)
            act(dst, src, AF.Copy, [rbank[b]], [dst_res])

        for tt in range(NT):
            dma("sp", xt(tt), x_d[tt * 128:(tt + 1) * 128, :], "x%d" % tt, writes=[r_x[tt]])
        for i in range(NT + 1):
            if i < NT:
                norm_b(i, xt(i), r_x[i], GB[:, 0:1024], r_gb[0],
                       ss1[:, i:i + 1], ln1[:, i:i + 1], rs1[:, i:i + 1], r_ss1[i])
            if i >= 1:
                norm_c(i - 1, r_hT[i - 1], (i - 1) % 8)

        dma("sp", GB[:, 0:1024], gains_d[1], "gb0", writes=[r_gb[0]])
        dma("sp", GB[:, 1024:2048], gains_d[2], "gb1", writes=[r_gb[1]])
        if stop_after == "P1":
            P.frozen = True
        bk = [0]

        def nextbank():
            b = bk[0] % 8
            bk[0] += 1
            return b

        for g in range(4):
            r_uT[g].pending = list(r_x)
        sl = [wload(wcols(w_in_d, c * 512), reads=([] if c == 0 else [r_x[NT - 1]])) for c in range(4)]

        def proj_fm(c, cc, s, b, ev, lazy=False):
            wv = wslot_view(sl[c])
            ops = []
            for k in range(8):
                ops.append((lambda k=k: mm(bank(b), wv[:, k, cc * 128:(cc + 1) * 128], hT_span(k, s), k == 0, k == 7,
                                           [r_ws[sl[c]]] + r_hT[4 * s:4 * s + 4], [rbank[b]])))
            if c == 0:
                dst, rd, sc = uT(cc)[:, s * 512:(s + 1) * 512], r_uT[cc], 1.0
            elif c == 1:
                dst, rd, sc = QT[:, cc, s * 512:(s + 1) * 512], r_QT[cc][s], 0.125
            else:
                dst, rd, sc = KT[:, cc, s * 512:(s + 1) * 512], r_KT[cc][s], 1.0

            def evac():
                if ev == "act":
                    act(dst, bank(b), AF.Copy, [rbank[b]], [rd], scale=sc)
                elif sc != 1.0:
                    P.add("dve", lambda e: e.tensor_scalar(out=dst, in0=bank(b), scalar1=sc, scalar2=None, op0=ALU.mult),
                          [rbank[b]], [rd])
                else:
                    P.add("dve", lambda e: e.tensor_copy(dst, bank(b)), [rbank[b]], [rd])
            ops.append(evac)
            if lazy:
                return ops
            for o in ops:
                o()

        def proj_v(tt, b, ev, lazy=False):
            wv = wslot_view(sl[3])
            ops = []
            for k in range(8):
                ops.append((lambda k=k: mm(bank(b), hT_tile(k, tt), wv[:, k, :], k == 0, k == 7,
                                           [r_ws[sl[3]], r_hT[tt]], [rbank[b]])))
            dst = VV[:, tt, :]

            def evac():
                if ev == "act":
                    act(dst, bank(b), AF.Copy, [rbank[b]], [r_V[tt]])
                else:
                    P.add("dve", lambda e: e.tensor_copy(dst, bank(b)), [rbank[b]], [r_V[tt]])
            ops.append(evac)
            if lazy:
                return ops
            for o in ops:
                o()

        for s in range(4):
            for cc in range(4):
                proj_fm(0, cc, s, nextbank(), "act")
        NPRE = 2
        for s in range(NPRE):
            for c in (1, 2):
                for cc in range(4):
                    proj_fm(c, cc, s, nextbank(), "act")
            for tt in range(4 * s, 4 * s + 4):
                proj_v(tt, nextbank(), "act")
        deferred = {}
        for s in range(NPRE, 4):
            micro = []
            for c in (1, 2):
                for cc in range(4):
                    micro += proj_fm(c, cc, s, 7, "dve", lazy=True)
            for tt in range(4 * s, 4 * s + 4):
                micro += proj_v(tt, 7, "dve", lazy=True)
            deferred[s] = micro

        if stop_after == "P2":
            P.frozen = True
        def dve_tt(out, a, b_, op, reads, writes):
            P.add("dve", lambda e: e.tensor_tensor(out=out, in0=a, in1=b_, op=op), reads, writes)

        P.add("dve", lambda e: e.memset(SA[:, 0:8], 0.0), (), [r_SA])
        WIN = (2, 4, 8, 16)
        r_tmp = Res("tmp16")
        for g in range(4):
            u = uT(g)
            dve_tt(SA[:, 9:2056], u[:, 1:2048], u[:, 0:2047], ALU.add, [r_uT[g]], [r_SA])
            P.add("dve", (lambda u=u: (lambda e: e.tensor_copy(SA[:, 8:9], u[:, 0:1])))(), [r_uT[g]], [r_SA])
            cur, rcur = SA, r_SA
            if g == 1:
                S4 = RX[:, 0:2048]
                dve_tt(S4, SA[:, 8:2056], SA[:, 6:2054], ALU.add, [r_SA], [r_SB, r_uT[0]])
                cur, rcur = None, r_SB
                cv1 = S4
            if g >= 2:
                if g == 2:
                    P.add("dve", lambda e: e.memset(SB_[:, 0:8], 0.0), (), [r_SB, r_uT[0], r_uT[1]])
                dve_tt(SB_[:, 8:2056], SA[:, 8:2056], SA[:, 6:2054], ALU.add, [r_SA], [r_SB, r_uT[0], r_uT[1]])
                cur, rcur = SB_, r_SB
            if g >= 2:
                dve_tt(SA[:, 8:2056], SB_[:, 8:2056], SB_[:, 4:2052], ALU.add, [r_SB], [r_SA])
                cur, rcur = SA, r_SA
            if g >= 3:
                dve_tt(SB_[:, 8:2056], SA[:, 8:2056], SA[:, 0:2048], ALU.add, [r_SA], [r_SB])
                cur, rcur = SB_, r_SB
            w = WIN[g]
            cv = cv1 if g == 1 else cur[:, 8:2056]
            P.add("dve", (lambda cv=cv, u=u, g=g, w=w: (lambda e: e.scalar_tensor_tensor(
                out=PLT[:, g, :], in0=cv, scalar=1.0 / w, in1=u, op0=ALU.mult, op1=ALU.subtract)))(),
                [rcur, r_uT[g]], [r_PLT[g]])
            tmp = ST[:, 272:288]
            P.add("dve", (lambda cv=cv, g=g, tmp=tmp: (lambda e: e.tensor_tensor(
                out=tmp, in0=cv[:, 0:16], in1=ICN[:, g * 16:(g + 1) * 16], op=ALU.mult)))(),
                [rcur, r_icn], [r_tmp])
            P.add("dve", (lambda u=u, g=g, tmp=tmp: (lambda e: e.tensor_tensor(
                out=PLT[:, g, 0:16], in0=tmp, in1=u[:, 0:16], op=ALU.subtract)))(),
                [r_tmp, r_uT[g]], [r_PLT[g]])
        for g in range(4):
            for s in range(4):
                b = nextbank()
                mm(bank(b), WPM[:, g * 128:(g + 1) * 128], PLT[:, g, s * 512:(s + 1) * 512], True, True,
                   [r_wpm, r_PLT[g]], [rbank[b]])
                act(YPT[:, g, s * 512:(s + 1) * 512], bank(b), AF.Copy, [rbank[b], r_psc], [r_YPT[g][s]],
                    scale=PSC[:, g:g + 1])

        r_hs = [Res("hs%d" % i) for i in range(8)]
        for i in range(8):
            r_hs[i].pending = [r_ws[i // 2]]
        hs_n = [0]

        def hslot_view(i):
            return WS[:, i * 2048:(i + 1) * 2048].rearrange("p (k e) -> p k e", k=8)

        def hload(parts):
            i = hs_n[0] % 8
            hs_n[0] += 1
            v = hslot_view(i)
            oi = []
            for (k0, src) in parts:
                oi.append((v[:, k0:k0 + src.shape[1], :], src))
            dma2("pool", oi, "hs%d" % i, writes=[r_hs[i]])
            return i

        def p5_load(gq, which):
            c0 = gq * 256
            if which == "a":
                v = w_gate_d.rearrange("(k p) e -> p k e", p=128)
                return hload([(0, v[:, 0:4, c0:c0 + 256]), (4, v[:, 4:8, c0:c0 + 256])])
            if which == "b":
                v = w_gate_d.rearrange("(k p) e -> p k e", p=128)
                return hload([(0, v[:, 0:4, 1024 + c0:1024 + c0 + 256]), (4, v[:, 4:8, 1024 + c0:1024 + c0 + 256])])
            vp = w_brp_d.rearrange("(k p) e -> p k e", p=128)
            vs = w_brs_d.rearrange("(k p) e -> p k e", p=128)
            return hload([(0, vp[:, :, c0:c0 + 256]), (4, vs[:, :, c0:c0 + 256])])
        p5hs = {}

        def p5_prefetch():
            for gq in (0, 1):
                for wh in "abc":
                    p5hs[(gq, wh)] = p5_load(gq, wh)
            p5hs[(2, "a")] = p5_load(2, "a")
            p5hs[(2, "b")] = p5_load(2, "b")
        p5w0 = [None]

        if stop_after == "P3":
            P.frozen = True
        def f32v(off):
            return RX[:, off:off + 1024].rearrange("p (h n) -> p h n", h=2)

        def bf16v(off32):
            return RXb[:, 2 * off32:2 * off32 + 1024].rearrange("p (h n) -> p h n", h=2)
        E_ = [f32v(4096), f32v(5120)]
        ARG = [f32v(6144), f32v(7168)]
        R1f_a = R1[:].bitcast(F32)
        CCS = [f32v(8192), R1f_a[:, 12288:13312].rearrange("p (h n) -> p h n", h=2)]
        SP_ = [bf16v(9216), bf16v(9728)]
        ATT = [bf16v(14400), bf16v(14912)]
        r_E = [Res("E0"), Res("E1")]
        r_ARG = [Res("ARG0"), Res("ARG1")]
        r_CCS = [Res("CC0"), Res("CC1")]
        r_CC = r_CCS[0]
        r_CCS[1].pending = list(r_PLT)
        r_SP = [Res("SP0"), Res("SP1")]
        r_ATT = [Res("ATT0"), Res("ATT1")]
        old = r_uT + [r_SA, r_SB]
        for r in r_E + r_ARG + [r_CC] + r_SP:
            r.pending = list(old)
        for c in range(4):
            for s in range(4):
                r_OT[c][s].pending = list(old)

        ZP = [PP[0], PP[1]]
        r_Z = [[rbank[0], rbank[1]], [rbank[2], rbank[3]]]
        TP = PP[2]
        r_T = [rbank[4], rbank[5]]
        OACC = [bank(6), bank(6)]
        r_OACC = [rbank[6], rbank[6]]

        chains = []
        chain_id = 0
        for j in range(4):
            for hp in range(4):
                kbs = [4 * j + 3, 4 * j + 2, 4 * j + 1, 4 * j] + list(range(4 * j - 1, -1, -1))
                ch = []
                for i, kb in enumerate(kbs):
                    c0 = 128 * (kb - 4 * j) if kb >= 4 * j else 0
                    ch.append(dict(j=j, hp=hp, kb=kb, c0=c0, first=(i == 0), last=(i == len(kbs) - 1),
                                   diag=(kb >= 4 * j), chain=chain_id))
                chains.append(ch)
                chain_id += 1
        tiles = []
        for ch in chains:
            tiles.extend(ch)
        sched = {}
        for s_ in range(NPRE, 4):
            lo = 0 if s_ == NPRE else min(n for n, t in enumerate(tiles) if t["j"] == s_ - 1)
            hi = min(n for n, t in enumerate(tiles) if t["j"] == s_)
            micro = deferred[s_]
            for g, op_ in enumerate(micro):
                n = lo + (g * (hi - lo)) // len(micro)
                sched.setdefault(n, []).append(op_)
        last_sched = max(sched)
        NTL = len(tiles)

        def pv(t3, c0):
            return t3[:, :, c0:512]

        def z3(zb):
            return ZP[zb][:].rearrange("p (h n) -> p h n", h=2)

        def emit_qk(n):
            t = tiles[n]
            zb = n % 2
            j, hp, kb, c0 = t["j"], t["hp"], t["kb"], t["c0"]
            for hd in range(2):
                rows = slice(64 * hd, 64 * hd + 64)
                out = ZP[zb][:, hd * 512 + c0:hd * 512 + 512]
                lhsT = KT[rows, hp, kb * 128:(kb + 1) * 128]
                rhs = QT[rows, hp, j * 512 + c0:(j + 1) * 512]
                mm(out, lhsT, rhs, True, not t["diag"], [r_KT[hp][kb // 4], r_QT[hp][j]], [r_Z[zb][hd]])
            if t["diag"]:
                for hd in range(2):
                    out = ZP[zb][:, hd * 512 + c0:hd * 512 + c0 + 128]
                    mm(out, ident, maskneg, False, True, [r_cbf], [r_Z[zb][hd]])

        def emit_exp1(n):
            t = tiles[n]
            zb = n % 2
            act(pv(E_[zb], t["c0"]), pv(z3(zb), t["c0"]), AF.Exp, r_Z[zb], [r_E[zb]])

        def emit_ln(n):
            t = tiles[n]
            zb = n % 2
            act(pv(SP_[zb], t["c0"]), pv(E_[zb], t["c0"]), AF.Ln, [r_E[zb], r_st0], [r_SP[zb]], bias=1.0)

        def emit_tri(n):
            t = tiles[n]
            zb = n % 2
            c0 = t["c0"]
            for hd in range(2):
                out = ZP[zb][:, hd * 512 + c0:hd * 512 + 512]
                mm(out, negtri, SP_[zb][:, hd, c0:512], False, True, [r_cbf, r_SP[zb]], [r_Z[zb][hd]], skip=True)
            for hd in range(2):
                out = TP[:, hd * 512 + c0:hd * 512 + 512]
                mm(out, ones, SP_[zb][:, hd, c0:512], True, True, [r_cbf, r_SP[zb]], [r_T[hd]])

        def emit_dve(n):
            t = tiles[n]
            zb = n % 2
            c0 = t["c0"]
            CC = CCS[t["chain"] % 2]
            rcc = r_CCS[t["chain"] % 2]
            if t["first"]:
                P.add("pool", (lambda CC=CC: (lambda e: e.memset(CC, 0.0)))(), (), [rcc])
            a = pv(ARG[zb], c0)
            z = pv(z3(zb), c0)
            c = pv(CC, c0)
            tp = pv(TP[:].rearrange("p (h n) -> p h n", h=2), c0)
            dve_tt(a, z, c, ALU.subtract, r_Z[zb] + [rcc], [r_ARG[zb]])
            if not t["last"]:
                dve_tt(c, tp, c, ALU.add, r_T + [rcc], [rcc])

        def emit_exp2(n):
            t = tiles[n]
            zb = n % 2
            act(pv(ATT[zb], t["c0"]), pv(ARG[zb], t["c0"]), AF.Exp, [r_ARG[zb]], [r_ATT[zb]])

        def emit_av(n):
            t = tiles[n]
            zb = n % 2
            c0, hp, kb, j = t["c0"], t["hp"], t["kb"], t["j"]
            ob = t["chain"] % 2
            for hd in range(2):
                out = OACC[ob][64 * hd:64 * hd + 64, c0:512]
                lhsT = VV[:, kb, (2 * hp + hd) * 64:(2 * hp + hd) * 64 + 64]
                mm(out, lhsT, ATT[zb][:, hd, c0:512], t["first"], t["last"], [r_V[kb], r_ATT[zb]], [r_OACC[ob]], skip=True)
            if t["last"]:
                dst = OT[:, hp, j * 512:(j + 1) * 512]
                src = OACC[ob]
                P.add("dve", lambda e: e.tensor_copy(dst, src), [r_OACC[ob]], [r_OT[hp][j]])

        emit_qk(0)
        for n in range(NTL + 2):
            if n + 1 < NTL:
                emit_qk(n + 1)
            if n < NTL:
                emit_exp1(n)
            if 2 <= n:
                emit_exp2(n - 2)
            if n < NTL:
                emit_ln(n)
                emit_tri(n)
                emit_dve(n)
            if 2 <= n:
                emit_av(n - 2)
            for op_ in sched.get(n, []):
                op_()
            if n == last_sched:
                p5_prefetch()

        if stop_after == "P4":
            P.frozen = True
        NXS = 6
        R1f = R1[:].bitcast(F32)
        XS = [R1f[:, 8192 + j * 1024:8192 + (j + 1) * 1024] for j in range(NXS)]
        r_xs = [Res("xs%d" % j) for j in range(NXS)]
        for j in range(NXS):
            r_xs[j].pending = r_V + r_PLT + [r_CCS[1]]
            dma("sp", XS[j], x_d[j * 128:(j + 1) * 128, :], "xs%d" % j, writes=[r_xs[j]])
        MT = R1[:, 0:16384].rearrange("p (c t) -> p c t", c=8)
        r_MT = [[Res("MT%d_%d" % (c, s)) for s in range(4)] for c in range(8)]
        oldq = [r for row in r_QT for r in row] + [r for row in r_KT for r in row]
        for c in range(8):
            for s in range(4):
                r_MT[c][s].pending = list(oldq)
        TM = [[RX[:, 4096 + (i * 4 + q) * 512:4096 + (i * 4 + q + 1) * 512] for q in range(4)] for i in range(2)]
        r_TM = [[Res("TM%d_%d" % (i, q)) for q in range(4)] for i in range(2)]
        olda = r_E + r_ARG + [r_CC] + r_SP
        for i in range(2):
            for q in range(4):
                r_TM[i][q].pending = list(olda)
        it = 0
        so = [None, None]
        for gq in range(4):
            ha, hb_, hc = p5hs[(gq, "a")], p5hs[(gq, "b")], p5hs[(gq, "c")]
            wa, wb, wc = hslot_view(ha), hslot_view(hb_), hslot_view(hc)
            for dcl in range(2):
                dc = 2 * gq + dcl
                cs = slice(dcl * 128, dcl * 128 + 128)
                for s in range(4):
                    pb = 4 * (it % 2)
                    ti = it % 2
                    it += 1
                    bgp, bgs, byp, bys = pb, pb + 1, pb + 2, pb + 3
                    for k in range(8):
                        mm(bank(bgp), wa[:, k, cs], hT_span(k, s), k == 0, k == 7,
                           [r_hs[ha]] + r_hT[4 * s:4 * s + 4], [rbank[bgp]])
                    for k in range(8):
                        mm(bank(bgs), wb[:, k, cs], hT_span(k, s), k == 0, k == 7,
                           [r_hs[hb_]] + r_hT[4 * s:4 * s + 4], [rbank[bgs]])
                    for k in range(4):
                        mm(bank(byp), wc[:, k, cs], YPT[:, k, s * 512:(s + 1) * 512], k == 0, k == 3,
                           [r_hs[hc], r_YPT[k][s]], [rbank[byp]])
                    for k in range(4):
                        mm(bank(bys), wc[:, 4 + k, cs], OT[:, k, s * 512:(s + 1) * 512], k == 0, k == 3,
                           [r_hs[hc], r_OT[k][s]], [rbank[bys]])
                    act(TM[ti][0], bank(bgp), AF.Sigmoid, [rbank[bgp], r_bgt], [r_TM[ti][0]], bias=BGT[:, dc:dc + 1])
                    act(TM[ti][1], bank(bgs), AF.Sigmoid, [rbank[bgs], r_bgt], [r_TM[ti][1]], bias=BGT[:, 8 + dc:9 + dc])
                    dve_tt(TM[ti][2], bank(byp), TM[ti][0], ALU.mult, [rbank[byp], r_TM[ti][0]], [r_TM[ti][2]])
                    dve_tt(TM[ti][3], bank(bys), TM[ti][1], ALU.mult, [rbank[bys], r_TM[ti][1]], [r_TM[ti][3]])
                    dve_tt(MT[:, dc, s * 512:(s + 1) * 512], TM[ti][2], TM[ti][3], ALU.add,
                           [r_TM[ti][2], r_TM[ti][3]], [r_MT[dc][s]])
            if gq == 0:
                p5hs[(2, "c")] = p5_load(2, "c")
                p5hs[(3, "a")] = p5_load(3, "a")
                p5hs[(3, "b")] = p5_load(3, "b")
            elif gq == 1:
                p5hs[(3, "c")] = p5_load(3, "c")
                assert hs_n[0] == 12
                for s_ in range(4):
                    r_ws[s_].pending = [r_hs[2 * s_], r_hs[2 * s_ + 1]]
                ws_n[0] = 6
                so[0] = wload(wcols(w_out_d, 0))
            elif gq == 2:
                so[1] = wload(wcols(w_out_d, 512))

        if stop_after == "P5":
            P.frozen = True
        oldrx = [r for row in r_YPT for r in row] + [r for row in r_OT for r in row] + \
                [r for row in r_TM for r in row] + r_ATT
        r_x1h = [[Res("x1_%d_0" % t), Res("x1_%d_1" % t)] for t in range(NT)]
        for tt in range(NT):
            r_x1h[tt][0].pending = list(oldrx)
            r_x1h[tt][1].pending = list(oldrx)
            if tt >= NXS:
                dma("sp", xt(tt), x_d[tt * 128:(tt + 1) * 128, :], "x%d" % tt, reads=[r_ws[so[1]]], writes=r_x1h[tt])
        r_h2T = [Res("h2T%d" % t) for t in range(NT)]
        for tt in range(NT):
            r_h2T[tt].pending = list(r_hT)
        ssA = stcol(32)
        ssS = stcol(16)
        lnA = stcol(16)
        rsA = stcol(16)
        ssB = stcol(16)
        lnB = stcol(16)
        rsB = stcol(16)
        r_stA = [Res("stA%d" % t) for t in range(NT)]
        r_stB = [Res("stB%d" % t) for t in range(NT)]
        HBS4 = HBS + [R1[:, 28672:29696], R1[:, 29696:30720]]
        r_hb4 = r_hb + [Res("hb2"), Res("hb3")]
        JK6 = R1[:, 30720:31232]
        r_jk6 = Res("jk6")
        for r in r_hb4[2:] + [r_jk6]:
            r.pending = r_V + r_PLT + [r_CCS[1]]

        JK6w = R1[:, 30720:31744]

        def p6_mm(T):
            b0 = 2 * (T % 3)
            for dh in range(2):
                wv = wslot_view(so[dh])
                for k in range(8):
                    mm(bank(b0 + dh), MT[:, k, T * 128:(T + 1) * 128], wv[:, k, :], k == 0, k == 7,
                       [r_ws[so[dh]], r_MT[k][T // 4]], [rbank[b0 + dh]])

        def p6_sqA(T):
            b0 = 2 * (T % 3)
            act(JK6w, PP[T % 3][:], AF.Square, [rbank[b0], rbank[b0 + 1]], [r_jk6, r_stA[T]], accum=ssS[:, T:T + 1])

        def p6_lnA(T):
            act(lnA[:, T:T + 1], ssS[:, T:T + 1], AF.Ln, [r_stA[T], r_st0], [r_stA[T]], bias=EPS, scale=1.0 / D)

        def p6_expA(T):
            act(rsA[:, T:T + 1], lnA[:, T:T + 1], AF.Exp, [r_stA[T]], [r_stA[T]], scale=-0.5)

        def p6_res(T):
            b0 = 2 * (T % 3)
            for dh in range(2):
                P.add("dve", (lambda b=b0 + dh, T=T, dh=dh: (lambda e: e.scalar_tensor_tensor(
                    out=bank(b), in0=bank(b), scalar=rsA[:, T:T + 1], in1=GB[:, dh * 512:(dh + 1) * 512],
                    op0=ALU.mult, op1=ALU.mult)))(),
                    [rbank[b0 + dh], r_stA[T], r_gb[0]], [rbank[b0 + dh]])
            for dh in range(2):
                xo = xt(T)[:, dh * 512:(dh + 1) * 512]
                if T < NXS:
                    xi = XS[T][:, dh * 512:(dh + 1) * 512]
                    dve_tt(xo, bank(b0 + dh), xi, ALU.add, [rbank[b0 + dh], r_xs[T]], [r_x1h[T][dh]])
                else:
                    dve_tt(xo, bank(b0 + dh), xo, ALU.add, [rbank[b0 + dh], r_x1h[T][dh]], [r_x1h[T][dh]])

        def p6_sqB(T):
            act(HBS4[T % 4], xt(T), AF.Square, r_x1h[T], [r_hb4[T % 4], r_stB[T]], accum=ssB[:, T:T + 1])

        def p6_lnB(T):
            act(lnB[:, T:T + 1], ssB[:, T:T + 1], AF.Ln, [r_stB[T], r_st0], [r_stB[T]], bias=EPS, scale=1.0 / D)

        def p6_expB(T):
            act(rsB[:, T:T + 1], lnB[:, T:T + 1], AF.Exp, [r_stB[T]], [r_stB[T]], scale=-0.5)

        def p6_h(T):
            hb = HBS4[T % 4]
            P.add("dve", (lambda hb=hb, T=T: (lambda e: e.scalar_tensor_tensor(
                out=hb, in0=xt(T), scalar=rsB[:, T:T + 1], in1=GB[:, 1024:2048], op0=ALU.mult, op1=ALU.mult)))(),
                r_x1h[T] + [r_stB[T], r_gb[1]], [r_hb4[T % 4]])

        def ok(T):
            return 0 <= T < NT

        for i in range(NT + 6):
            if ok(i - 5):
                norm_c(i - 5, r_h2T[i - 5], 6 + (i - 5) % 2, HBS4, r_hb4, evac="none")
            if ok(i):
                p6_mm(i)
            if ok(i - 1):
                p6_sqA(i - 1)
            if ok(i - 3):
                p6_sqB(i - 3)
            if ok(i - 1):
                p6_lnA(i - 1)
            if ok(i - 3):
                p6_lnB(i - 3)
            if ok(i - 1):
                p6_expA(i - 1)
            if ok(i - 3):
                p6_expB(i - 3)
            if ok(i - 6):
                norm_evac(i - 6, r_h2T[i - 6], 6 + (i - 6) % 2)
            if ok(i - 2):
                p6_res(i - 2)
            if ok(i - 4):
                p6_h(i - 4)

        if stop_after == "P6":
            P.frozen = True
        AT = R1[:].rearrange("p (c t) -> p c t", c=32)
        r_AT = [[Res("AT%d_%d" % (c, s)) for s in range(2)] for c in range(32)]
        oldm = [r for row in r_MT for r in row] + r_V + r_PLT + r_xs + r_hb4[2:] + [r_jk6, r_CCS[1]]
        dma("sp", GB[:, 0:1024], gains_d[3], "gb0", writes=[r_gb[0]])
        FFS = [R2[:, h * 4096:(h + 1) * 4096].rearrange("p (t e) -> p t e", t=8) for h in range(2)]
        ssC = stcol(32)
        ssT = stcol(16)
        lnC = stcol(16)
        rsC = stcol(16)
        r_stC = [Res("stC%d" % t) for t in range(NT)]
        r_rt = [Res("rt0"), Res("rt1")]
        for r in r_rt:
            r.pending = [r_gb[1]]
        for half in range(2):
            for c in range(32):
                for s in range(2):
                    r_AT[c][s].pending = list(oldm)
            oldm = []
            for f4 in range(8):
                su = wload(wcols(w_up_d, f4 * 512))
                wv = wslot_view(su)
                for fl in range(4):
                    fc = 4 * f4 + fl
                    for s in range(2):
                        b = nextbank()
                        for k in range(8):
                            mm(bank(b), wv[:, k, fl * 128:(fl + 1) * 128],
                               hT[:, half, k, s * 512:(s + 1) * 512], k == 0, k == 7,
                               [r_ws[su]] + r_h2T[8 * half + 4 * s:8 * half + 4 * s + 4], [rbank[b]])
                        dst = AT[:, fc, s * 512:(s + 1) * 512]
                        ti = b % 2
                        tmpr = GB[:, 1024 + ti * 512:1024 + (ti + 1) * 512]
                        act(tmpr, bank(b), AF.Relu, [rbank[b]], [r_rt[ti]])
                        dve_tt(dst, bank(b), tmpr, ALU.mult, [rbank[b], r_rt[ti]], [r_AT[fc][s]])
            if stop_after == "P7a":
                P.frozen = True
            r_ffs = [Res("ffs%d_%d" % (half, t)) for t in range(8)]
            for t in range(8):
                r_ffs[t].pending = list(r_h2T[8 * half:8 * half + 8])
            for dh in range(2):
                for f8 in range(4):
                    vd = w_down_d.rearrange("(c p) d -> p c d", p=128)
                    sd = wload([(0, vd[:, 8 * f8:8 * f8 + 4, dh * 512:(dh + 1) * 512]),
                                (4, vd[:, 8 * f8 + 4:8 * f8 + 8, dh * 512:(dh + 1) * 512])])
                    wv = wslot_view(sd)
                    if f8 == 3:
                        for t in range(8):
                            for fl in range(8):
                                fc = 8 * f8 + fl
                                mm(bank(t), AT[:, fc, t * 128:(t + 1) * 128], wv[:, fl, :], fc == 0, fc == 31,
                                   [r_ws[sd], r_AT[fc][t // 4]], [rbank[t]])
                    else:
                        for fl in range(8):
                            fc = 8 * f8 + fl
                            for t in range(8):
                                mm(bank(t), AT[:, fc, t * 128:(t + 1) * 128], wv[:, fl, :], fc == 0, fc == 31,
                                   [r_ws[sd], r_AT[fc][t // 4]], [rbank[t]])
                if stop_after == "P7b":
                    P.frozen = True
                def ev_stats(t, half=half, dh=dh):
                    tt = 8 * half + t
                    act(HBS[t % 2][:, 0:512], bank(t), AF.Square, [rbank[t]], [r_hb[t % 2], r_stC[tt]],
                        accum=ssC[:, 2 * tt + dh:2 * tt + dh + 1])
                    if dh == 0:
                        P.add("dve", (lambda t=t, half=half: (lambda e: e.tensor_copy(FFS[half][:, t, :], bank(t))))(),
                              [rbank[t]], [r_ffs[t]])
                    else:
                        dve_tt(ssT[:, tt:tt + 1], ssC[:, 2 * tt:2 * tt + 1], ssC[:, 2 * tt + 1:2 * tt + 2], ALU.add,
                               [r_stC[tt]], [r_stC[tt]])
                        act(lnC[:, tt:tt + 1], ssT[:, tt:tt + 1], AF.Ln, [r_stC[tt], r_st0], [r_stC[tt]],
                            bias=EPS, scale=1.0 / D)
                        act(rsC[:, tt:tt + 1], lnC[:, tt:tt + 1], AF.Exp, [r_stC[tt]], [r_stC[tt]], scale=-0.5)

                def ev_big(t, half=half):
                    tt = 8 * half + t
                    for d2 in range(2):
                        src = FFS[half][:, t, :] if d2 == 0 else bank(t)
                        rsrc = r_ffs[t] if d2 == 0 else rbank[t]
                        P.add("dve", (lambda src=src, tt=tt, d2=d2: (lambda e: e.scalar_tensor_tensor(
                            out=src, in0=src, scalar=rsC[:, tt:tt + 1], in1=GB[:, d2 * 512:(d2 + 1) * 512],
                            op0=ALU.mult, op1=ALU.mult)))(),
                            [rsrc, r_stC[tt], r_gb[0]], [rsrc])
                    for d2 in range(2):
                        src = FFS[half][:, t, :] if d2 == 0 else bank(t)
                        rsrc = r_ffs[t] if d2 == 0 else rbank[t]
                        xs = xt(tt)[:, d2 * 512:(d2 + 1) * 512]
                        P.add("dve", (lambda xs=xs, src=src: (lambda e: e.tensor_tensor(out=xs, in0=src, in1=xs, op=ALU.add)))(),
                              [rsrc, r_x1h[tt][d2]], [r_x1h[tt][d2]])
                    if stop_after != "P7d":
                        dma("sp", out_d[tt * 128:(tt + 1) * 128, :], xt(tt), "st%d" % tt, reads=r_x1h[tt])

                for t in range(9):
                    if t < 8:
                        ev_stats(t)
                    if dh == 1 and t >= 1:
                        ev_big(t - 1)
                if stop_after == "P7c" and dh == 0:
                    P.frozen = True

        P.frozen = False
        if debug:
            dbg_src = {
                "hT": (R2b, []), "R1": (R1[:], []), "RX": (RX[:], []),
            }
            for name, _, _ in debug:
                src, _r = dbg_src[name]
                P.add("sp", (lambda src=src, name=name: (lambda e: [e.dma_start(out=dbg_d[name], in_=src)]))(),
                      [], [], dma_key="dbg_" + name, ninst=1, barrier=True)

        P.finalize()
        P.plan_waits()
        for key in P.dma_cnt:
            sem_dma[key] = es.enter_context(nc.semaphore("sd_" + key))
        final_waits = [(k, v) for k, v in P.dma_cnt.items() if k.startswith("st") or k.startswith("dbg_")]

        def emit_engine(name, e):
            waited = {}
            for op in P.ops:
                if op.eng != name:
                    continue
                todo = [(sem_dma[key[1]] if key[0] == "d" else sem_eng[key[1]], val) for key, val in op.waits]
                embed = None
                if todo and op.dma_key is None and name in ("act", "dve", "pool") and not op.multi:
                    embed = todo.pop()
                for sem, val in todo:
                    e.wait_ge(sem, val)
                res = op.fn(e)
                if embed is not None:
                    res._wait_ge(embed[0], embed[1])
                if op.dma_key is not None:
                    for inst in res:
                        inst.then_inc(sem_dma[op.dma_key], 16)
                elif op.signal:
                    res.then_inc(sem_eng[name], 1)
            if name == "sp":
                if debug:
                    for en in ("pe", "act", "dve", "pool"):
                        pass
                for k, v in final_waits:
                    e.wait_ge(sem_dma[k], v)

        with nc.Block() as block:
            @block.tensor
            def _(e):
                emit_engine("pe", e)

            @block.scalar
            def _(e):
                emit_engine("act", e)

            @block.vector
            def _(e):
                emit_engine("dve", e)

            @block.gpsimd
            def _(e):
                emit_engine("pool", e)

            @block.sync
            def _(e):
                emit_engine("sp", e)
    return nc


_NC_CACHE = {}


def _consts():
    bf = ml_dtypes.bfloat16
    p = np.arange(128)[:, None]
    c = np.arange(128)[None, :]
    ident = (p == c).astype(np.float32)
    negtri = -(p >= c).astype(np.float32)
    ones = np.ones((128, 128), np.float32)
    maskneg = np.where(p < c, 0.0, NEG).astype(np.float32)
    cbf = np.concatenate([ident, negtri, ones, maskneg], axis=1).astype(bf)
    invcnt = np.zeros((128, 64), np.float32)
    for g, w in enumerate((2, 4, 8, 16)):
        t = np.arange(16)
        invcnt[:, g * 16:(g + 1) * 16] = (1.0 / np.minimum(t + 1, w))[None, :]
    return cbf, invcnt


def kernel(x, g_pre_mix, w_in, w_pool_mix, pool_scale, w_br_pool, w_br_sb, w_gate, b_gate,
           w_out, g_post_mix, g_pre_mlp, w_up, w_down, g_post_mlp, _debug=None, _stop=None):
    f = lambda a: np.ascontiguousarray(np.asarray(a, dtype=np.float32))
    x = f(x)
    B = x.shape[0]
    key = "dbg" if _debug else "main"
    if key not in _NC_CACHE:
        _NC_CACHE[key] = build_nc(_debug, _stop)
    nc = _NC_CACHE[key]
    cbf, invcnt = _consts()
    gains = np.stack([np.broadcast_to(f(g)[None, :], (128, D)) for g in
                      (g_pre_mix, g_post_mix, g_pre_mlp, g_post_mlp)]).copy()
    pscale = np.ascontiguousarray(f(pool_scale).reshape(4, 128).T)
    bgate = np.ascontiguousarray(f(b_gate).reshape(16, 128).T)
    shared = {
        "w_in": f(w_in), "w_gate": f(w_gate), "w_br_pool": f(w_br_pool), "w_br_sb": f(w_br_sb),
        "w_out": f(w_out), "w_up": f(w_up), "w_down": f(w_down), "w_pool_mix": f(w_pool_mix),
        "gains": gains, "pscale": pscale, "bgate": bgate, "invcnt": invcnt, "cbf": cbf,
    }
    in_maps = [dict(shared, x=x[b]) for b in range(B)]
    res = run_bass_kernel_spmd(nc, in_maps, core_ids=list(range(B)))
    out = np.stack([np.asarray(r["out"], dtype=np.float32) for r in res.results], axis=0)
    if _debug:
        return out, [{k: np.asarray(v) for k, v in r.items()} for r in res.results]
    return out
```

```python
import os
import numpy as np
import ml_dtypes
from contextlib import ExitStack
import concourse.bass as bass
import concourse.mybir as mybir
from concourse.bass_utils import run_bass_kernel_spmd

F32 = mybir.dt.float32
BF16 = mybir.dt.bfloat16
AF = mybir.ActivationFunctionType
ALU = mybir.AluOpType

S = 2048
D = 1024
NT = S // 128
NEG = -30000.0
EPS = 1e-6


class Res:
    __slots__ = ("name", "w", "rs", "pending", "excl")

    def __init__(self, name, excl=False):
        self.name = name
        self.excl = excl
        self.w = None
        self.rs = {}
        self.pending = []


class Op:
    __slots__ = ("idx", "eng", "fn", "dma_key", "ninst", "deps", "signal", "sigval", "multi", "waits")

    def __init__(self, idx, eng, fn, dma_key, ninst):
        self.idx = idx
        self.eng = eng
        self.fn = fn
        self.dma_key = dma_key
        self.ninst = ninst
        self.deps = set()
        self.signal = False
        self.sigval = 0
        self.multi = False
        self.waits = []


class Prog:
    ENGS = ("pe", "act", "dve", "pool", "sp")

    def __init__(self):
        self.ops = []
        self.dma_cnt = {}
        self.frozen = False

    def add(self, eng, fn, reads=(), writes=(), dma_key=None, ninst=1, barrier=False):
        if self.frozen:
            return None
        op = Op(len(self.ops), eng, fn, dma_key, ninst)
        if barrier:
            last = {}
            for o in self.ops:
                last[(o.eng, o.dma_key)] = o
            for o in last.values():
                op.deps.add(o)
        is_dma = dma_key is not None
        writes = list(writes)
        extra = []
        for w in writes:
            if w.pending:
                extra.extend(w.pending)
                w.pending = []
        deps = []
        for r in reads:
            if r.w is not None:
                deps.append((r.w, "raw"))
            if r.excl:
                for rd in r.rs.values():
                    if rd.eng != eng:
                        deps.append((rd, "raw"))
        for w in writes + extra:
            if w.w is not None:
                deps.append((w.w, "waw"))
            for rd in w.rs.values():
                deps.append((rd, "war"))
        for d, kind in deps:
            if d is op:
                continue
            d_dma = d.dma_key is not None
            if (not d_dma) and (not is_dma) and d.eng == eng and eng == "pe":
                continue
            op.deps.add(d)
        for r in reads:
            r.rs[(eng, dma_key)] = op
        for w in writes:
            w.w = op
            w.rs = {}
        if is_dma:
            self.dma_cnt[dma_key] = self.dma_cnt.get(dma_key, 0) + 16 * ninst
            op.sigval = self.dma_cnt[dma_key]
        self.ops.append(op)
        return op

    def finalize(self):
        for op in self.ops:
            for d in op.deps:
                if d.dma_key is None:
                    d.signal = True
        cnt = {e: 0 for e in self.ENGS}
        for op in self.ops:
            if op.dma_key is None and op.signal:
                cnt[op.eng] += 1
                op.sigval = cnt[op.eng]


    def plan_waits(self):
        waited = {e: {} for e in self.ENGS}
        know = {}
        for op in self.ops:
            w = waited[op.eng]
            need = {}
            for d in op.deps:
                key = ("d", d.dma_key) if d.dma_key is not None else ("e", d.eng)
                if d.sigval > need.get(key, 0):
                    need[key] = d.sigval
            op.waits = []
            cand = [(key, val) for key, val in sorted(need.items(), key=lambda kv: str(kv[0])) if w.get(key, 0) < val]
            keep = []
            for c in cand:
                implied = False
                for o in cand:
                    if o is not c and know.get(o, {}).get(c[0], 0) >= c[1] and not (
                            know.get(c, {}).get(o[0], 0) >= o[1] and cand.index(c) < cand.index(o)):
                        implied = True
                        break
                if not implied:
                    keep.append(c)
            for key, val in keep:
                op.waits.append((key, val))
            for key, val in cand:
                w[key] = max(w.get(key, 0), val)
                for k2, v2 in know.get((key, val), {}).items():
                    if v2 > w.get(k2, 0):
                        w[k2] = v2
            if op.dma_key is not None:
                know[(("d", op.dma_key), op.sigval)] = dict(w)
            elif op.signal:
                kk = dict(w)
                kk[("e", op.eng)] = max(kk.get(("e", op.eng), 0), op.sigval)
                know[(("e", op.eng), op.sigval)] = kk


def build_nc(debug=None, stop_after=None):
    nc = bass.Bass("TRN2", target_bir_lowering=False)
    P = Prog()

    def dram_in(name, shape, dt=F32):
        return nc.dram_tensor(name, list(shape), dt, kind="ExternalInput").ap()

    x_d = dram_in("x", [S, D])
    w_in_d = dram_in("w_in", [D, 2048])
    w_gate_d = dram_in("w_gate", [D, 2048])
    w_brp_d = dram_in("w_br_pool", [512, D])
    w_brs_d = dram_in("w_br_sb", [512, D])
    w_out_d = dram_in("w_out", [D, D])
    w_up_d = dram_in("w_up", [D, 4096])
    w_down_d = dram_in("w_down", [4096, D])
    wpm_d = dram_in("w_pool_mix", [4, 128, 128])
    gains_d = dram_in("gains", [4, 128, D])
    pscale_d = dram_in("pscale", [128, 4])
    bgate_d = dram_in("bgate", [128, 16])
    invcnt_d = dram_in("invcnt", [128, 64])
    cbf_d = dram_in("cbf", [128, 512], BF16)
    out_d = nc.dram_tensor("out", [S, D], F32, kind="ExternalOutput").ap()
    dbg_d = {}
    if debug:
        for name, shape, dt in debug:
            dbg_d[name] = nc.dram_tensor("dbg_" + name, list(shape), dt, kind="ExternalOutput").ap()

    es = ExitStack()
    with es:
        def sb(name, shape, dt):
            return es.enter_context(nc.sbuf_tensor(name, list(shape), dt))

        RX = sb("RX", [128, 16384], F32)
        R1 = sb("R1", [128, 32768], BF16)
        R2 = sb("R2", [128, 8192], F32)
        WS = sb("WS", [128, 16384], BF16)
        GB = sb("GB", [128, 2048], F32)
        HB = sb("HB", [128, 1024], BF16)
        JK = sb("JK", [128, 1024], BF16)
        CBF = sb("CBF", [128, 512], BF16)
        WPM = sb("WPM", [128, 512], BF16)
        PSC = sb("PSC", [128, 4], F32)
        BGT = sb("BGT", [128, 16], F32)
        ICN = sb("ICN", [128, 64], F32)
        ST = sb("ST", [128, 288], F32)
        PP = [es.enter_context(nc.psum_tensor("pp%d" % i, [128, 1024], F32)) for i in range(4)]

        sem_eng = {e: es.enter_context(nc.semaphore("se_" + e)) for e in Prog.ENGS}
        sem_dma = {}

        def bank(i):
            return PP[i // 2][:, (i % 2) * 512:(i % 2) * 512 + 512]

        rbank = [Res("bank%d" % i, excl=True) for i in range(8)]
        ident = CBF[:, 0:128]
        negtri = CBF[:, 128:256]
        ones = CBF[:, 256:384]
        maskneg = CBF[:, 384:512]

        EPSC = ST[:, 0:1]
        ONEC = ST[:, 1:2]
        st_next = [2]

        def stcol(n=1):
            c = st_next[0]
            st_next[0] += n
            assert st_next[0] <= 272
            return ST[:, c:c + n]

        def dma(eng, out, in_, key, reads=(), writes=()):
            P.add(eng, lambda e: [e.dma_start(out=out, in_=in_)], reads, writes, dma_key=key, ninst=1)

        def dma2(eng, outs_ins, key, reads=(), writes=()):
            def fn(e):
                return [e.dma_start(out=o, in_=i) for (o, i) in outs_ins]
            P.add(eng, fn, reads, writes, dma_key=key, ninst=len(outs_ins))

        def mm(out, lhsT, rhs, start, stop, reads, writes, skip=False):
            if skip:
                P.add("pe", lambda e: e.matmul(out, lhsT, rhs, start=start, stop=stop, skip_group_check=True), reads, writes)
            else:
                P.add("pe", lambda e: e.matmul(out, lhsT, rhs, start=start, stop=stop), reads, writes)

        def act(out, in_, func, reads, writes, bias=None, scale=1.0, accum=None):
            kw = {}
            if bias is not None:
                kw["bias"] = bias
            if accum is not None:
                kw["accum_out"] = accum
            o_ = P.add("act", lambda e: e.activation(out, in_, func, scale=scale, **kw), reads, writes)
            if o_ is not None and accum is not None:
                o_.multi = True

        def junk():
            return Res("junk")

        r_cbf = Res("cbf")
        r_wpm = Res("wpm")
        r_psc = Res("psc")
        r_bgt = Res("bgt")
        r_icn = Res("icn")
        r_st0 = Res("st0")
        r_gb = [Res("gb0"), Res("gb1")]
        dma("sp", CBF[:], cbf_d, "cbf", writes=[r_cbf])
        dma("sp", GB[:, 0:1024], gains_d[0], "gb0", writes=[r_gb[0]])
        dma("sp", PSC[:], pscale_d, "psc", writes=[r_psc])
        dma("sp", BGT[:], bgate_d, "bgt", writes=[r_bgt])
        dma("sp", ICN[:], invcnt_d, "icn", writes=[r_icn])
        P.add("pool", lambda e: e.memset(EPSC, EPS), (), [r_st0])
        P.add("pool", lambda e: e.memset(ONEC, 1.0), (), [r_st0])
        dma("pool", WPM[:].rearrange("p (g d) -> p g d", g=4), wpm_d.rearrange("g p d -> p g d"), "wpm", writes=[r_wpm])

        r_ws = [Res("ws%d" % i) for i in range(4)]
        ws_n = [0]

        def wslot_view(s):
            return WS[:, s * 4096:(s + 1) * 4096].rearrange("p (k e) -> p k e", k=8)

        def wload(parts, reads=()):
            s = ws_n[0] % 4
            ws_n[0] += 1
            v = wslot_view(s)
            oi = []
            for (k0, src) in parts:
                kk = src.shape[1]
                oi.append((v[:, k0:k0 + kk, :], src))
            dma2("pool", oi, "ws%d" % s, reads=reads, writes=[r_ws[s]])
            return s

        def wcols(wd, c0, k=8):
            v = wd.rearrange("(k p) e -> p k e", p=128)
            return [(0, v[:, 0:k // 2, c0:c0 + 512]), (k // 2, v[:, k // 2:k, c0:c0 + 512])]

        RXb = RX[:].bitcast(BF16)
        R2b = R2[:].bitcast(BF16)
        hT = R2b.rearrange("p (h k t) -> p h k t", h=2, k=8)

        def hT_span(k, s):
            return hT[:, s // 2, k, (s % 2) * 512:(s % 2) * 512 + 512]

        def hT_tile(k, tt):
            return hT[:, tt // 8, k, (tt % 8) * 128:(tt % 8) * 128 + 128]

        r_hT = [Res("hT%d" % t) for t in range(NT)]
        r_x = [Res("x%d" % t) for t in range(NT)]

        def xt(tt):
            return RX[:, tt * 1024:(tt + 1) * 1024]

        QT = R1[:, 0:8192].rearrange("p (c t) -> p c t", c=4)
        KT = R1[:, 8192:16384].rearrange("p (c t) -> p c t", c=4)
        VV = R1[:, 16384:24576].rearrange("p (t e) -> p t e", t=16)
        PLT = R1[:, 24576:32768].rearrange("p (c t) -> p c t", c=4)
        r_QT = [[Res("QT%d_%d" % (c, s)) for s in range(4)] for c in range(4)]
        r_KT = [[Res("KT%d_%d" % (c, s)) for s in range(4)] for c in range(4)]
        r_V = [Res("V%d" % t) for t in range(NT)]
        r_PLT = [Res("PLT%d" % g) for g in range(4)]

        def uT(g):
            return RX[:, g * 2048:(g + 1) * 2048]
        r_uT = [Res("uT%d" % g) for g in range(4)]
        SA = RX[:, 8192:10248]
        SB_ = RX[:, 0:2056]
        r_SA = Res("SA")
        r_SB = Res("SB")
        YPT = RXb[:, 20608:28800].rearrange("p (c t) -> p c t", c=4)
        r_YPT = [[Res("YPT%d_%d" % (g, s)) for s in range(4)] for g in range(4)]
        OT = RXb[:, 0:8192].rearrange("p (c t) -> p c t", c=4)
        r_OT = [[Res("OT%d_%d" % (c, s)) for s in range(4)] for c in range(4)]

        ss1 = stcol(16)
        ln1 = stcol(16)
        rs1 = stcol(16)
        r_ss1 = [Res("ss1_%d" % t) for t in range(NT)]
        r_hb = [Res("hb0"), Res("hb1")]

        HBS = [HB[:, 0:1024], JK[:, 0:1024]]

        def norm_b(tt, xin, r_xin, gb_ap, r_gbx, ssc, lnc, rsc, r_stat, hbs=None, rhbs=None):
            hbs = hbs or HBS
            rhbs = rhbs or r_hb
            hb = hbs[tt % len(hbs)]
            rhb = rhbs[tt % len(hbs)]
            rxl = list(r_xin) if isinstance(r_xin, list) else [r_xin]
            act(hb, xin, AF.Square, rxl, [rhb, r_stat], accum=ssc)
            act(lnc, ssc, AF.Ln, [r_stat, r_st0], [r_stat], bias=EPS, scale=1.0 / D)
            act(rsc, lnc, AF.Exp, [r_stat], [r_stat], scale=-0.5)
            P.add("dve", lambda e: e.scalar_tensor_tensor(out=hb, in0=xin, scalar=rsc, in1=gb_ap,
                                                          op0=ALU.mult, op1=ALU.mult),
                  rxl + [r_stat, r_gbx], [rhb])

        def norm_c(tt, dst_res, b, hbs=None, rhbs=None, evac="dve"):
            hbs = hbs or HBS
            rhbs = rhbs or r_hb
            hb = hbs[tt % len(hbs)]
            rhb = rhbs[tt % len(hbs)]
            pb = bank(b).bitcast(BF16)
            for k in range(8):
                o = pb[:, k * 128:(k + 1) * 128]
                i_ = hb[:, k * 128:(k + 1) * 128]
                P.add("pe", (lambda o=o, i_=i_: (lambda e: e.transpose(o, i_, ident)))(),
                      [rhb, r_cbf], [rbank[b]])
            dst = hT[:, tt // 8, :, (tt % 8) * 128:(tt % 8) * 128 + 128]
            src = pb.rearrange("p (k t) -> p k t", k=8)
            if evac == "dve":
                P.add("dve", lambda e: e.tensor_copy(dst, src), [rbank[b]], [dst_res])
            elif evac == "act":
                act(dst, src, AF.Copy, [rbank[b]], [dst_res])

        def norm_evac(tt, dst_res, b):
            pb = bank(b).bitcast(BF16)
            dst = hT[:, tt // 8, :, (tt % 8) * 128:(tt % 8) * 128 + 128]
            src = pb.rearrange("p (k t) -> p k t", k=8)
            act(dst, src, AF.Copy, [rbank[b]], [dst_res])

        for tt in range(NT):
            dma("sp", xt(tt), x_d[tt * 128:(tt + 1) * 128, :], "x%d" % tt, writes=[r_x[tt]])
        for i in range(NT + 1):
            if i < NT:
                norm_b(i, xt(i), r_x[i], GB[:, 0:1024], r_gb[0],
                       ss1[:, i:i + 1], ln1[:, i:i + 1], rs1[:, i:i + 1], r_ss1[i])
            if i >= 1:
                norm_c(i - 1, r_hT[i - 1], (i - 1) % 8)

        dma("sp", GB[:, 0:1024], gains_d[1], "gb0", writes=[r_gb[0]])
        dma("sp", GB[:, 1024:2048], gains_d[2], "gb1", writes=[r_gb[1]])
        if stop_after == "P1":
            P.frozen = True
        bk = [0]

        def nextbank():
            b = bk[0] % 8
            bk[0] += 1
            return b

        for g in range(4):
            r_uT[g].pending = list(r_x)
        sl = [wload(wcols(w_in_d, c * 512), reads=([] if c == 0 else [r_x[NT - 1]])) for c in range(4)]

        def proj_fm(c, cc, s, b, ev, lazy=False):
            wv = wslot_view(sl[c])
            ops = []
            for k in range(8):
                ops.append((lambda k=k: mm(bank(b), wv[:, k, cc * 128:(cc + 1) * 128], hT_span(k, s), k == 0, k == 7,
                                           [r_ws[sl[c]]] + r_hT[4 * s:4 * s + 4], [rbank[b]])))
            if c == 0:
                dst, rd, sc = uT(cc)[:, s * 512:(s + 1) * 512], r_uT[cc], 1.0
            elif c == 1:
                dst, rd, sc = QT[:, cc, s * 512:(s + 1) * 512], r_QT[cc][s], 0.125
            else:
                dst, rd, sc = KT[:, cc, s * 512:(s + 1) * 512], r_KT[cc][s], 1.0

            def evac():
                if ev == "act":
                    act(dst, bank(b), AF.Copy, [rbank[b]], [rd], scale=sc)
                elif sc != 1.0:
                    P.add("dve", lambda e: e.tensor_scalar(out=dst, in0=bank(b), scalar1=sc, scalar2=None, op0=ALU.mult),
                          [rbank[b]], [rd])
                else:
                    P.add("dve", lambda e: e.tensor_copy(dst, bank(b)), [rbank[b]], [rd])
            ops.append(evac)
            if lazy:
                return ops
            for o in ops:
                o()

        def proj_v(tt, b, ev, lazy=False):
            wv = wslot_view(sl[3])
            ops = []
            for k in range(8):
                ops.append((lambda k=k: mm(bank(b), hT_tile(k, tt), wv[:, k, :], k == 0, k == 7,
                                           [r_ws[sl[3]], r_hT[tt]], [rbank[b]])))
            dst = VV[:, tt, :]

            def evac():
                if ev == "act":
                    act(dst, bank(b), AF.Copy, [rbank[b]], [r_V[tt]])
                else:
                    P.add("dve", lambda e: e.tensor_copy(dst, bank(b)), [rbank[b]], [r_V[tt]])
            ops.append(evac)
            if lazy:
                return ops
            for o in ops:
                o()

        for s in range(4):
            for cc in range(4):
                proj_fm(0, cc, s, nextbank(), "act")
        NPRE = 2
        for s in range(NPRE):
            for c in (1, 2):
                for cc in range(4):
                    proj_fm(c, cc, s, nextbank(), "act")
            for tt in range(4 * s, 4 * s + 4):
                proj_v(tt, nextbank(), "act")
        deferred = {}
        for s in range(NPRE, 4):
            micro = []
            for c in (1, 2):
                for cc in range(4):
                    micro += proj_fm(c, cc, s, 7, "dve", lazy=True)
            for tt in range(4 * s, 4 * s + 4):
                micro += proj_v(tt, 7, "dve", lazy=True)
            deferred[s] = micro

        if stop_after == "P2":
            P.frozen = True
        def dve_tt(out, a, b_, op, reads, writes):
            P.add("dve", lambda e: e.tensor_tensor(out=out, in0=a, in1=b_, op=op), reads, writes)

        P.add("dve", lambda e: e.memset(SA[:, 0:8], 0.0), (), [r_SA])
        WIN = (2, 4, 8, 16)
        r_tmp = Res("tmp16")
        for g in range(4):
            u = uT(g)
            dve_tt(SA[:, 9:2056], u[:, 1:2048], u[:, 0:2047], ALU.add, [r_uT[g]], [r_SA])
            P.add("dve", (lambda u=u: (lambda e: e.tensor_copy(SA[:, 8:9], u[:, 0:1])))(), [r_uT[g]], [r_SA])
            cur, rcur = SA, r_SA
            if g == 1:
                S4 = RX[:, 0:2048]
                dve_tt(S4, SA[:, 8:2056], SA[:, 6:2054], ALU.add, [r_SA], [r_SB, r_uT[0]])
                cur, rcur = None, r_SB
                cv1 = S4
            if g >= 2:
                if g == 2:
                    P.add("dve", lambda e: e.memset(SB_[:, 0:8], 0.0), (), [r_SB, r_uT[0], r_uT[1]])
                dve_tt(SB_[:, 8:2056], SA[:, 8:2056], SA[:, 6:2054], ALU.add, [r_SA], [r_SB, r_uT[0], r_uT[1]])
                cur, rcur = SB_, r_SB
            if g >= 2:
                dve_tt(SA[:, 8:2056], SB_[:, 8:2056], SB_[:, 4:2052], ALU.add, [r_SB], [r_SA])
                cur, rcur = SA, r_SA
            if g >= 3:
                dve_tt(SB_[:, 8:2056], SA[:, 8:2056], SA[:, 0:2048], ALU.add, [r_SA], [r_SB])
                cur, rcur = SB_, r_SB
            w = WIN[g]
            cv = cv1 if g == 1 else cur[:, 8:2056]
            P.add("dve", (lambda cv=cv, u=u, g=g, w=w: (lambda e: e.scalar_tensor_tensor(
                out=PLT[:, g, :], in0=cv, scalar=1.0 / w, in1=u, op0=ALU.mult, op1=ALU.subtract)))(),
                [rcur, r_uT[g]], [r_PLT[g]])
            tmp = ST[:, 272:288]
            P.add("dve", (lambda cv=cv, g=g, tmp=tmp: (lambda e: e.tensor_tensor(
                out=tmp, in0=cv[:, 0:16], in1=ICN[:, g * 16:(g + 1) * 16], op=ALU.mult)))(),
                [rcur, r_icn], [r_tmp])
            P.add("dve", (lambda u=u, g=g, tmp=tmp: (lambda e: e.tensor_tensor(
                out=PLT[:, g, 0:16], in0=tmp, in1=u[:, 0:16], op=ALU.subtract)))(),
                [r_tmp, r_uT[g]], [r_PLT[g]])
        for g in range(4):
            for s in range(4):
                b = nextbank()
                mm(bank(b), WPM[:, g * 128:(g + 1) * 128], PLT[:, g, s * 512:(s + 1) * 512], True, True,
                   [r_wpm, r_PLT[g]], [rbank[b]])
                act(YPT[:, g, s * 512:(s + 1) * 512], bank(b), AF.Copy, [rbank[b], r_psc], [r_YPT[g][s]],
                    scale=PSC[:, g:g + 1])

        r_hs = [Res("hs%d" % i) for i in range(8)]
        for i in range(8):
            r_hs[i].pending = [r_ws[i // 2]]
        hs_n = [0]

        def hslot_view(i):
            return WS[:, i * 2048:(i + 1) * 2048].rearrange("p (k e) -> p k e", k=8)

        def hload(parts):
            i = hs_n[0] % 8
            hs_n[0] += 1
            v = hslot_view(i)
            oi = []
            for (k0, src) in parts:
                oi.append((v[:, k0:k0 + src.shape[1], :], src))
            dma2("pool", oi, "hs%d" % i, writes=[r_hs[i]])
            return i

        def p5_load(gq, which):
            c0 = gq * 256
            if which == "a":
                v = w_gate_d.rearrange("(k p) e -> p k e", p=128)
                return hload([(0, v[:, 0:4, c0:c0 + 256]), (4, v[:, 4:8, c0:c0 + 256])])
            if which == "b":
                v = w_gate_d.rearrange("(k p) e -> p k e", p=128)
                return hload([(0, v[:, 0:4, 1024 + c0:1024 + c0 + 256]), (4, v[:, 4:8, 1024 + c0:1024 + c0 + 256])])
            vp = w_brp_d.rearrange("(k p) e -> p k e", p=128)
            vs = w_brs_d.rearrange("(k p) e -> p k e", p=128)
            return hload([(0, vp[:, :, c0:c0 + 256]), (4, vs[:, :, c0:c0 + 256])])
        p5hs = {}

        def p5_prefetch():
            for gq in (0, 1):
                for wh in "abc":
                    p5hs[(gq, wh)] = p5_load(gq, wh)
            p5hs[(2, "a")] = p5_load(2, "a")
            p5hs[(2, "b")] = p5_load(2, "b")
        p5w0 = [None]

        if stop_after == "P3":
            P.frozen = True
        def f32v(off):
            return RX[:, off:off + 1024].rearrange("p (h n) -> p h n", h=2)

        def bf16v(off32):
            return RXb[:, 2 * off32:2 * off32 + 1024].rearrange("p (h n) -> p h n", h=2)
        E_ = [f32v(4096), f32v(5120)]
        ARG = [f32v(6144), f32v(7168)]
        R1f_a = R1[:].bitcast(F32)
        CCS = [f32v(8192), R1f_a[:, 12288:13312].rearrange("p (h n) -> p h n", h=2)]
        SP_ = [bf16v(9216), bf16v(9728)]
        ATT = [bf16v(14400), bf16v(14912)]
        r_E = [Res("E0"), Res("E1")]
        r_ARG = [Res("ARG0"), Res("ARG1")]
        r_CCS = [Res("CC0"), Res("CC1")]
        r_CC = r_CCS[0]
        r_CCS[1].pending = list(r_PLT)
        r_SP = [Res("SP0"), Res("SP1")]
        r_ATT = [Res("ATT0"), Res("ATT1")]
        old = r_uT + [r_SA, r_SB]
        for r in r_E + r_ARG + [r_CC] + r_SP:
            r.pending = list(old)
        for c in range(4):
            for s in range(4):
                r_OT[c][s].pending = list(old)

        ZP = [PP[0], PP[1]]
        r_Z = [[rbank[0], rbank[1]], [rbank[2], rbank[3]]]
        TP = PP[2]
        r_T = [rbank[4], rbank[5]]
        OACC = [bank(6), bank(6)]
        r_OACC = [rbank[6], rbank[6]]

        chains = []
        chain_id = 0
        for j in range(4):
            for hp in range(4):
                kbs = [4 * j + 3, 4 * j + 2, 4 * j + 1, 4 * j] + list(range(4 * j - 1, -1, -1))
                ch = []
                for i, kb in enumerate(kbs):
                    c0 = 128 * (kb - 4 * j) if kb >= 4 * j else 0
                    ch.append(dict(j=j, hp=hp, kb=kb, c0=c0, first=(i == 0), last=(i == len(kbs) - 1),
                                   diag=(kb >= 4 * j), chain=chain_id))
                chains.append(ch)
                chain_id += 1
        tiles = []
        for ch in chains:
            tiles.extend(ch)
        sched = {}
        for s_ in range(NPRE, 4):
            lo = 0 if s_ == NPRE else min(n for n, t in enumerate(tiles) if t["j"] == s_ - 1)
            hi = min(n for n, t in enumerate(tiles) if t["j"] == s_)
            micro = deferred[s_]
            for g, op_ in enumerate(micro):
                n = lo + (g * (hi - lo)) // len(micro)
                sched.setdefault(n, []).append(op_)
        last_sched = max(sched)
        NTL = len(tiles)

        def pv(t3, c0):
            return t3[:, :, c0:512]

        def z3(zb):
            return ZP[zb][:].rearrange("p (h n) -> p h n", h=2)

        def emit_qk(n):
            t = tiles[n]
            zb = n % 2
            j, hp, kb, c0 = t["j"], t["hp"], t["kb"], t["c0"]
            for hd in range(2):
                rows = slice(64 * hd, 64 * hd + 64)
                out = ZP[zb][:, hd * 512 + c0:hd * 512 + 512]
                lhsT = KT[rows, hp, kb * 128:(kb + 1) * 128]
                rhs = QT[rows, hp, j * 512 + c0:(j + 1) * 512]
                mm(out, lhsT, rhs, True, not t["diag"], [r_KT[hp][kb // 4], r_QT[hp][j]], [r_Z[zb][hd]])
            if t["diag"]:
                for hd in range(2):
                    out = ZP[zb][:, hd * 512 + c0:hd * 512 + c0 + 128]
                    mm(out, ident, maskneg, False, True, [r_cbf], [r_Z[zb][hd]])

        def emit_exp1(n):
            t = tiles[n]
            zb = n % 2
            act(pv(E_[zb], t["c0"]), pv(z3(zb), t["c0"]), AF.Exp, r_Z[zb], [r_E[zb]])

        def emit_ln(n):
            t = tiles[n]
            zb = n % 2
            act(pv(SP_[zb], t["c0"]), pv(E_[zb], t["c0"]), AF.Ln, [r_E[zb], r_st0], [r_SP[zb]], bias=1.0)

        def emit_tri(n):
            t = tiles[n]
            zb = n % 2
            c0 = t["c0"]
            for hd in range(2):
                out = ZP[zb][:, hd * 512 + c0:hd * 512 + 512]
                mm(out, negtri, SP_[zb][:, hd, c0:512], False, True, [r_cbf, r_SP[zb]], [r_Z[zb][hd]], skip=True)
            for hd in range(2):
                out = TP[:, hd * 512 + c0:hd * 512 + 512]
                mm(out, ones, SP_[zb][:, hd, c0:512], True, True, [r_cbf, r_SP[zb]], [r_T[hd]])

        def emit_dve(n):
            t = tiles[n]
            zb = n % 2
            c0 = t["c0"]
            CC = CCS[t["chain"] % 2]
            rcc = r_CCS[t["chain"] % 2]
            if t["first"]:
                P.add("pool", (lambda CC=CC: (lambda e: e.memset(CC, 0.0)))(), (), [rcc])
            a = pv(ARG[zb], c0)
            z = pv(z3(zb), c0)
            c = pv(CC, c0)
            tp = pv(TP[:].rearrange("p (h n) -> p h n", h=2), c0)
            dve_tt(a, z, c, ALU.subtract, r_Z[zb] + [rcc], [r_ARG[zb]])
            if not t["last"]:
                dve_tt(c, tp, c, ALU.add, r_T + [rcc], [rcc])

        def emit_exp2(n):
            t = tiles[n]
            zb = n % 2
            act(pv(ATT[zb], t["c0"]), pv(ARG[zb], t["c0"]), AF.Exp, [r_ARG[zb]], [r_ATT[zb]])

        def emit_av(n):
            t = tiles[n]
            zb = n % 2
            c0, hp, kb, j = t["c0"], t["hp"], t["kb"], t["j"]
            ob = t["chain"] % 2
            for hd in range(2):
                out = OACC[ob][64 * hd:64 * hd + 64, c0:512]
                lhsT = VV[:, kb, (2 * hp + hd) * 64:(2 * hp + hd) * 64 + 64]
                mm(out, lhsT, ATT[zb][:, hd, c0:512], t["first"], t["last"], [r_V[kb], r_ATT[zb]], [r_OACC[ob]], skip=True)
            if t["last"]:
                dst = OT[:, hp, j * 512:(j + 1) * 512]
                src = OACC[ob]
                P.add("dve", lambda e: e.tensor_copy(dst, src), [r_OACC[ob]], [r_OT[hp][j]])

        emit_qk(0)
        for n in range(NTL + 2):
            if n + 1 < NTL:
                emit_qk(n + 1)
            if n < NTL:
                emit_exp1(n)
            if 2 <= n:
                emit_exp2(n - 2)
            if n < NTL:
                emit_ln(n)
                emit_tri(n)
                emit_dve(n)
            if 2 <= n:
                emit_av(n - 2)
            for op_ in sched.get(n, []):
                op_()
            if n == last_sched:
                p5_prefetch()

        if stop_after == "P4":
            P.frozen = True
        NXS = 6
        R1f = R1[:].bitcast(F32)
        XS = [R1f[:, 8192 + j * 1024:8192 + (j + 1) * 1024] for j in range(NXS)]
        r_xs = [Res("xs%d" % j) for j in range(NXS)]
        for j in range(NXS):
            r_xs[j].pending = r_V + r_PLT + [r_CCS[1]]
            dma("sp", XS[j], x_d[j * 128:(j + 1) * 128, :], "xs%d" % j, writes=[r_xs[j]])
        MT = R1[:, 0:16384].rearrange("p (c t) -> p c t", c=8)
        r_MT = [[Res("MT%d_%d" % (c, s)) for s in range(4)] for c in range(8)]
        oldq = [r for row in r_QT for r in row] + [r for row in r_KT for r in row]
        for c in range(8):
            for s in range(4):
                r_MT[c][s].pending = list(oldq)
        TM = [[RX[:, 4096 + (i * 4 + q) * 512:4096 + (i * 4 + q + 1) * 512] for q in range(4)] for i in range(2)]
        r_TM = [[Res("TM%d_%d" % (i, q)) for q in range(4)] for i in range(2)]
        olda = r_E + r_ARG + [r_CC] + r_SP
        for i in range(2):
            for q in range(4):
                r_TM[i][q].pending = list(olda)
        it = 0
        so = [None, None]
        for gq in range(4):
            ha, hb_, hc = p5hs[(gq, "a")], p5hs[(gq, "b")], p5hs[(gq, "c")]
            wa, wb, wc = hslot_view(ha), hslot_view(hb_), hslot_view(hc)
            for dcl in range(2):
                dc = 2 * gq + dcl
                cs = slice(dcl * 128, dcl * 128 + 128)
                for s in range(4):
                    pb = 4 * (it % 2)
                    ti = it % 2
                    it += 1
                    bgp, bgs, byp, bys = pb, pb + 1, pb + 2, pb + 3
                    for k in range(8):
                        mm(bank(bgp), wa[:, k, cs], hT_span(k, s), k == 0, k == 7,
                           [r_hs[ha]] + r_hT[4 * s:4 * s + 4], [rbank[bgp]])
                    for k in range(8):
                        mm(bank(bgs), wb[:, k, cs], hT_span(k, s), k == 0, k == 7,
                           [r_hs[hb_]] + r_hT[4 * s:4 * s + 4], [rbank[bgs]])
                    for k in range(4):
                        mm(bank(byp), wc[:, k, cs], YPT[:, k, s * 512:(s + 1) * 512], k == 0, k == 3,
                           [r_hs[hc], r_YPT[k][s]], [rbank[byp]])
                    for k in range(4):
                        mm(bank(bys), wc[:, 4 + k, cs], OT[:, k, s * 512:(s + 1) * 512], k == 0, k == 3,
                           [r_hs[hc], r_OT[k][s]], [rbank[bys]])
                    act(TM[ti][0], bank(bgp), AF.Sigmoid, [rbank[bgp], r_bgt], [r_TM[ti][0]], bias=BGT[:, dc:dc + 1])
                    act(TM[ti][1], bank(bgs), AF.Sigmoid, [rbank[bgs], r_bgt], [r_TM[ti][1]], bias=BGT[:, 8 + dc:9 + dc])
                    dve_tt(TM[ti][2], bank(byp), TM[ti][0], ALU.mult, [rbank[byp], r_TM[ti][0]], [r_TM[ti][2]])
                    dve_tt(TM[ti][3], bank(bys), TM[ti][1], ALU.mult, [rbank[bys], r_TM[ti][1]], [r_TM[ti][3]])
                    dve_tt(MT[:, dc, s * 512:(s + 1) * 512], TM[ti][2], TM[ti][3], ALU.add,
                           [r_TM[ti][2], r_TM[ti][3]], [r_MT[dc][s]])
            if gq == 0:
                p5hs[(2, "c")] = p5_load(2, "c")
                p5hs[(3, "a")] = p5_load(3, "a")
                p5hs[(3, "b")] = p5_load(3, "b")
            elif gq == 1:
                p5hs[(3, "c")] = p5_load(3, "c")
                assert hs_n[0] == 12
                for s_ in range(4):
                    r_ws[s_].pending = [r_hs[2 * s_], r_hs[2 * s_ + 1]]
                ws_n[0] = 6
                so[0] = wload(wcols(w_out_d, 0))
            elif gq == 2:
                so[1] = wload(wcols(w_out_d, 512))

        if stop_after == "P5":
            P.frozen = True
        oldrx = [r for row in r_YPT for r in row] + [r for row in r_OT for r in row] + \
                [r for row in r_TM for r in row] + r_ATT
        r_x1h = [[Res("x1_%d_0" % t), Res("x1_%d_1" % t)] for t in range(NT)]
        for tt in range(NT):
            r_x1h[tt][0].pending = list(oldrx)
            r_x1h[tt][1].pending = list(oldrx)
            if tt >= NXS:
                dma("sp", xt(tt), x_d[tt * 128:(tt + 1) * 128, :], "x%d" % tt, reads=[r_ws[so[1]]], writes=r_x1h[tt])
        r_h2T = [Res("h2T%d" % t) for t in range(NT)]
        for tt in range(NT):
            r_h2T[tt].pending = list(r_hT)
        ssA = stcol(32)
        ssS = stcol(16)
        lnA = stcol(16)
        rsA = stcol(16)
        ssB = stcol(16)
        lnB = stcol(16)
        rsB = stcol(16)
        r_stA = [Res("stA%d" % t) for t in range(NT)]
        r_stB = [Res("stB%d" % t) for t in range(NT)]
        HBS4 = HBS + [R1[:, 28672:29696], R1[:, 29696:30720]]
        r_hb4 = r_hb + [Res("hb2"), Res("hb3")]
        JK6 = R1[:, 30720:31232]
        r_jk6 = Res("jk6")
        for r in r_hb4[2:] + [r_jk6]:
            r.pending = r_V + r_PLT + [r_CCS[1]]

        JK6w = R1[:, 30720:31744]

        def p6_mm(T):
            b0 = 2 * (T % 3)
            for dh in range(2):
                wv = wslot_view(so[dh])
                for k in range(8):
                    mm(bank(b0 + dh), MT[:, k, T * 128:(T + 1) * 128], wv[:, k, :], k == 0, k == 7,
                       [r_ws[so[dh]], r_MT[k][T // 4]], [rbank[b0 + dh]])

        def p6_sqA(T):
            b0 = 2 * (T % 3)
            act(JK6w, PP[T % 3][:], AF.Square, [rbank[b0], rbank[b0 + 1]], [r_jk6, r_stA[T]], accum=ssS[:, T:T + 1])

        def p6_lnA(T):
            act(lnA[:, T:T + 1], ssS[:, T:T + 1], AF.Ln, [r_stA[T], r_st0], [r_stA[T]], bias=EPS, scale=1.0 / D)

        def p6_expA(T):
            act(rsA[:, T:T + 1], lnA[:, T:T + 1], AF.Exp, [r_stA[T]], [r_stA[T]], scale=-0.5)

        def p6_res(T):
            b0 = 2 * (T % 3)
            for dh in range(2):
                P.add("dve", (lambda b=b0 + dh, T=T, dh=dh: (lambda e: e.scalar_tensor_tensor(
                    out=bank(b), in0=bank(b), scalar=rsA[:, T:T + 1], in1=GB[:, dh * 512:(dh + 1) * 512],
                    op0=ALU.mult, op1=ALU.mult)))(),
                    [rbank[b0 + dh], r_stA[T], r_gb[0]], [rbank[b0 + dh]])
            for dh in range(2):
                xo = xt(T)[:, dh * 512:(dh + 1) * 512]
                if T < NXS:
                    xi = XS[T][:, dh * 512:(dh + 1) * 512]
                    dve_tt(xo, bank(b0 + dh), xi, ALU.add, [rbank[b0 + dh], r_xs[T]], [r_x1h[T][dh]])
                else:
                    dve_tt(xo, bank(b0 + dh), xo, ALU.add, [rbank[b0 + dh], r_x1h[T][dh]], [r_x1h[T][dh]])

        def p6_sqB(T):
            act(HBS4[T % 4], xt(T), AF.Square, r_x1h[T], [r_hb4[T % 4], r_stB[T]], accum=ssB[:, T:T + 1])

        def p6_lnB(T):
            act(lnB[:, T:T + 1], ssB[:, T:T + 1], AF.Ln, [r_stB[T], r_st0], [r_stB[T]], bias=EPS, scale=1.0 / D)

        def p6_expB(T):
            act(rsB[:, T:T + 1], lnB[:, T:T + 1], AF.Exp, [r_stB[T]], [r_stB[T]], scale=-0.5)

        def p6_h(T):
            hb = HBS4[T % 4]
            P.add("dve", (lambda hb=hb, T=T: (lambda e: e.scalar_tensor_tensor(
                out=hb, in0=xt(T), scalar=rsB[:, T:T + 1], in1=GB[:, 1024:2048], op0=ALU.mult, op1=ALU.mult)))(),
                r_x1h[T] + [r_stB[T], r_gb[1]], [r_hb4[T % 4]])

        def ok(T):
            return 0 <= T < NT

        for i in range(NT + 6):
            if ok(i - 5):
                norm_c(i - 5, r_h2T[i - 5], 6 + (i - 5) % 2, HBS4, r_hb4, evac="none")
            if ok(i):
                p6_mm(i)
            if ok(i - 1):
                p6_sqA(i - 1)
            if ok(i - 3):
                p6_sqB(i - 3)
            if ok(i - 1):
                p6_lnA(i - 1)
            if ok(i - 3):
                p6_lnB(i - 3)
            if ok(i - 1):
                p6_expA(i - 1)
            if ok(i - 3):
                p6_expB(i - 3)
            if ok(i - 6):
                norm_evac(i - 6, r_h2T[i - 6], 6 + (i - 6) % 2)
            if ok(i - 2):
                p6_res(i - 2)
            if ok(i - 4):
                p6_h(i - 4)

        if stop_after == "P6":
            P.frozen = True
        AT = R1[:].rearrange("p (c t) -> p c t", c=32)
        r_AT = [[Res("AT%d_%d" % (c, s)) for s in range(2)] for c in range(32)]
        oldm = [r for row in r_MT for r in row] + r_V + r_PLT + r_xs + r_hb4[2:] + [r_jk6, r_CCS[1]]
        dma("sp", GB[:, 0:1024], gains_d[3], "gb0", writes=[r_gb[0]])
        FFS = [R2[:, h * 4096:(h + 1) * 4096].rearrange("p (t e) -> p t e", t=8) for h in range(2)]
        ssC = stcol(32)
        ssT = stcol(16)
        lnC = stcol(16)
        rsC = stcol(16)
        r_stC = [Res("stC%d" % t) for t in range(NT)]
        r_rt = [Res("rt0"), Res("rt1")]
        for r in r_rt:
            r.pending = [r_gb[1]]
        for half in range(2):
            for c in range(32):
                for s in range(2):
                    r_AT[c][s].pending = list(oldm)
            oldm = []
            for f4 in range(8):
                su = wload(wcols(w_up_d, f4 * 512))
                wv = wslot_view(su)
                for fl in range(4):
                    fc = 4 * f4 + fl
                    for s in range(2):
                        b = nextbank()
                        for k in range(8):
                            mm(bank(b), wv[:, k, fl * 128:(fl + 1) * 128],
                               hT[:, half, k, s * 512:(s + 1) * 512], k == 0, k == 7,
                               [r_ws[su]] + r_h2T[8 * half + 4 * s:8 * half + 4 * s + 4], [rbank[b]])
                        dst = AT[:, fc, s * 512:(s + 1) * 512]
                        ti = b % 2
                        tmpr = GB[:, 1024 + ti * 512:1024 + (ti + 1) * 512]
                        act(tmpr, bank(b), AF.Relu, [rbank[b]], [r_rt[ti]])
                        dve_tt(dst, bank(b), tmpr, ALU.mult, [rbank[b], r_rt[ti]], [r_AT[fc][s]])
            if stop_after == "P7a":
                P.frozen = True
            r_ffs = [Res("ffs%d_%d" % (half, t)) for t in range(8)]
            for t in range(8):
                r_ffs[t].pending = list(r_h2T[8 * half:8 * half + 8])
            for dh in range(2):
                for f8 in range(4):
                    vd = w_down_d.rearrange("(c p) d -> p c d", p=128)
                    sd = wload([(0, vd[:, 8 * f8:8 * f8 + 4, dh * 512:(dh + 1) * 512]),
                                (4, vd[:, 8 * f8 + 4:8 * f8 + 8, dh * 512:(dh + 1) * 512])])
                    wv = wslot_view(sd)
                    if f8 == 3:
                        for t in range(8):
                            for fl in range(8):
                                fc = 8 * f8 + fl
                                mm(bank(t), AT[:, fc, t * 128:(t + 1) * 128], wv[:, fl, :], fc == 0, fc == 31,
                                   [r_ws[sd], r_AT[fc][t // 4]], [rbank[t]])
                    else:
                        for fl in range(8):
                            fc = 8 * f8 + fl
                            for t in range(8):
                                mm(bank(t), AT[:, fc, t * 128:(t + 1) * 128], wv[:, fl, :], fc == 0, fc == 31,
                                   [r_ws[sd], r_AT[fc][t // 4]], [rbank[t]])
                if stop_after == "P7b":
                    P.frozen = True
                def ev_stats(t, half=half, dh=dh):
                    tt = 8 * half + t
                    act(HBS[t % 2][:, 0:512], bank(t), AF.Square, [rbank[t]], [r_hb[t % 2], r_stC[tt]],
                        accum=ssC[:, 2 * tt + dh:2 * tt + dh + 1])
                    if dh == 0:
                        P.add("dve", (lambda t=t, half=half: (lambda e: e.tensor_copy(FFS[half][:, t, :], bank(t))))(),
                              [rbank[t]], [r_ffs[t]])
                    else:
                        dve_tt(ssT[:, tt:tt + 1], ssC[:, 2 * tt:2 * tt + 1], ssC[:, 2 * tt + 1:2 * tt + 2], ALU.add,
                               [r_stC[tt]], [r_stC[tt]])
                        act(lnC[:, tt:tt + 1], ssT[:, tt:tt + 1], AF.Ln, [r_stC[tt], r_st0], [r_stC[tt]],
                            bias=EPS, scale=1.0 / D)
                        act(rsC[:, tt:tt + 1], lnC[:, tt:tt + 1], AF.Exp, [r_stC[tt]], [r_stC[tt]], scale=-0.5)

                def ev_big(t, half=half):
                    tt = 8 * half + t
                    for d2 in range(2):
                        src = FFS[half][:, t, :] if d2 == 0 else bank(t)
                        rsrc = r_ffs[t] if d2 == 0 else rbank[t]
                        P.add("dve", (lambda src=src, tt=tt, d2=d2: (lambda e: e.scalar_tensor_tensor(
                            out=src, in0=src, scalar=rsC[:, tt:tt + 1], in1=GB[:, d2 * 512:(d2 + 1) * 512],
                            op0=ALU.mult, op1=ALU.mult)))(),
                            [rsrc, r_stC[tt], r_gb[0]], [rsrc])
                    for d2 in range(2):
                        src = FFS[half][:, t, :] if d2 == 0 else bank(t)
                        rsrc = r_ffs[t] if d2 == 0 else rbank[t]
                        xs = xt(tt)[:, d2 * 512:(d2 + 1) * 512]
                        P.add("dve", (lambda xs=xs, src=src: (lambda e: e.tensor_tensor(out=xs, in0=src, in1=xs, op=ALU.add)))(),
                              [rsrc, r_x1h[tt][d2]], [r_x1h[tt][d2]])
                    if stop_after != "P7d":
                        dma("sp", out_d[tt * 128:(tt + 1) * 128, :], xt(tt), "st%d" % tt, reads=r_x1h[tt])

                for t in range(9):
                    if t < 8:
                        ev_stats(t)
                    if dh == 1 and t >= 1:
                        ev_big(t - 1)
                if stop_after == "P7c" and dh == 0:
                    P.frozen = True

        P.frozen = False
        if debug:
            dbg_src = {
                "hT": (R2b, []), "R1": (R1[:], []), "RX": (RX[:], []),
            }
            for name, _, _ in debug:
                src, _r = dbg_src[name]
                P.add("sp", (lambda src=src, name=name: (lambda e: [e.dma_start(out=dbg_d[name], in_=src)]))(),
                      [], [], dma_key="dbg_" + name, ninst=1, barrier=True)

        P.finalize()
        P.plan_waits()
        for key in P.dma_cnt:
            sem_dma[key] = es.enter_context(nc.semaphore("sd_" + key))
        final_waits = [(k, v) for k, v in P.dma_cnt.items() if k.startswith("st") or k.startswith("dbg_")]

        def emit_engine(name, e):
            waited = {}
            for op in P.ops:
                if op.eng != name:
                    continue
                todo = [(sem_dma[key[1]] if key[0] == "d" else sem_eng[key[1]], val) for key, val in op.waits]
                embed = None
                if todo and op.dma_key is None and name in ("act", "dve", "pool") and not op.multi:
                    embed = todo.pop()
                for sem, val in todo:
                    e.wait_ge(sem, val)
                res = op.fn(e)
                if embed is not None:
                    res._wait_ge(embed[0], embed[1])
                if op.dma_key is not None:
                    for inst in res:
                        inst.then_inc(sem_dma[op.dma_key], 16)
                elif op.signal:
                    res.then_inc(sem_eng[name], 1)
            if name == "sp":
                if debug:
                    for en in ("pe", "act", "dve", "pool"):
                        pass
                for k, v in final_waits:
                    e.wait_ge(sem_dma[k], v)

        with nc.Block() as block:
            @block.tensor
            def _(e):
                emit_engine("pe", e)

            @block.scalar
            def _(e):
                emit_engine("act", e)

            @block.vector
            def _(e):
                emit_engine("dve", e)

            @block.gpsimd
            def _(e):
                emit_engine("pool", e)

            @block.sync
            def _(e):
                emit_engine("sp", e)
    return nc


_NC_CACHE = {}


def _consts():
    bf = ml_dtypes.bfloat16
    p = np.arange(128)[:, None]
    c = np.arange(128)[None, :]
    ident = (p == c).astype(np.float32)
    negtri = -(p >= c).astype(np.float32)
    ones = np.ones((128, 128), np.float32)
    maskneg = np.where(p < c, 0.0, NEG).astype(np.float32)
    cbf = np.concatenate([ident, negtri, ones, maskneg], axis=1).astype(bf)
    invcnt = np.zeros((128, 64), np.float32)
    for g, w in enumerate((2, 4, 8, 16)):
        t = np.arange(16)
        invcnt[:, g * 16:(g + 1) * 16] = (1.0 / np.minimum(t + 1, w))[None, :]
    return cbf, invcnt


def kernel(x, g_pre_mix, w_in, w_pool_mix, pool_scale, w_br_pool, w_br_sb, w_gate, b_gate,
           w_out, g_post_mix, g_pre_mlp, w_up, w_down, g_post_mlp, _debug=None, _stop=None):
    f = lambda a: np.ascontiguousarray(np.asarray(a, dtype=np.float32))
    x = f(x)
    B = x.shape[0]
    key = "dbg" if _debug else "main"
    if key not in _NC_CACHE:
        _NC_CACHE[key] = build_nc(_debug, _stop)
    nc = _NC_CACHE[key]
    cbf, invcnt = _consts()
    gains = np.stack([np.broadcast_to(f(g)[None, :], (128, D)) for g in
                      (g_pre_mix, g_post_mix, g_pre_mlp, g_post_mlp)]).copy()
    pscale = np.ascontiguousarray(f(pool_scale).reshape(4, 128).T)
    bgate = np.ascontiguousarray(f(b_gate).reshape(16, 128).T)
    shared = {
        "w_in": f(w_in), "w_gate": f(w_gate), "w_br_pool": f(w_br_pool), "w_br_sb": f(w_br_sb),
        "w_out": f(w_out), "w_up": f(w_up), "w_down": f(w_down), "w_pool_mix": f(w_pool_mix),
        "gains": gains, "pscale": pscale, "bgate": bgate, "invcnt": invcnt, "cbf": cbf,
    }
    in_maps = [dict(shared, x=x[b]) for b in range(B)]
    res = run_bass_kernel_spmd(nc, in_maps, core_ids=list(range(B)))
    out = np.stack([np.asarray(r["out"], dtype=np.float32) for r in res.results], axis=0)
    if _debug:
        return out, [{k: np.asarray(v) for k, v in r.items()} for r in res.results]
    return out
```

```python
import os
import numpy as np
import ml_dtypes
from contextlib import ExitStack
import concourse.bass as bass
import concourse.mybir as mybir
from concourse.bass_utils import run_bass_kernel_spmd

F32 = mybir.dt.float32
BF16 = mybir.dt.bfloat16
AF = mybir.ActivationFunctionType
ALU = mybir.AluOpType

S = 2048
D = 1024
NT = S // 128
NEG = -30000.0
EPS = 1e-6


class Res:
    __slots__ = ("name", "w", "rs", "pending", "excl")

    def __init__(self, name, excl=False):
        self.name = name
        self.excl = excl
        self.w = None
        self.rs = {}
        self.pending = []


class Op:
    __slots__ = ("idx", "eng", "fn", "dma_key", "ninst", "deps", "signal", "sigval", "multi", "waits")

    def __init__(self, idx, eng, fn, dma_key, ninst):
        self.idx = idx
        self.eng = eng
        self.fn = fn
        self.dma_key = dma_key
        self.ninst = ninst
        self.deps = set()
        self.signal = False
        self.sigval = 0
        self.multi = False
        self.waits = []


class Prog:
    ENGS = ("pe", "act", "dve", "pool", "sp")

    def __init__(self):
        self.ops = []
        self.dma_cnt = {}
        self.frozen = False

    def add(self, eng, fn, reads=(), writes=(), dma_key=None, ninst=1, barrier=False):
        if self.frozen:
            return None
        op = Op(len(self.ops), eng, fn, dma_key, ninst)
        if barrier:
            last = {}
            for o in self.ops:
                last[(o.eng, o.dma_key)] = o
            for o in last.values():
                op.deps.add(o)
        is_dma = dma_key is not None
        writes = list(writes)
        extra = []
        for w in writes:
            if w.pending:
                extra.extend(w.pending)
                w.pending = []
        deps = []
        for r in reads:
            if r.w is not None:
                deps.append((r.w, "raw"))
            if r.excl:
                for rd in r.rs.values():
                    if rd.eng != eng:
                        deps.append((rd, "raw"))
        for w in writes + extra:
            if w.w is not None:
                deps.append((w.w, "waw"))
            for rd in w.rs.values():
                deps.append((rd, "war"))
        for d, kind in deps:
            if d is op:
                continue
            d_dma = d.dma_key is not None
            if (not d_dma) and (not is_dma) and d.eng == eng and eng == "pe":
                continue
            op.deps.add(d)
        for r in reads:
            r.rs[(eng, dma_key)] = op
        for w in writes:
            w.w = op
            w.rs = {}
        if is_dma:
            self.dma_cnt[dma_key] = self.dma_cnt.get(dma_key, 0) + 16 * ninst
            op.sigval = self.dma_cnt[dma_key]
        self.ops.append(op)
        return op

    def finalize(self):
        for op in self.ops:
            for d in op.deps:
                if d.dma_key is None:
                    d.signal = True
        cnt = {e: 0 for e in self.ENGS}
        for op in self.ops:
            if op.dma_key is None and op.signal:
                cnt[op.eng] += 1
                op.sigval = cnt[op.eng]


    def plan_waits(self):
        waited = {e: {} for e in self.ENGS}
        know = {}
        for op in self.ops:
            w = waited[op.eng]
            need = {}
            for d in op.deps:
                key = ("d", d.dma_key) if d.dma_key is not None else ("e", d.eng)
                if d.sigval > need.get(key, 0):
                    need[key] = d.sigval
            op.waits = []
            cand = [(key, val) for key, val in sorted(need.items(), key=lambda kv: str(kv[0])) if w.get(key, 0) < val]
            keep = []
            for c in cand:
                implied = False
                for o in cand:
                    if o is not c and know.get(o, {}).get(c[0], 0) >= c[1] and not (
                            know.get(c, {}).get(o[0], 0) >= o[1] and cand.index(c) < cand.index(o)):
                        implied = True
                        break
                if not implied:
                    keep.append(c)
            for key, val in keep:
                op.waits.append((key, val))
            for key, val in cand:
                w[key] = max(w.get(key, 0), val)
                for k2, v2 in know.get((key, val), {}).items():
                    if v2 > w.get(k2, 0):
                        w[k2] = v2
            if op.dma_key is not None:
                know[(("d", op.dma_key), op.sigval)] = dict(w)
            elif op.signal:
                kk = dict(w)
                kk[("e", op.eng)] = max(kk.get(("e", op.eng), 0), op.sigval)
                know[(("e", op.eng), op.sigval)] = kk


def build_nc(debug=None, stop_after=None):
    nc = bass.Bass("TRN2", target_bir_lowering=False)
    P = Prog()

    def dram_in(name, shape, dt=F32):
        return nc.dram_tensor(name, list(shape), dt, kind="ExternalInput").ap()

    x_d = dram_in("x", [S, D])
    w_in_d = dram_in("w_in", [D, 2048])
    w_gate_d = dram_in("w_gate", [D, 2048])
    w_brp_d = dram_in("w_br_pool", [512, D])
    w_brs_d = dram_in("w_br_sb", [512, D])
    w_out_d = dram_in("w_out", [D, D])
    w_up_d = dram_in("w_up", [D, 4096])
    w_down_d = dram_in("w_down", [4096, D])
    wpm_d = dram_in("w_pool_mix", [4, 128, 128])
    gains_d = dram_in("gains", [4, 128, D])
    pscale_d = dram_in("pscale", [128, 4])
    bgate_d = dram_in("bgate", [128, 16])
    invcnt_d = dram_in("invcnt", [128, 64])
    cbf_d = dram_in("cbf", [128, 512], BF16)
    out_d = nc.dram_tensor("out", [S, D], F32, kind="ExternalOutput").ap()
    dbg_d = {}
    if debug:
        for name, shape, dt in debug:
            dbg_d[name] = nc.dram_tensor("dbg_" + name, list(shape), dt, kind="ExternalOutput").ap()

    es = ExitStack()
    with es:
        def sb(name, shape, dt):
            return es.enter_context(nc.sbuf_tensor(name, list(shape), dt))

        RX = sb("RX", [128, 16384], F32)
        R1 = sb("R1", [128, 32768], BF16)
        R2 = sb("R2", [128, 8192], F32)
        WS = sb("WS", [128, 16384], BF16)
        GB = sb("GB", [128, 2048], F32)
        HB = sb("HB", [128, 1024], BF16)
        JK = sb("JK", [128, 1024], BF16)
        CBF = sb("CBF", [128, 512], BF16)
        WPM = sb("WPM", [128, 512], BF16)
        PSC = sb("PSC", [128, 4], F32)
        BGT = sb("BGT", [128, 16], F32)
        ICN = sb("ICN", [128, 64], F32)
        ST = sb("ST", [128, 288], F32)
        PP = [es.enter_context(nc.psum_tensor("pp%d" % i, [128, 1024], F32)) for i in range(4)]

        sem_eng = {e: es.enter_context(nc.semaphore("se_" + e)) for e in Prog.ENGS}
        sem_dma = {}

        def bank(i):
            return PP[i // 2][:, (i % 2) * 512:(i % 2) * 512 + 512]

        rbank = [Res("bank%d" % i, excl=True) for i in range(8)]
        ident = CBF[:, 0:128]
        negtri = CBF[:, 128:256]
        ones = CBF[:, 256:384]
        maskneg = CBF[:, 384:512]

        EPSC = ST[:, 0:1]
        ONEC = ST[:, 1:2]
        st_next = [2]

        def stcol(n=1):
            c = st_next[0]
            st_next[0] += n
            assert st_next[0] <= 272
            return ST[:, c:c + n]

        def dma(eng, out, in_, key, reads=(), writes=()):
            P.add(eng, lambda e: [e.dma_start(out=out, in_=in_)], reads, writes, dma_key=key, ninst=1)

        def dma2(eng, outs_ins, key, reads=(), writes=()):
            def fn(e):
                return [e.dma_start(out=o, in_=i) for (o, i) in outs_ins]
            P.add(eng, fn, reads, writes, dma_key=key, ninst=len(outs_ins))

        def mm(out, lhsT, rhs, start, stop, reads, writes, skip=False):
            if skip:
                P.add("pe", lambda e: e.matmul(out, lhsT, rhs, start=start, stop=stop, skip_group_check=True), reads, writes)
            else:
                P.add("pe", lambda e: e.matmul(out, lhsT, rhs, start=start, stop=stop), reads, writes)

        def act(out, in_, func, reads, writes, bias=None, scale=1.0, accum=None):
            kw = {}
            if bias is not None:
                kw["bias"] = bias
            if accum is not None:
                kw["accum_out"] = accum
            o_ = P.add("act", lambda e: e.activation(out, in_, func, scale=scale, **kw), reads, writes)
            if o_ is not None and accum is not None:
                o_.multi = True

        def junk():
            return Res("junk")

        r_cbf = Res("cbf")
        r_wpm = Res("wpm")
        r_psc = Res("psc")
        r_bgt = Res("bgt")
        r_icn = Res("icn")
        r_st0 = Res("st0")
        r_gb = [Res("gb0"), Res("gb1")]
        dma("sp", CBF[:], cbf_d, "cbf", writes=[r_cbf])
        dma("sp", GB[:, 0:1024], gains_d[0], "gb0", writes=[r_gb[0]])
        dma("sp", PSC[:], pscale_d, "psc", writes=[r_psc])
        dma("sp", BGT[:], bgate_d, "bgt", writes=[r_bgt])
        dma("sp", ICN[:], invcnt_d, "icn", writes=[r_icn])
        P.add("pool", lambda e: e.memset(EPSC, EPS), (), [r_st0])
        P.add("pool", lambda e: e.memset(ONEC, 1.0), (), [r_st0])
        dma("pool", WPM[:].rearrange("p (g d) -> p g d", g=4), wpm_d.rearrange("g p d -> p g d"), "wpm", writes=[r_wpm])

        r_ws = [Res("ws%d" % i) for i in range(4)]
        ws_n = [0]

        def wslot_view(s):
            return WS[:, s * 4096:(s + 1) * 4096].rearrange("p (k e) -> p k e", k=8)

        def wload(parts, reads=()):
            s = ws_n[0] % 4
            ws_n[0] += 1
            v = wslot_view(s)
            oi = []
            for (k0, src) in parts:
                kk = src.shape[1]
                oi.append((v[:, k0:k0 + kk, :], src))
            dma2("pool", oi, "ws%d" % s, reads=reads, writes=[r_ws[s]])
            return s

        def wcols(wd, c0, k=8):
            v = wd.rearrange("(k p) e -> p k e", p=128)
            return [(0, v[:, 0:k // 2, c0:c0 + 512]), (k // 2, v[:, k // 2:k, c0:c0 + 512])]

        RXb = RX[:].bitcast(BF16)
        R2b = R2[:].bitcast(BF16)
        hT = R2b.rearrange("p (h k t) -> p h k t", h=2, k=8)

        def hT_span(k, s):
            return hT[:, s // 2, k, (s % 2) * 512:(s % 2) * 512 + 512]

        def hT_tile(k, tt):
            return hT[:, tt // 8, k, (tt % 8) * 128:(tt % 8) * 128 + 128]

        r_hT = [Res("hT%d" % t) for t in range(NT)]
        r_x = [Res("x%d" % t) for t in range(NT)]

        def xt(tt):
            return RX[:, tt * 1024:(tt + 1) * 1024]

        QT = R1[:, 0:8192].rearrange("p (c t) -> p c t", c=4)
        KT = R1[:, 8192:16384].rearrange("p (c t) -> p c t", c=4)
        VV = R1[:, 16384:24576].rearrange("p (t e) -> p t e", t=16)
        PLT = R1[:, 24576:32768].rearrange("p (c t) -> p c t", c=4)
        r_QT = [[Res("QT%d_%d" % (c, s)) for s in range(4)] for c in range(4)]
        r_KT = [[Res("KT%d_%d" % (c, s)) for s in range(4)] for c in range(4)]
        r_V = [Res("V%d" % t) for t in range(NT)]
        r_PLT = [Res("PLT%d" % g) for g in range(4)]

        def uT(g):
            return RX[:, g * 2048:(g + 1) * 2048]
        r_uT = [Res("uT%d" % g) for g in range(4)]
        SA = RX[:, 8192:10248]
        SB_ = RX[:, 0:2056]
        r_SA = Res("SA")
        r_SB = Res("SB")
        YPT = RXb[:, 20608:28800].rearrange("p (c t) -> p c t", c=4)
        r_YPT = [[Res("YPT%d_%d" % (g, s)) for s in range(4)] for g in range(4)]
        OT = RXb[:, 0:8192].rearrange("p (c t) -> p c t", c=4)
        r_OT = [[Res("OT%d_%d" % (c, s)) for s in range(4)] for c in range(4)]

        ss1 = stcol(16)
        ln1 = stcol(16)
        rs1 = stcol(16)
        r_ss1 = [Res("ss1_%d" % t) for t in range(NT)]
        r_hb = [Res("hb0"), Res("hb1")]

        HBS = [HB[:, 0:1024], JK[:, 0:1024]]

        def norm_b(tt, xin, r_xin, gb_ap, r_gbx, ssc, lnc, rsc, r_stat, hbs=None, rhbs=None):
            hbs = hbs or HBS
            rhbs = rhbs or r_hb
            hb = hbs[tt % len(hbs)]
            rhb = rhbs[tt % len(hbs)]
            rxl = list(r_xin) if isinstance(r_xin, list) else [r_xin]
            act(hb, xin, AF.Square, rxl, [rhb, r_stat], accum=ssc)
            act(lnc, ssc, AF.Ln, [r_stat, r_st0], [r_stat], bias=EPS, scale=1.0 / D)
            act(rsc, lnc, AF.Exp, [r_stat], [r_stat], scale=-0.5)
            P.add("dve", lambda e: e.scalar_tensor_tensor(out=hb, in0=xin, scalar=rsc, in1=gb_ap,
                                                          op0=ALU.mult, op1=ALU.mult),
                  rxl + [r_stat, r_gbx], [rhb])

        def norm_c(tt, dst_res, b, hbs=None, rhbs=None, evac="dve"):
            hbs = hbs or HBS
            rhbs = rhbs or r_hb
            hb = hbs[tt % len(hbs)]
            rhb = rhbs[tt % len(hbs)]
            pb = bank(b).bitcast(BF16)
            for k in range(8):
                o = pb[:, k * 128:(k + 1) * 128]
                i_ = hb[:, k * 128:(k + 1) * 128]
                P.add("pe", (lambda o=o, i_=i_: (lambda e: e.transpose(o, i_, ident)))(),
                      [rhb, r_cbf], [rbank[b]])
            dst = hT[:, tt // 8, :, (tt % 8) * 128:(tt % 8) * 128 + 128]
            src = pb.rearrange("p (k t) -> p k t", k=8)
            if evac == "dve":
                P.add("dve", lambda e: e.tensor_copy(dst, src), [rbank[b]], [dst_res])
            elif evac == "act":
                act(dst, src, AF.Copy, [rbank[b]], [dst_res])

        def norm_evac(tt, dst_res, b):
            pb = bank(b).bitcast(BF16)
            dst = hT[:, tt // 8, :, (tt % 8) * 128:(tt % 8) * 128 + 128]
            src = pb.rearrange("p (k t) -> p k t", k=8)
            act(dst, src, AF.Copy, [rbank[b]], [dst_res])

        for tt in range(NT):
            dma("sp", xt(tt), x_d[tt * 128:(tt + 1) * 128, :], "x%d" % tt, writes=[r_x[tt]])
        for i in range(NT + 1):
            if i < NT:
                norm_b(i, xt(i), r_x[i], GB[:, 0:1024], r_gb[0],
                       ss1[:, i:i + 1], ln1[:, i:i + 1], rs1[:, i:i + 1], r_ss1[i])
            if i >= 1:
                norm_c(i - 1, r_hT[i - 1], (i - 1) % 8)

        dma("sp", GB[:, 0:1024], gains_d[1], "gb0", writes=[r_gb[0]])
        dma("sp", GB[:, 1024:2048], gains_d[2], "gb1", writes=[r_gb[1]])
        if stop_after == "P1":
            P.frozen = True
        bk = [0]

        def nextbank():
            b = bk[0] % 8
            bk[0] += 1
            return b

        for g in range(4):
            r_uT[g].pending = list(r_x)
        sl = [wload(wcols(w_in_d, c * 512), reads=([] if c == 0 else [r_x[NT - 1]])) for c in range(4)]

        def proj_fm(c, cc, s, b, ev, lazy=False):
            wv = wslot_view(sl[c])
            ops = []
            for k in range(8):
                ops.append((lambda k=k: mm(bank(b), wv[:, k, cc * 128:(cc + 1) * 128], hT_span(k, s), k == 0, k == 7,
                                           [r_ws[sl[c]]] + r_hT[4 * s:4 * s + 4], [rbank[b]])))
            if c == 0:
                dst, rd, sc = uT(cc)[:, s * 512:(s + 1) * 512], r_uT[cc], 1.0
            elif c == 1:
                dst, rd, sc = QT[:, cc, s * 512:(s + 1) * 512], r_QT[cc][s], 0.125
            else:
                dst, rd, sc = KT[:, cc, s * 512:(s + 1) * 512], r_KT[cc][s], 1.0

            def evac():
                if ev == "act":
                    act(dst, bank(b), AF.Copy, [rbank[b]], [rd], scale=sc)
                elif sc != 1.0:
                    P.add("dve", lambda e: e.tensor_scalar(out=dst, in0=bank(b), scalar1=sc, scalar2=None, op0=ALU.mult),
                          [rbank[b]], [rd])
                else:
                    P.add("dve", lambda e: e.tensor_copy(dst, bank(b)), [rbank[b]], [rd])
            ops.append(evac)
            if lazy:
                return ops
            for o in ops:
                o()

        def proj_v(tt, b, ev, lazy=False):
            wv = wslot_view(sl[3])
            ops = []
            for k in range(8):
                ops.append((lambda k=k: mm(bank(b), hT_tile(k, tt), wv[:, k, :], k == 0, k == 7,
                                           [r_ws[sl[3]], r_hT[tt]], [rbank[b]])))
            dst = VV[:, tt, :]

            def evac():
                if ev == "act":
                    act(dst, bank(b), AF.Copy, [rbank[b]], [r_V[tt]])
                else:
                    P.add("dve", lambda e: e.tensor_copy(dst, bank(b)), [rbank[b]], [r_V[tt]])
            ops.append(evac)
            if lazy:
                return ops
            for o in ops:
                o()

        for s in range(4):
            for cc in range(4):
                proj_fm(0, cc, s, nextbank(), "act")
        NPRE = 2
        for s in range(NPRE):
            for c in (1, 2):
                for cc in range(4):
                    proj_fm(c, cc, s, nextbank(), "act")
            for tt in range(4 * s, 4 * s + 4):
                proj_v(tt, nextbank(), "act")
        deferred = {}
        for s in range(NPRE, 4):
            micro = []
            for c in (1, 2):
                for cc in range(4):
                    micro += proj_fm(c, cc, s, 7, "dve", lazy=True)
            for tt in range(4 * s, 4 * s + 4):
                micro += proj_v(tt, 7, "dve", lazy=True)
            deferred[s] = micro

        if stop_after == "P2":
            P.frozen = True
        def dve_tt(out, a, b_, op, reads, writes):
            P.add("dve", lambda e: e.tensor_tensor(out=out, in0=a, in1=b_, op=op), reads, writes)

        P.add("dve", lambda e: e.memset(SA[:, 0:8], 0.0), (), [r_SA])
        WIN = (2, 4, 8, 16)
        r_tmp = Res("tmp16")
        for g in range(4):
            u = uT(g)
            dve_tt(SA[:, 9:2056], u[:, 1:2048], u[:, 0:2047], ALU.add, [r_uT[g]], [r_SA])
            P.add("dve", (lambda u=u: (lambda e: e.tensor_copy(SA[:, 8:9], u[:, 0:1])))(), [r_uT[g]], [r_SA])
            cur, rcur = SA, r_SA
            if g == 1:
                S4 = RX[:, 0:2048]
                dve_tt(S4, SA[:, 8:2056], SA[:, 6:2054], ALU.add, [r_SA], [r_SB, r_uT[0]])
                cur, rcur = None, r_SB
                cv1 = S4
            if g >= 2:
                if g == 2:
                    P.add("dve", lambda e: e.memset(SB_[:, 0:8], 0.0), (), [r_SB, r_uT[0], r_uT[1]])
                dve_tt(SB_[:, 8:2056], SA[:, 8:2056], SA[:, 6:2054], ALU.add, [r_SA], [r_SB, r_uT[0], r_uT[1]])
                cur, rcur = SB_, r_SB
            if g >= 2:
                dve_tt(SA[:, 8:2056], SB_[:, 8:2056], SB_[:, 4:2052], ALU.add, [r_SB], [r_SA])
                cur, rcur = SA, r_SA
            if g >= 3:
                dve_tt(SB_[:, 8:2056], SA[:, 8:2056], SA[:, 0:2048], ALU.add, [r_SA], [r_SB])
                cur, rcur = SB_, r_SB
            w = WIN[g]
            cv = cv1 if g == 1 else cur[:, 8:2056]
            P.add("dve", (lambda cv=cv, u=u, g=g, w=w: (lambda e: e.scalar_tensor_tensor(
                out=PLT[:, g, :], in0=cv, scalar=1.0 / w, in1=u, op0=ALU.mult, op1=ALU.subtract)))(),
                [rcur, r_uT[g]], [r_PLT[g]])
            tmp = ST[:, 272:288]
            P.add("dve", (lambda cv=cv, g=g, tmp=tmp: (lambda e: e.tensor_tensor(
                out=tmp, in0=cv[:, 0:16], in1=ICN[:, g * 16:(g + 1) * 16], op=ALU.mult)))(),
                [rcur, r_icn], [r_tmp])
            P.add("dve", (lambda u=u, g=g, tmp=tmp: (lambda e: e.tensor_tensor(
                out=PLT[:, g, 0:16], in0=tmp, in1=u[:, 0:16], op=ALU.subtract)))(),
                [r_tmp, r_uT[g]], [r_PLT[g]])
        for g in range(4):
            for s in range(4):
                b = nextbank()
                mm(bank(b), WPM[:, g * 128:(g + 1) * 128], PLT[:, g, s * 512:(s + 1) * 512], True, True,
                   [r_wpm, r_PLT[g]], [rbank[b]])
                act(YPT[:, g, s * 512:(s + 1) * 512], bank(b), AF.Copy, [rbank[b], r_psc], [r_YPT[g][s]],
                    scale=PSC[:, g:g + 1])

        r_hs = [Res("hs%d" % i) for i in range(8)]
        for i in range(8):
            r_hs[i].pending = [r_ws[i // 2]]
        hs_n = [0]

        def hslot_view(i):
            return WS[:, i * 2048:(i + 1) * 2048].rearrange("p (k e) -> p k e", k=8)

        def hload(parts):
            i = hs_n[0] % 8
            hs_n[0] += 1
            v = hslot_view(i)
            oi = []
            for (k0, src) in parts:
                oi.append((v[:, k0:k0 + src.shape[1], :], src))
            dma2("pool", oi, "hs%d" % i, writes=[r_hs[i]])
            return i

        def p5_load(gq, which):
            c0 = gq * 256
            if which == "a":
                v = w_gate_d.rearrange("(k p) e -> p k e", p=128)
                return hload([(0, v[:, 0:4, c0:c0 + 256]), (4, v[:, 4:8, c0:c0 + 256])])
            if which == "b":
                v = w_gate_d.rearrange("(k p) e -> p k e", p=128)
                return hload([(0, v[:, 0:4, 1024 + c0:1024 + c0 + 256]), (4, v[:, 4:8, 1024 + c0:1024 + c0 + 256])])
            vp = w_brp_d.rearrange("(k p) e -> p k e", p=128)
            vs = w_brs_d.rearrange("(k p) e -> p k e", p=128)
            return hload([(0, vp[:, :, c0:c0 + 256]), (4, vs[:, :, c0:c0 + 256])])
        p5hs = {}

        def p5_prefetch():
            for gq in (0, 1):
                for wh in "abc":
                    p5hs[(gq, wh)] = p5_load(gq, wh)
            p5hs[(2, "a")] = p5_load(2, "a")
            p5hs[(2, "b")] = p5_load(2, "b")
        p5w0 = [None]

        if stop_after == "P3":
            P.frozen = True
        def f32v(off):
            return RX[:, off:off + 1024].rearrange("p (h n) -> p h n", h=2)

        def bf16v(off32):
            return RXb[:, 2 * off32:2 * off32 + 1024].rearrange("p (h n) -> p h n", h=2)
        E_ = [f32v(4096), f32v(5120)]
        ARG = [f32v(6144), f32v(7168)]
        R1f_a = R1[:].bitcast(F32)
        CCS = [f32v(8192), R1f_a[:, 12288:13312].rearrange("p (h n) -> p h n", h=2)]
        SP_ = [bf16v(9216), bf16v(9728)]
        ATT = [bf16v(14400), bf16v(14912)]
        r_E = [Res("E0"), Res("E1")]
        r_ARG = [Res("ARG0"), Res("ARG1")]
        r_CCS = [Res("CC0"), Res("CC1")]
        r_CC = r_CCS[0]
        r_CCS[1].pending = list(r_PLT)
        r_SP = [Res("SP0"), Res("SP1")]
        r_ATT = [Res("ATT0"), Res("ATT1")]
        old = r_uT + [r_SA, r_SB]
        for r in r_E + r_ARG + [r_CC] + r_SP:
            r.pending = list(old)
        for c in range(4):
            for s in range(4):
                r_OT[c][s].pending = list(old)

        ZP = [PP[0], PP[1]]
        r_Z = [[rbank[0], rbank[1]], [rbank[2], rbank[3]]]
        TP = PP[2]
        r_T = [rbank[4], rbank[5]]
        OACC = [bank(6), bank(6)]
        r_OACC = [rbank[6], rbank[6]]

        chains = []
        chain_id = 0
        for j in range(4):
            for hp in range(4):
                kbs = [4 * j + 3, 4 * j + 2, 4 * j + 1, 4 * j] + list(range(4 * j - 1, -1, -1))
                ch = []
                for i, kb in enumerate(kbs):
                    c0 = 128 * (kb - 4 * j) if kb >= 4 * j else 0
                    ch.append(dict(j=j, hp=hp, kb=kb, c0=c0, first=(i == 0), last=(i == len(kbs) - 1),
                                   diag=(kb >= 4 * j), chain=chain_id))
                chains.append(ch)
                chain_id += 1
        tiles = []
        for ch in chains:
            tiles.extend(ch)
        sched = {}
        for s_ in range(NPRE, 4):
            lo = 0 if s_ == NPRE else min(n for n, t in enumerate(tiles) if t["j"] == s_ - 1)
            hi = min(n for n, t in enumerate(tiles) if t["j"] == s_)
            micro = deferred[s_]
            for g, op_ in enumerate(micro):
                n = lo + (g * (hi - lo)) // len(micro)
                sched.setdefault(n, []).append(op_)
        last_sched = max(sched)
        NTL = len(tiles)

        def pv(t3, c0):
            return t3[:, :, c0:512]

        def z3(zb):
            return ZP[zb][:].rearrange("p (h n) -> p h n", h=2)

        def emit_qk(n):
            t = tiles[n]
            zb = n % 2
            j, hp, kb, c0 = t["j"], t["hp"], t["kb"], t["c0"]
            for hd in range(2):
                rows = slice(64 * hd, 64 * hd + 64)
                out = ZP[zb][:, hd * 512 + c0:hd * 512 + 512]
                lhsT = KT[rows, hp, kb * 128:(kb + 1) * 128]
                rhs = QT[rows, hp, j * 512 + c0:(j + 1) * 512]
                mm(out, lhsT, rhs, True, not t["diag"], [r_KT[hp][kb // 4], r_QT[hp][j]], [r_Z[zb][hd]])
            if t["diag"]:
                for hd in range(2):
                    out = ZP[zb][:, hd * 512 + c0:hd * 512 + c0 + 128]
                    mm(out, ident, maskneg, False, True, [r_cbf], [r_Z[zb][hd]])

        def emit_exp1(n):
            t = tiles[n]
            zb = n % 2
            act(pv(E_[zb], t["c0"]), pv(z3(zb), t["c0"]), AF.Exp, r_Z[zb], [r_E[zb]])

        def emit_ln(n):
            t = tiles[n]
            zb = n % 2
            act(pv(SP_[zb], t["c0"]), pv(E_[zb], t["c0"]), AF.Ln, [r_E[zb], r_st0], [r_SP[zb]], bias=1.0)

        def emit_tri(n):
            t = tiles[n]
            zb = n % 2
            c0 = t["c0"]
            for hd in range(2):
                out = ZP[zb][:, hd * 512 + c0:hd * 512 + 512]
                mm(out, negtri, SP_[zb][:, hd, c0:512], False, True, [r_cbf, r_SP[zb]], [r_Z[zb][hd]], skip=True)
            for hd in range(2):
                out = TP[:, hd * 512 + c0:hd * 512 + 512]
                mm(out, ones, SP_[zb][:, hd, c0:512], True, True, [r_cbf, r_SP[zb]], [r_T[hd]])

        def emit_dve(n):
            t = tiles[n]
            zb = n % 2
            c0 = t["c0"]
            CC = CCS[t["chain"] % 2]
            rcc = r_CCS[t["chain"] % 2]
            if t["first"]:
                P.add("pool", (lambda CC=CC: (lambda e: e.memset(CC, 0.0)))(), (), [rcc])
            a = pv(ARG[zb], c0)
            z = pv(z3(zb), c0)
            c = pv(CC, c0)
            tp = pv(TP[:].rearrange("p (h n) -> p h n", h=2), c0)
            dve_tt(a, z, c, ALU.subtract, r_Z[zb] + [rcc], [r_ARG[zb]])
            if not t["last"]:
                dve_tt(c, tp, c, ALU.add, r_T + [rcc], [rcc])

        def emit_exp2(n):
            t = tiles[n]
            zb = n % 2
            act(pv(ATT[zb], t["c0"]), pv(ARG[zb], t["c0"]), AF.Exp, [r_ARG[zb]], [r_ATT[zb]])

        def emit_av(n):
            t = tiles[n]
            zb = n % 2
            c0, hp, kb, j = t["c0"], t["hp"], t["kb"], t["j"]
            ob = t["chain"] % 2
            for hd in range(2):
                out = OACC[ob][64 * hd:64 * hd + 64, c0:512]
                lhsT = VV[:, kb, (2 * hp + hd) * 64:(2 * hp + hd) * 64 + 64]
                mm(out, lhsT, ATT[zb][:, hd, c0:512], t["first"], t["last"], [r_V[kb], r_ATT[zb]], [r_OACC[ob]], skip=True)
            if t["last"]:
                dst = OT[:, hp, j * 512:(j + 1) * 512]
                src = OACC[ob]
                P.add("dve", lambda e: e.tensor_copy(dst, src), [r_OACC[ob]], [r_OT[hp][j]])

        emit_qk(0)
        for n in range(NTL + 2):
            if n + 1 < NTL:
                emit_qk(n + 1)
            if n < NTL:
                emit_exp1(n)
            if 2 <= n:
                emit_exp2(n - 2)
            if n < NTL:
                emit_ln(n)
                emit_tri(n)
                emit_dve(n)
            if 2 <= n:
                emit_av(n - 2)
            for op_ in sched.get(n, []):
                op_()
            if n == last_sched:
                p5_prefetch()

        if stop_after == "P4":
            P.frozen = True
        NXS = 6
        R1f = R1[:].bitcast(F32)
        XS = [R1f[:, 8192 + j * 1024:8192 + (j + 1) * 1024] for j in range(NXS)]
        r_xs = [Res("xs%d" % j) for j in range(NXS)]
        for j in range(NXS):
            r_xs[j].pending = r_V + r_PLT + [r_CCS[1]]
            dma("sp", XS[j], x_d[j * 128:(j + 1) * 128, :], "xs%d" % j, writes=[r_xs[j]])
        MT = R1[:, 0:16384].rearrange("p (c t) -> p c t", c=8)
        r_MT = [[Res("MT%d_%d" % (c, s)) for s in range(4)] for c in range(8)]
        oldq = [r for row in r_QT for r in row] + [r for row in r_KT for r in row]
        for c in range(8):
            for s in range(4):
                r_MT[c][s].pending = list(oldq)
        TM = [[RX[:, 4096 + (i * 4 + q) * 512:4096 + (i * 4 + q + 1) * 512] for q in range(4)] for i in range(2)]
        r_TM = [[Res("TM%d_%d" % (i, q)) for q in range(4)] for i in range(2)]
        olda = r_E + r_ARG + [r_CC] + r_SP
        for i in range(2):
            for q in range(4):
                r_TM[i][q].pending = list(olda)
        it = 0
        so = [None, None]
        for gq in range(4):
            ha, hb_, hc = p5hs[(gq, "a")], p5hs[(gq, "b")], p5hs[(gq, "c")]
            wa, wb, wc = hslot_view(ha), hslot_view(hb_), hslot_view(hc)
            for dcl in range(2):
                dc = 2 * gq + dcl
                cs = slice(dcl * 128, dcl * 128 + 128)
                for s in range(4):
                    pb = 4 * (it % 2)
                    ti = it % 2
                    it += 1
                    bgp, bgs, byp, bys = pb, pb + 1, pb + 2, pb + 3
                    for k in range(8):
                        mm(bank(bgp), wa[:, k, cs], hT_span(k, s), k == 0, k == 7,
                           [r_hs[ha]] + r_hT[4 * s:4 * s + 4], [rbank[bgp]])
                    for k in range(8):
                        mm(bank(bgs), wb[:, k, cs], hT_span(k, s), k == 0, k == 7,
                           [r_hs[hb_]] + r_hT[4 * s:4 * s + 4], [rbank[bgs]])
                    for k in range(4):
                        mm(bank(byp), wc[:, k, cs], YPT[:, k, s * 512:(s + 1) * 512], k == 0, k == 3,
                           [r_hs[hc], r_YPT[k][s]], [rbank[byp]])
                    for k in range(4):
                        mm(bank(bys), wc[:, 4 + k, cs], OT[:, k, s * 512:(s + 1) * 512], k == 0, k == 3,
                           [r_hs[hc], r_OT[k][s]], [rbank[bys]])
                    act(TM[ti][0], bank(bgp), AF.Sigmoid, [rbank[bgp], r_bgt], [r_TM[ti][0]], bias=BGT[:, dc:dc + 1])
                    act(TM[ti][1], bank(bgs), AF.Sigmoid, [rbank[bgs], r_bgt], [r_TM[ti][1]], bias=BGT[:, 8 + dc:9 + dc])
                    dve_tt(TM[ti][2], bank(byp), TM[ti][0], ALU.mult, [rbank[byp], r_TM[ti][0]], [r_TM[ti][2]])
                    dve_tt(TM[ti][3], bank(bys), TM[ti][1], ALU.mult, [rbank[bys], r_TM[ti][1]], [r_TM[ti][3]])
                    dve_tt(MT[:, dc, s * 512:(s + 1) * 512], TM[ti][2], TM[ti][3], ALU.add,
                           [r_TM[ti][2], r_TM[ti][3]], [r_MT[dc][s]])
            if gq == 0:
                p5hs[(2, "c")] = p5_load(2, "c")
                p5hs[(3, "a")] = p5_load(3, "a")
                p5hs[(3, "b")] = p5_load(3, "b")
            elif gq == 1:
                p5hs[(3, "c")] = p5_load(3, "c")
                assert hs_n[0] == 12
                for s_ in range(4):
                    r_ws[s_].pending = [r_hs[2 * s_], r_hs[2 * s_ + 1]]
                ws_n[0] = 6
                so[0] = wload(wcols(w_out_d, 0))
            elif gq == 2:
                so[1] = wload(wcols(w_out_d, 512))

        if stop_after == "P5":
            P.frozen = True
        oldrx = [r for row in r_YPT for r in row] + [r for row in r_OT for r in row] + \
                [r for row in r_TM for r in row] + r_ATT
        r_x1h = [[Res("x1_%d_0" % t), Res("x1_%d_1" % t)] for t in range(NT)]
        for tt in range(NT):
            r_x1h[tt][0].pending = list(oldrx)
            r_x1h[tt][1].pending = list(oldrx)
            if tt >= NXS:
                dma("sp", xt(tt), x_d[tt * 128:(tt + 1) * 128, :], "x%d" % tt, reads=[r_ws[so[1]]], writes=r_x1h[tt])
        r_h2T = [Res("h2T%d" % t) for t in range(NT)]
        for tt in range(NT):
            r_h2T[tt].pending = list(r_hT)
        ssA = stcol(32)
        ssS = stcol(16)
        lnA = stcol(16)
        rsA = stcol(16)
        ssB = stcol(16)
        lnB = stcol(16)
        rsB = stcol(16)
        r_stA = [Res("stA%d" % t) for t in range(NT)]
        r_stB = [Res("stB%d" % t) for t in range(NT)]
        HBS4 = HBS + [R1[:, 28672:29696], R1[:, 29696:30720]]
        r_hb4 = r_hb + [Res("hb2"), Res("hb3")]
        JK6 = R1[:, 30720:31232]
        r_jk6 = Res("jk6")
        for r in r_hb4[2:] + [r_jk6]:
            r.pending = r_V + r_PLT + [r_CCS[1]]

        JK6w = R1[:, 30720:31744]

        def p6_mm(T):
            b0 = 2 * (T % 3)
            for dh in range(2):
                wv = wslot_view(so[dh])
                for k in range(8):
                    mm(bank(b0 + dh), MT[:, k, T * 128:(T + 1) * 128], wv[:, k, :], k == 0, k == 7,
                       [r_ws[so[dh]], r_MT[k][T // 4]], [rbank[b0 + dh]])

        def p6_sqA(T):
            b0 = 2 * (T % 3)
            act(JK6w, PP[T % 3][:], AF.Square, [rbank[b0], rbank[b0 + 1]], [r_jk6, r_stA[T]], accum=ssS[:, T:T + 1])

        def p6_lnA(T):
            act(lnA[:, T:T + 1], ssS[:, T:T + 1], AF.Ln, [r_stA[T], r_st0], [r_stA[T]], bias=EPS, scale=1.0 / D)

        def p6_expA(T):
            act(rsA[:, T:T + 1], lnA[:, T:T + 1], AF.Exp, [r_stA[T]], [r_stA[T]], scale=-0.5)

        def p6_res(T):
            b0 = 2 * (T % 3)
            for dh in range(2):
                P.add("dve", (lambda b=b0 + dh, T=T, dh=dh: (lambda e: e.scalar_tensor_tensor(
                    out=bank(b), in0=bank(b), scalar=rsA[:, T:T + 1], in1=GB[:, dh * 512:(dh + 1) * 512],
                    op0=ALU.mult, op1=ALU.mult)))(),
                    [rbank[b0 + dh], r_stA[T], r_gb[0]], [rbank[b0 + dh]])
            for dh in range(2):
                xo = xt(T)[:, dh * 512:(dh + 1) * 512]
                if T < NXS:
                    xi = XS[T][:, dh * 512:(dh + 1) * 512]
                    dve_tt(xo, bank(b0 + dh), xi, ALU.add, [rbank[b0 + dh], r_xs[T]], [r_x1h[T][dh]])
                else:
                    dve_tt(xo, bank(b0 + dh), xo, ALU.add, [rbank[b0 + dh], r_x1h[T][dh]], [r_x1h[T][dh]])

        def p6_sqB(T):
            act(HBS4[T % 4], xt(T), AF.Square, r_x1h[T], [r_hb4[T % 4], r_stB[T]], accum=ssB[:, T:T + 1])

        def p6_lnB(T):
            act(lnB[:, T:T + 1], ssB[:, T:T + 1], AF.Ln, [r_stB[T], r_st0], [r_stB[T]], bias=EPS, scale=1.0 / D)

        def p6_expB(T):
            act(rsB[:, T:T + 1], lnB[:, T:T + 1], AF.Exp, [r_stB[T]], [r_stB[T]], scale=-0.5)

        def p6_h(T):
            hb = HBS4[T % 4]
            P.add("dve", (lambda hb=hb, T=T: (lambda e: e.scalar_tensor_tensor(
                out=hb, in0=xt(T), scalar=rsB[:, T:T + 1], in1=GB[:, 1024:2048], op0=ALU.mult, op1=ALU.mult)))(),
                r_x1h[T] + [r_stB[T], r_gb[1]], [r_hb4[T % 4]])

        def ok(T):
            return 0 <= T < NT

        for i in range(NT + 6):
            if ok(i - 5):
                norm_c(i - 5, r_h2T[i - 5], 6 + (i - 5) % 2, HBS4, r_hb4, evac="none")
            if ok(i):
                p6_mm(i)
            if ok(i - 1):
                p6_sqA(i - 1)
            if ok(i - 3):
                p6_sqB(i - 3)
            if ok(i - 1):
                p6_lnA(i - 1)
            if ok(i - 3):
                p6_lnB(i - 3)
            if ok(i - 1):
                p6_expA(i - 1)
            if ok(i - 3):
                p6_expB(i - 3)
            if ok(i - 6):
                norm_evac(i - 6, r_h2T[i - 6], 6 + (i - 6) % 2)
            if ok(i - 2):
                p6_res(i - 2)
            if ok(i - 4):
                p6_h(i - 4)

        if stop_after == "P6":
            P.frozen = True
        AT = R1[:].rearrange("p (c t) -> p c t", c=32)
        r_AT = [[Res("AT%d_%d" % (c, s)) for s in range(2)] for c in range(32)]
        oldm = [r for row in r_MT for r in row] + r_V + r_PLT + r_xs + r_hb4[2:] + [r_jk6, r_CCS[1]]
        dma("sp", GB[:, 0:1024], gains_d[3], "gb0", writes=[r_gb[0]])
        FFS = [R2[:, h * 4096:(h + 1) * 4096].rearrange("p (t e) -> p t e", t=8) for h in range(2)]
        ssC = stcol(32)
        ssT = stcol(16)
        lnC = stcol(16)
        rsC = stcol(16)
        r_stC = [Res("stC%d" % t) for t in range(NT)]
        r_rt = [Res("rt0"), Res("rt1")]
        for r in r_rt:
            r.pending = [r_gb[1]]
        for half in range(2):
            for c in range(32):
                for s in range(2):
                    r_AT[c][s].pending = list(oldm)
            oldm = []
            for f4 in range(8):
                su = wload(wcols(w_up_d, f4 * 512))
                wv = wslot_view(su)
                for fl in range(4):
                    fc = 4 * f4 + fl
                    for s in range(2):
                        b = nextbank()
                        for k in range(8):
                            mm(bank(b), wv[:, k, fl * 128:(fl + 1) * 128],
                               hT[:, half, k, s * 512:(s + 1) * 512], k == 0, k == 7,
                               [r_ws[su]] + r_h2T[8 * half + 4 * s:8 * half + 4 * s + 4], [rbank[b]])
                        dst = AT[:, fc, s * 512:(s + 1) * 512]
                        ti = b % 2
                        tmpr = GB[:, 1024 + ti * 512:1024 + (ti + 1) * 512]
                        act(tmpr, bank(b), AF.Relu, [rbank[b]], [r_rt[ti]])
                        dve_tt(dst, bank(b), tmpr, ALU.mult, [rbank[b], r_rt[ti]], [r_AT[fc][s]])
            if stop_after == "P7a":
                P.frozen = True
            r_ffs = [Res("ffs%d_%d" % (half, t)) for t in range(8)]
            for t in range(8):
                r_ffs[t].pending = list(r_h2T[8 * half:8 * half + 8])
            for dh in range(2):
                for f8 in range(4):
                    vd = w_down_d.rearrange("(c p) d -> p c d", p=128)
                    sd = wload([(0, vd[:, 8 * f8:8 * f8 + 4, dh * 512:(dh + 1) * 512]),
                                (4, vd[:, 8 * f8 + 4:8 * f8 + 8, dh * 512:(dh + 1) * 512])])
                    wv = wslot_view(sd)
                    if f8 == 3:
                        for t in range(8):
                            for fl in range(8):
                                fc = 8 * f8 + fl
                                mm(bank(t), AT[:, fc, t * 128:(t + 1) * 128], wv[:, fl, :], fc == 0, fc == 31,
                                   [r_ws[sd], r_AT[fc][t // 4]], [rbank[t]])
                    else:
                        for fl in range(8):
                            fc = 8 * f8 + fl
                            for t in range(8):
                                mm(bank(t), AT[:, fc, t * 128:(t + 1) * 128], wv[:, fl, :], fc == 0, fc == 31,
                                   [r_ws[sd], r_AT[fc][t // 4]], [rbank[t]])
                if stop_after == "P7b":
                    P.frozen = True
                def ev_stats(t, half=half, dh=dh):
                    tt = 8 * half + t
                    act(HBS[t % 2][:, 0:512], bank(t), AF.Square, [rbank[t]], [r_hb[t % 2], r_stC[tt]],
                        accum=ssC[:, 2 * tt + dh:2 * tt + dh + 1])
                    if dh == 0:
                        P.add("dve", (lambda t=t, half=half: (lambda e: e.tensor_copy(FFS[half][:, t, :], bank(t))))(),
                              [rbank[t]], [r_ffs[t]])
                    else:
                        dve_tt(ssT[:, tt:tt + 1], ssC[:, 2 * tt:2 * tt + 1], ssC[:, 2 * tt + 1:2 * tt + 2], ALU.add,
                               [r_stC[tt]], [r_stC[tt]])
                        act(lnC[:, tt:tt + 1], ssT[:, tt:tt + 1], AF.Ln, [r_stC[tt], r_st0], [r_stC[tt]],
                            bias=EPS, scale=1.0 / D)
                        act(rsC[:, tt:tt + 1], lnC[:, tt:tt + 1], AF.Exp, [r_stC[tt]], [r_stC[tt]], scale=-0.5)

                def ev_big(t, half=half):
                    tt = 8 * half + t
                    for d2 in range(2):
                        src = FFS[half][:, t, :] if d2 == 0 else bank(t)
                        rsrc = r_ffs[t] if d2 == 0 else rbank[t]
                        P.add("dve", (lambda src=src, tt=tt, d2=d2: (lambda e: e.scalar_tensor_tensor(
                            out=src, in0=src, scalar=rsC[:, tt:tt + 1], in1=GB[:, d2 * 512:(d2 + 1) * 512],
                            op0=ALU.mult, op1=ALU.mult)))(),
                            [rsrc, r_stC[tt], r_gb[0]], [rsrc])
                    for d2 in range(2):
                        src = FFS[half][:, t, :] if d2 == 0 else bank(t)
                        rsrc = r_ffs[t] if d2 == 0 else rbank[t]
                        xs = xt(tt)[:, d2 * 512:(d2 + 1) * 512]
                        P.add("dve", (lambda xs=xs, src=src: (lambda e: e.tensor_tensor(out=xs, in0=src, in1=xs, op=ALU.add)))(),
                              [rsrc, r_x1h[tt][d2]], [r_x1h[tt][d2]])
                    if stop_after != "P7d":
                        dma("sp", out_d[tt * 128:(tt + 1) * 128, :], xt(tt), "st%d" % tt, reads=r_x1h[tt])

                for t in range(9):
                    if t < 8:
                        ev_stats(t)
                    if dh == 1 and t >= 1:
                        ev_big(t - 1)
                if stop_after == "P7c" and dh == 0:
                    P.frozen = True

        P.frozen = False
        if debug:
            dbg_src = {
                "hT": (R2b, []), "R1": (R1[:], []), "RX": (RX[:], []),
            }
            for name, _, _ in debug:
                src, _r = dbg_src[name]
                P.add("sp", (lambda src=src, name=name: (lambda e: [e.dma_start(out=dbg_d[name], in_=src)]))(),
                      [], [], dma_key="dbg_" + name, ninst=1, barrier=True)

        P.finalize()
        P.plan_waits()
        for key in P.dma_cnt:
            sem_dma[key] = es.enter_context(nc.semaphore("sd_" + key))
        final_waits = [(k, v) for k, v in P.dma_cnt.items() if k.startswith("st") or k.startswith("dbg_")]

        def emit_engine(name, e):
            waited = {}
            for op in P.ops:
                if op.eng != name:
                    continue
                todo = [(sem_dma[key[1]] if key[0] == "d" else sem_eng[key[1]], val) for key, val in op.waits]
                embed = None
                if todo and op.dma_key is None and name in ("act", "dve", "pool", "pe") and not op.multi:
                    embed = todo.pop()
                for sem, val in todo:
                    e.wait_ge(sem, val)
                res = op.fn(e)
                if embed is not None:
                    res._wait_ge(embed[0], embed[1])
                if op.dma_key is not None:
                    for inst in res:
                        inst.then_inc(sem_dma[op.dma_key], 16)
                elif op.signal:
                    res.then_inc(sem_eng[name], 1)
            if name == "sp":
                if debug:
                    for en in ("pe", "act", "dve", "pool"):
                        pass
                for k, v in final_waits:
                    e.wait_ge(sem_dma[k], v)

        with nc.Block() as block:
            @block.tensor
            def _(e):
                emit_engine("pe", e)

            @block.scalar
            def _(e):
                emit_engine("act", e)

            @block.vector
            def _(e):
                emit_engine("dve", e)

            @block.gpsimd
            def _(e):
                emit_engine("pool", e)

            @block.sync
            def _(e):
                emit_engine("sp", e)
    return nc


_NC_CACHE = {}


def _consts():
    bf = ml_dtypes.bfloat16
    p = np.arange(128)[:, None]
    c = np.arange(128)[None, :]
    ident = (p == c).astype(np.float32)
    negtri = -(p >= c).astype(np.float32)
    ones = np.ones((128, 128), np.float32)
    maskneg = np.where(p < c, 0.0, NEG).astype(np.float32)
    cbf = np.concatenate([ident, negtri, ones, maskneg], axis=1).astype(bf)
    invcnt = np.zeros((128, 64), np.float32)
    for g, w in enumerate((2, 4, 8, 16)):
        t = np.arange(16)
        invcnt[:, g * 16:(g + 1) * 16] = (1.0 / np.minimum(t + 1, w))[None, :]
    return cbf, invcnt


def kernel(x, g_pre_mix, w_in, w_pool_mix, pool_scale, w_br_pool, w_br_sb, w_gate, b_gate,
           w_out, g_post_mix, g_pre_mlp, w_up, w_down, g_post_mlp, _debug=None, _stop=None):
    f = lambda a: np.ascontiguousarray(np.asarray(a, dtype=np.float32))
    x = f(x)
    B = x.shape[0]
    key = "dbg" if _debug else "main"
    if key not in _NC_CACHE:
        _NC_CACHE[key] = build_nc(_debug, _stop)
    nc = _NC_CACHE[key]
    cbf, invcnt = _consts()
    gains = np.stack([np.broadcast_to(f(g)[None, :], (128, D)) for g in
                      (g_pre_mix, g_post_mix, g_pre_mlp, g_post_mlp)]).copy()
    pscale = np.ascontiguousarray(f(pool_scale).reshape(4, 128).T)
    bgate = np.ascontiguousarray(f(b_gate).reshape(16, 128).T)
    shared = {
        "w_in": f(w_in), "w_gate": f(w_gate), "w_br_pool": f(w_br_pool), "w_br_sb": f(w_br_sb),
        "w_out": f(w_out), "w_up": f(w_up), "w_down": f(w_down), "w_pool_mix": f(w_pool_mix),
        "gains": gains, "pscale": pscale, "bgate": bgate, "invcnt": invcnt, "cbf": cbf,
    }
    in_maps = [dict(shared, x=x[b]) for b in range(B)]
    res = run_bass_kernel_spmd(nc, in_maps, core_ids=list(range(B)))
    out = np.stack([np.asarray(r["out"], dtype=np.float32) for r in res.results], axis=0)
    if _debug:
        return out, [{k: np.asarray(v) for k, v in r.items()} for r in res.results]
    return out
```

```python
import os
import numpy as np
import ml_dtypes
from contextlib import ExitStack
import concourse.bass as bass
import concourse.mybir as mybir
from concourse.bass_utils import run_bass_kernel_spmd

F32 = mybir.dt.float32
BF16 = mybir.dt.bfloat16
AF = mybir.ActivationFunctionType
ALU = mybir.AluOpType

S = 2048
D = 1024
NT = S // 128
NEG = -30000.0
EPS = 1e-6


class Res:
    __slots__ = ("name", "w", "rs", "pending", "excl")

    def __init__(self, name, excl=False):
        self.name = name
        self.excl = excl
        self.w = None
        self.rs = {}
        self.pending = []


class Op:
    __slots__ = ("idx", "eng", "fn", "dma_key", "ninst", "deps", "signal", "sigval", "multi", "waits")

    def __init__(self, idx, eng, fn, dma_key, ninst):
        self.idx = idx
        self.eng = eng
        self.fn = fn
        self.dma_key = dma_key
        self.ninst = ninst
        self.deps = set()
        self.signal = False
        self.sigval = 0
        self.multi = False
        self.waits = []


class Prog:
    ENGS = ("pe", "act", "dve", "pool", "sp")

    def __init__(self):
        self.ops = []
        self.dma_cnt = {}
        self.frozen = False

    def add(self, eng, fn, reads=(), writes=(), dma_key=None, ninst=1, barrier=False):
        if self.frozen:
            return None
        op = Op(len(self.ops), eng, fn, dma_key, ninst)
        if barrier:
            last = {}
            for o in self.ops:
                last[(o.eng, o.dma_key)] = o
            for o in last.values():
                op.deps.add(o)
        is_dma = dma_key is not None
        writes = list(writes)
        extra = []
        for w in writes:
            if w.pending:
                extra.extend(w.pending)
                w.pending = []
        deps = []
        for r in reads:
            if r.w is not None:
                deps.append((r.w, "raw"))
            if r.excl:
                for rd in r.rs.values():
                    if rd.eng != eng:
                        deps.append((rd, "raw"))
        for w in writes + extra:
            if w.w is not None:
                deps.append((w.w, "waw"))
            for rd in w.rs.values():
                deps.append((rd, "war"))
        for d, kind in deps:
            if d is op:
                continue
            d_dma = d.dma_key is not None
            if (not d_dma) and (not is_dma) and d.eng == eng and eng == "pe":
                continue
            op.deps.add(d)
        for r in reads:
            r.rs[(eng, dma_key)] = op
        for w in writes:
            w.w = op
            w.rs = {}
        if is_dma:
            self.dma_cnt[dma_key] = self.dma_cnt.get(dma_key, 0) + 16 * ninst
            op.sigval = self.dma_cnt[dma_key]
        self.ops.append(op)
        return op

    def finalize(self):
        for op in self.ops:
            for d in op.deps:
                if d.dma_key is None:
                    d.signal = True
        cnt = {e: 0 for e in self.ENGS}
        for op in self.ops:
            if op.dma_key is None and op.signal:
                cnt[op.eng] += 1
                op.sigval = cnt[op.eng]


    def plan_waits(self):
        waited = {e: {} for e in self.ENGS}
        know = {}
        for op in self.ops:
            w = waited[op.eng]
            need = {}
            for d in op.deps:
                key = ("d", d.dma_key) if d.dma_key is not None else ("e", d.eng)
                if d.sigval > need.get(key, 0):
                    need[key] = d.sigval
            op.waits = []
            cand = [(key, val) for key, val in sorted(need.items(), key=lambda kv: str(kv[0])) if w.get(key, 0) < val]
            keep = []
            for c in cand:
                implied = False
                for o in cand:
                    if o is not c and know.get(o, {}).get(c[0], 0) >= c[1] and not (
                            know.get(c, {}).get(o[0], 0) >= o[1] and cand.index(c) < cand.index(o)):
                        implied = True
                        break
                if not implied:
                    keep.append(c)
            for key, val in keep:
                op.waits.append((key, val))
            for key, val in cand:
                w[key] = max(w.get(key, 0), val)
                for k2, v2 in know.get((key, val), {}).items():
                    if v2 > w.get(k2, 0):
                        w[k2] = v2
            if op.dma_key is not None:
                know[(("d", op.dma_key), op.sigval)] = dict(w)
            elif op.signal:
                kk = dict(w)
                kk[("e", op.eng)] = max(kk.get(("e", op.eng), 0), op.sigval)
                know[(("e", op.eng), op.sigval)] = kk


def build_nc(debug=None, stop_after=None):
    nc = bass.Bass("TRN2", target_bir_lowering=False)
    P = Prog()

    def dram_in(name, shape, dt=F32):
        return nc.dram_tensor(name, list(shape), dt, kind="ExternalInput").ap()

    x_d = dram_in("x", [S, D])
    w_in_d = dram_in("w_in", [D, 2048])
    w_gate_d = dram_in("w_gate", [D, 2048])
    w_brp_d = dram_in("w_br_pool", [512, D])
    w_brs_d = dram_in("w_br_sb", [512, D])
    w_out_d = dram_in("w_out", [D, D])
    w_up_d = dram_in("w_up", [D, 4096])
    w_down_d = dram_in("w_down", [4096, D])
    wpm_d = dram_in("w_pool_mix", [4, 128, 128])
    gains_d = dram_in("gains", [4, 128, D])
    pscale_d = dram_in("pscale", [128, 4])
    bgate_d = dram_in("bgate", [128, 16])
    invcnt_d = dram_in("invcnt", [128, 64])
    cbf_d = dram_in("cbf", [128, 512], BF16)
    out_d = nc.dram_tensor("out", [S, D], F32, kind="ExternalOutput").ap()
    dbg_d = {}
    if debug:
        for name, shape, dt in debug:
            dbg_d[name] = nc.dram_tensor("dbg_" + name, list(shape), dt, kind="ExternalOutput").ap()

    es = ExitStack()
    with es:
        def sb(name, shape, dt):
            return es.enter_context(nc.sbuf_tensor(name, list(shape), dt))

        RX = sb("RX", [128, 16384], F32)
        R1 = sb("R1", [128, 32768], BF16)
        R2 = sb("R2", [128, 8192], F32)
        WS = sb("WS", [128, 16384], BF16)
        GB = sb("GB", [128, 2048], F32)
        HB = sb("HB", [128, 1024], BF16)
        JK = sb("JK", [128, 1024], BF16)
        CBF = sb("CBF", [128, 512], BF16)
        WPM = sb("WPM", [128, 512], BF16)
        PSC = sb("PSC", [128, 4], F32)
        BGT = sb("BGT", [128, 16], F32)
        ICN = sb("ICN", [128, 64], F32)
        ST = sb("ST", [128, 288], F32)
        PP = [es.enter_context(nc.psum_tensor("pp%d" % i, [128, 1024], F32)) for i in range(4)]

        sem_eng = {e: es.enter_context(nc.semaphore("se_" + e)) for e in Prog.ENGS}
        sem_dma = {}

        def bank(i):
            return PP[i // 2][:, (i % 2) * 512:(i % 2) * 512 + 512]

        rbank = [Res("bank%d" % i, excl=True) for i in range(8)]
        ident = CBF[:, 0:128]
        negtri = CBF[:, 128:256]
        ones = CBF[:, 256:384]
        maskneg = CBF[:, 384:512]

        EPSC = ST[:, 0:1]
        ONEC = ST[:, 1:2]
        st_next = [3]

        def stcol(n=1):
            c = st_next[0]
            st_next[0] += n
            assert st_next[0] <= 272
            return ST[:, c:c + n]

        def dma(eng, out, in_, key, reads=(), writes=()):
            P.add(eng, lambda e: [e.dma_start(out=out, in_=in_)], reads, writes, dma_key=key, ninst=1)

        def dma2(eng, outs_ins, key, reads=(), writes=()):
            def fn(e):
                return [e.dma_start(out=o, in_=i) for (o, i) in outs_ins]
            P.add(eng, fn, reads, writes, dma_key=key, ninst=len(outs_ins))

        def mm(out, lhsT, rhs, start, stop, reads, writes, skip=False):
            if skip:
                P.add("pe", lambda e: e.matmul(out, lhsT, rhs, start=start, stop=stop, skip_group_check=True), reads, writes)
            else:
                P.add("pe", lambda e: e.matmul(out, lhsT, rhs, start=start, stop=stop), reads, writes)

        def act(out, in_, func, reads, writes, bias=None, scale=1.0, accum=None):
            kw = {}
            if bias is not None:
                kw["bias"] = bias
            if accum is not None:
                kw["accum_out"] = accum
            o_ = P.add("act", lambda e: e.activation(out, in_, func, scale=scale, **kw), reads, writes)
            if o_ is not None and accum is not None:
                o_.multi = True

        def junk():
            return Res("junk")

        r_cbf = Res("cbf")
        r_wpm = Res("wpm")
        r_psc = Res("psc")
        r_bgt = Res("bgt")
        r_icn = Res("icn")
        r_st0 = Res("st0")
        r_gb = [Res("gb0"), Res("gb1")]
        dma("sp", CBF[:], cbf_d, "cbf", writes=[r_cbf])
        P.add("pool", lambda e: e.memset(EPSC, EPS), (), [r_st0])
        P.add("pool", lambda e: e.memset(ONEC, 1.0), (), [r_st0])
        act(ST[:, 2:3], ONEC, AF.Ln, [r_st0], [Res("warm")])
        dma("pool", WPM[:].rearrange("p (g d) -> p g d", g=4), wpm_d.rearrange("g p d -> p g d"), "wpm", writes=[r_wpm])

        r_ws = [Res("ws%d" % i) for i in range(4)]
        ws_n = [0]

        def wslot_view(s):
            return WS[:, s * 4096:(s + 1) * 4096].rearrange("p (k e) -> p k e", k=8)

        def wload(parts, reads=()):
            s = ws_n[0] % 4
            ws_n[0] += 1
            v = wslot_view(s)
            oi = []
            for (k0, src) in parts:
                kk = src.shape[1]
                oi.append((v[:, k0:k0 + kk, :], src))
            dma2("pool", oi, "ws%d" % s, reads=reads, writes=[r_ws[s]])
            return s

        def wcols(wd, c0, k=8):
            v = wd.rearrange("(k p) e -> p k e", p=128)
            return [(0, v[:, 0:k // 2, c0:c0 + 512]), (k // 2, v[:, k // 2:k, c0:c0 + 512])]

        RXb = RX[:].bitcast(BF16)
        R2b = R2[:].bitcast(BF16)
        hT = R2b.rearrange("p (h k t) -> p h k t", h=2, k=8)

        def hT_span(k, s):
            return hT[:, s // 2, k, (s % 2) * 512:(s % 2) * 512 + 512]

        def hT_tile(k, tt):
            return hT[:, tt // 8, k, (tt % 8) * 128:(tt % 8) * 128 + 128]

        r_hT = [Res("hT%d" % t) for t in range(NT)]
        r_x = [Res("x%d" % t) for t in range(NT)]

        def xt(tt):
            return RX[:, tt * 1024:(tt + 1) * 1024]

        QT = R1[:, 0:8192].rearrange("p (c t) -> p c t", c=4)
        KT = R1[:, 8192:16384].rearrange("p (c t) -> p c t", c=4)
        VV = R1[:, 16384:24576].rearrange("p (t e) -> p t e", t=16)
        PLT = R1[:, 24576:32768].rearrange("p (c t) -> p c t", c=4)
        r_QT = [[Res("QT%d_%d" % (c, s)) for s in range(4)] for c in range(4)]
        r_KT = [[Res("KT%d_%d" % (c, s)) for s in range(4)] for c in range(4)]
        r_V = [Res("V%d" % t) for t in range(NT)]
        r_PLT = [Res("PLT%d" % g) for g in range(4)]

        def uT(g):
            return RX[:, g * 2048:(g + 1) * 2048]
        r_uT = [Res("uT%d" % g) for g in range(4)]
        SA = RX[:, 8192:10248]
        SB_ = RX[:, 0:2056]
        r_SA = Res("SA")
        r_SB = Res("SB")
        YPT = RXb[:, 20608:28800].rearrange("p (c t) -> p c t", c=4)
        r_YPT = [[Res("YPT%d_%d" % (g, s)) for s in range(4)] for g in range(4)]
        OT = RXb[:, 0:8192].rearrange("p (c t) -> p c t", c=4)
        r_OT = [[Res("OT%d_%d" % (c, s)) for s in range(4)] for c in range(4)]

        ss1 = stcol(16)
        ln1 = stcol(16)
        rs1 = stcol(16)
        r_ss1 = [Res("ss1_%d" % t) for t in range(NT)]
        r_hb = [Res("hb0"), Res("hb1")]

        HBS = [HB[:, 0:1024], JK[:, 0:1024]]

        def norm_b(tt, xin, r_xin, gb_ap, r_gbx, ssc, lnc, rsc, r_stat, hbs=None, rhbs=None):
            hbs = hbs or HBS
            rhbs = rhbs or r_hb
            hb = hbs[tt % len(hbs)]
            rhb = rhbs[tt % len(hbs)]
            rxl = list(r_xin) if isinstance(r_xin, list) else [r_xin]
            act(hb, xin, AF.Square, rxl, [rhb, r_stat], accum=ssc)
            act(lnc, ssc, AF.Ln, [r_stat, r_st0], [r_stat], bias=EPS, scale=1.0 / D)
            act(rsc, lnc, AF.Exp, [r_stat], [r_stat], scale=-0.5)
            P.add("dve", lambda e: e.scalar_tensor_tensor(out=hb, in0=xin, scalar=rsc, in1=gb_ap,
                                                          op0=ALU.mult, op1=ALU.mult),
                  rxl + [r_stat, r_gbx], [rhb])

        def norm_c(tt, dst_res, b, hbs=None, rhbs=None, evac="dve"):
            hbs = hbs or HBS
            rhbs = rhbs or r_hb
            hb = hbs[tt % len(hbs)]
            rhb = rhbs[tt % len(hbs)]
            pb = bank(b).bitcast(BF16)
            for k in range(8):
                o = pb[:, k * 128:(k + 1) * 128]
                i_ = hb[:, k * 128:(k + 1) * 128]
                P.add("pe", (lambda o=o, i_=i_: (lambda e: e.transpose(o, i_, ident)))(),
                      [rhb, r_cbf], [rbank[b]])
            dst = hT[:, tt // 8, :, (tt % 8) * 128:(tt % 8) * 128 + 128]
            src = pb.rearrange("p (k t) -> p k t", k=8)
            if evac == "dve":
                P.add("dve", lambda e: e.tensor_copy(dst, src), [rbank[b]], [dst_res])
            elif evac == "act":
                act(dst, src, AF.Copy, [rbank[b]], [dst_res])

        def norm_evac(tt, dst_res, b):
            pb = bank(b).bitcast(BF16)
            dst = hT[:, tt // 8, :, (tt % 8) * 128:(tt % 8) * 128 + 128]
            src = pb.rearrange("p (k t) -> p k t", k=8)
            act(dst, src, AF.Copy, [rbank[b]], [dst_res])

        for tt in range(NT):
            dma("sp", xt(tt), x_d[tt * 128:(tt + 1) * 128, :], "x%d" % tt, writes=[r_x[tt]])
            if tt == 0:
                dma("sp", GB[:, 0:1024], gains_d[0], "gb0", writes=[r_gb[0]])
        dma("sp", PSC[:], pscale_d, "psc", writes=[r_psc])
        dma("sp", BGT[:], bgate_d, "bgt", writes=[r_bgt])
        dma("sp", ICN[:], invcnt_d, "icn", writes=[r_icn])
        for i in range(NT + 1):
            if i < NT:
                norm_b(i, xt(i), r_x[i], GB[:, 0:1024], r_gb[0],
                       ss1[:, i:i + 1], ln1[:, i:i + 1], rs1[:, i:i + 1], r_ss1[i])
            if i >= 1:
                norm_c(i - 1, r_hT[i - 1], (i - 1) % 8)

        dma("sp", GB[:, 0:1024], gains_d[1], "gb0", writes=[r_gb[0]])
        dma("sp", GB[:, 1024:2048], gains_d[2], "gb1", writes=[r_gb[1]])
        if stop_after == "P1":
            P.frozen = True
        bk = [0]

        def nextbank():
            b = bk[0] % 8
            bk[0] += 1
            return b

        for g in range(4):
            r_uT[g].pending = list(r_x)
        sl = [wload(wcols(w_in_d, c * 512), reads=([] if c == 0 else [r_x[NT - 1]])) for c in range(4)]

        def proj_fm(c, cc, s, b, ev, lazy=False):
            wv = wslot_view(sl[c])
            ops = []
            for k in range(8):
                ops.append((lambda k=k: mm(bank(b), wv[:, k, cc * 128:(cc + 1) * 128], hT_span(k, s), k == 0, k == 7,
                                           [r_ws[sl[c]]] + r_hT[4 * s:4 * s + 4], [rbank[b]])))
            if c == 0:
                dst, rd, sc = uT(cc)[:, s * 512:(s + 1) * 512], r_uT[cc], 1.0
            elif c == 1:
                dst, rd, sc = QT[:, cc, s * 512:(s + 1) * 512], r_QT[cc][s], 0.125
            else:
                dst, rd, sc = KT[:, cc, s * 512:(s + 1) * 512], r_KT[cc][s], 1.0

            def evac():
                if ev == "act":
                    act(dst, bank(b), AF.Copy, [rbank[b]], [rd], scale=sc)
                elif sc != 1.0:
                    P.add("dve", lambda e: e.tensor_scalar(out=dst, in0=bank(b), scalar1=sc, scalar2=None, op0=ALU.mult),
                          [rbank[b]], [rd])
                else:
                    P.add("dve", lambda e: e.tensor_copy(dst, bank(b)), [rbank[b]], [rd])
            ops.append(evac)
            if lazy:
                return ops
            for o in ops:
                o()

        def proj_v(tt, b, ev, lazy=False):
            wv = wslot_view(sl[3])
            ops = []
            for k in range(8):
                ops.append((lambda k=k: mm(bank(b), hT_tile(k, tt), wv[:, k, :], k == 0, k == 7,
                                           [r_ws[sl[3]], r_hT[tt]], [rbank[b]])))
            dst = VV[:, tt, :]

            def evac():
                if ev == "act":
                    act(dst, bank(b), AF.Copy, [rbank[b]], [r_V[tt]])
                else:
                    P.add("dve", lambda e: e.tensor_copy(dst, bank(b)), [rbank[b]], [r_V[tt]])
            ops.append(evac)
            if lazy:
                return ops
            for o in ops:
                o()

        for s in range(4):
            for cc in range(4):
                proj_fm(0, cc, s, nextbank(), "act")
        NPRE = 2
        for s in range(NPRE):
            for c in (1, 2):
                for cc in range(4):
                    proj_fm(c, cc, s, nextbank(), "act")
            for tt in range(4 * s, 4 * s + 4):
                proj_v(tt, nextbank(), "act")
        deferred = {}
        for s in range(NPRE, 4):
            micro = []
            for c in (1, 2):
                for cc in range(4):
                    micro += proj_fm(c, cc, s, 7, "dve", lazy=True)
            for tt in range(4 * s, 4 * s + 4):
                micro += proj_v(tt, 7, "dve", lazy=True)
            deferred[s] = micro

        if stop_after == "P2":
            P.frozen = True
        def dve_tt(out, a, b_, op, reads, writes):
            P.add("dve", lambda e: e.tensor_tensor(out=out, in0=a, in1=b_, op=op), reads, writes)

        P.add("dve", lambda e: e.memset(SA[:, 0:8], 0.0), (), [r_SA])
        WIN = (2, 4, 8, 16)
        r_tmp = Res("tmp16")
        for g in range(4):
            u = uT(g)
            dve_tt(SA[:, 9:2056], u[:, 1:2048], u[:, 0:2047], ALU.add, [r_uT[g]], [r_SA])
            P.add("dve", (lambda u=u: (lambda e: e.tensor_copy(SA[:, 8:9], u[:, 0:1])))(), [r_uT[g]], [r_SA])
            cur, rcur = SA, r_SA
            if g == 1:
                S4 = RX[:, 0:2048]
                dve_tt(S4, SA[:, 8:2056], SA[:, 6:2054], ALU.add, [r_SA], [r_SB, r_uT[0]])
                cur, rcur = None, r_SB
                cv1 = S4
            if g >= 2:
                if g == 2:
                    P.add("dve", lambda e: e.memset(SB_[:, 0:8], 0.0), (), [r_SB, r_uT[0], r_uT[1]])
                dve_tt(SB_[:, 8:2056], SA[:, 8:2056], SA[:, 6:2054], ALU.add, [r_SA], [r_SB, r_uT[0], r_uT[1]])
                cur, rcur = SB_, r_SB
            if g >= 2:
                dve_tt(SA[:, 8:2056], SB_[:, 8:2056], SB_[:, 4:2052], ALU.add, [r_SB], [r_SA])
                cur, rcur = SA, r_SA
            if g >= 3:
                dve_tt(SB_[:, 8:2056], SA[:, 8:2056], SA[:, 0:2048], ALU.add, [r_SA], [r_SB])
                cur, rcur = SB_, r_SB
            w = WIN[g]
            cv = cv1 if g == 1 else cur[:, 8:2056]
            P.add("dve", (lambda cv=cv, u=u, g=g, w=w: (lambda e: e.scalar_tensor_tensor(
                out=PLT[:, g, :], in0=cv, scalar=1.0 / w, in1=u, op0=ALU.mult, op1=ALU.subtract)))(),
                [rcur, r_uT[g]], [r_PLT[g]])
            tmp = ST[:, 272:288]
            P.add("dve", (lambda cv=cv, g=g, tmp=tmp: (lambda e: e.tensor_tensor(
                out=tmp, in0=cv[:, 0:16], in1=ICN[:, g * 16:(g + 1) * 16], op=ALU.mult)))(),
                [rcur, r_icn], [r_tmp])
            P.add("dve", (lambda u=u, g=g, tmp=tmp: (lambda e: e.tensor_tensor(
                out=PLT[:, g, 0:16], in0=tmp, in1=u[:, 0:16], op=ALU.subtract)))(),
                [r_tmp, r_uT[g]], [r_PLT[g]])
        for g in range(4):
            for s in range(4):
                b = nextbank()
                mm(bank(b), WPM[:, g * 128:(g + 1) * 128], PLT[:, g, s * 512:(s + 1) * 512], True, True,
                   [r_wpm, r_PLT[g]], [rbank[b]])
                act(YPT[:, g, s * 512:(s + 1) * 512], bank(b), AF.Copy, [rbank[b], r_psc], [r_YPT[g][s]],
                    scale=PSC[:, g:g + 1])

        r_hs = [Res("hs%d" % i) for i in range(8)]
        for i in range(8):
            r_hs[i].pending = [r_ws[i // 2]]
        hs_n = [0]

        def hslot_view(i):
            return WS[:, i * 2048:(i + 1) * 2048].rearrange("p (k e) -> p k e", k=8)

        def hload(parts):
            i = hs_n[0] % 8
            hs_n[0] += 1
            v = hslot_view(i)
            oi = []
            for (k0, src) in parts:
                oi.append((v[:, k0:k0 + src.shape[1], :], src))
            dma2("pool", oi, "hs%d" % i, writes=[r_hs[i]])
            return i

        def p5_load(gq, which):
            c0 = gq * 256
            if which == "a":
                v = w_gate_d.rearrange("(k p) e -> p k e", p=128)
                return hload([(0, v[:, 0:4, c0:c0 + 256]), (4, v[:, 4:8, c0:c0 + 256])])
            if which == "b":
                v = w_gate_d.rearrange("(k p) e -> p k e", p=128)
                return hload([(0, v[:, 0:4, 1024 + c0:1024 + c0 + 256]), (4, v[:, 4:8, 1024 + c0:1024 + c0 + 256])])
            vp = w_brp_d.rearrange("(k p) e -> p k e", p=128)
            vs = w_brs_d.rearrange("(k p) e -> p k e", p=128)
            return hload([(0, vp[:, :, c0:c0 + 256]), (4, vs[:, :, c0:c0 + 256])])
        p5hs = {}

        def p5_prefetch():
            for gq in (0, 1):
                for wh in "abc":
                    p5hs[(gq, wh)] = p5_load(gq, wh)
            p5hs[(2, "a")] = p5_load(2, "a")
            p5hs[(2, "b")] = p5_load(2, "b")
        p5w0 = [None]

        if stop_after == "P3":
            P.frozen = True
        def f32v(off):
            return RX[:, off:off + 1024].rearrange("p (h n) -> p h n", h=2)

        def bf16v(off32):
            return RXb[:, 2 * off32:2 * off32 + 1024].rearrange("p (h n) -> p h n", h=2)
        E_ = [f32v(4096), f32v(5120)]
        ARG = [f32v(6144), f32v(7168)]
        R1f_a = R1[:].bitcast(F32)
        CCS = [f32v(8192), R1f_a[:, 12288:13312].rearrange("p (h n) -> p h n", h=2)]
        SP_ = [bf16v(9216), bf16v(9728)]
        ATT = [bf16v(14400), bf16v(14912)]
        r_E = [Res("E0"), Res("E1")]
        r_ARG = [Res("ARG0"), Res("ARG1")]
        r_CCS = [Res("CC0"), Res("CC1")]
        r_CC = r_CCS[0]
        r_CCS[1].pending = list(r_PLT)
        r_SP = [Res("SP0"), Res("SP1")]
        r_ATT = [Res("ATT0"), Res("ATT1")]
        old = r_uT + [r_SA, r_SB]
        for r in r_E + r_ARG + [r_CC] + r_SP:
            r.pending = list(old)
        for c in range(4):
            for s in range(4):
                r_OT[c][s].pending = list(old)

        ZP = [PP[0], PP[1]]
        r_Z = [[rbank[0], rbank[1]], [rbank[2], rbank[3]]]
        TP = PP[2]
        r_T = [rbank[4], rbank[5]]
        OACC = [bank(6), bank(6)]
        r_OACC = [rbank[6], rbank[6]]

        chains = []
        chain_id = 0
        for j in range(4):
            for hp in range(4):
                kbs = [4 * j + 3, 4 * j + 2, 4 * j + 1, 4 * j] + list(range(4 * j - 1, -1, -1))
                ch = []
                for i, kb in enumerate(kbs):
                    c0 = 128 * (kb - 4 * j) if kb >= 4 * j else 0
                    ch.append(dict(j=j, hp=hp, kb=kb, c0=c0, first=(i == 0), last=(i == len(kbs) - 1),
                                   diag=(kb >= 4 * j), chain=chain_id))
                chains.append(ch)
                chain_id += 1
        tiles = []
        for ch in chains:
            tiles.extend(ch)
        sched = {}
        for s_ in range(NPRE, 4):
            lo = 0 if s_ == NPRE else min(n for n, t in enumerate(tiles) if t["j"] == s_ - 1)
            hi = min(n for n, t in enumerate(tiles) if t["j"] == s_)
            micro = deferred[s_]
            for g, op_ in enumerate(micro):
                n = lo + (g * (hi - lo)) // len(micro)
                sched.setdefault(n, []).append(op_)
        last_sched = max(sched)
        NTL = len(tiles)

        def pv(t3, c0):
            return t3[:, :, c0:512]

        def z3(zb):
            return ZP[zb][:].rearrange("p (h n) -> p h n", h=2)

        def emit_qk(n):
            t = tiles[n]
            zb = n % 2
            j, hp, kb, c0 = t["j"], t["hp"], t["kb"], t["c0"]
            for hd in range(2):
                rows = slice(64 * hd, 64 * hd + 64)
                out = ZP[zb][:, hd * 512 + c0:hd * 512 + 512]
                lhsT = KT[rows, hp, kb * 128:(kb + 1) * 128]
                rhs = QT[rows, hp, j * 512 + c0:(j + 1) * 512]
                mm(out, lhsT, rhs, True, not t["diag"], [r_KT[hp][kb // 4], r_QT[hp][j]], [r_Z[zb][hd]])
            if t["diag"]:
                for hd in range(2):
                    out = ZP[zb][:, hd * 512 + c0:hd * 512 + c0 + 128]
                    mm(out, ident, maskneg, False, True, [r_cbf], [r_Z[zb][hd]])

        def emit_exp1(n):
            t = tiles[n]
            zb = n % 2
            act(pv(E_[zb], t["c0"]), pv(z3(zb), t["c0"]), AF.Exp, r_Z[zb], [r_E[zb]])

        def emit_ln(n):
            t = tiles[n]
            zb = n % 2
            act(pv(SP_[zb], t["c0"]), pv(E_[zb], t["c0"]), AF.Ln, [r_E[zb], r_st0], [r_SP[zb]], bias=1.0)

        def emit_tri(n):
            t = tiles[n]
            zb = n % 2
            c0 = t["c0"]
            for hd in range(2):
                out = ZP[zb][:, hd * 512 + c0:hd * 512 + 512]
                mm(out, negtri, SP_[zb][:, hd, c0:512], False, True, [r_cbf, r_SP[zb]], [r_Z[zb][hd]], skip=True)
            for hd in range(2):
                out = TP[:, hd * 512 + c0:hd * 512 + 512]
                mm(out, ones, SP_[zb][:, hd, c0:512], True, True, [r_cbf, r_SP[zb]], [r_T[hd]])

        def emit_dve(n):
            t = tiles[n]
            zb = n % 2
            c0 = t["c0"]
            CC = CCS[t["chain"] % 2]
            rcc = r_CCS[t["chain"] % 2]
            if t["first"]:
                P.add("pool", (lambda CC=CC: (lambda e: e.memset(CC, 0.0)))(), (), [rcc])
            a = pv(ARG[zb], c0)
            z = pv(z3(zb), c0)
            c = pv(CC, c0)
            tp = pv(TP[:].rearrange("p (h n) -> p h n", h=2), c0)
            dve_tt(a, z, c, ALU.subtract, r_Z[zb] + [rcc], [r_ARG[zb]])
            if not t["last"]:
                dve_tt(c, tp, c, ALU.add, r_T + [rcc], [rcc])

        def emit_exp2(n):
            t = tiles[n]
            zb = n % 2
            act(pv(ATT[zb], t["c0"]), pv(ARG[zb], t["c0"]), AF.Exp, [r_ARG[zb]], [r_ATT[zb]])

        def emit_av(n):
            t = tiles[n]
            zb = n % 2
            c0, hp, kb, j = t["c0"], t["hp"], t["kb"], t["j"]
            ob = t["chain"] % 2
            for hd in range(2):
                out = OACC[ob][64 * hd:64 * hd + 64, c0:512]
                lhsT = VV[:, kb, (2 * hp + hd) * 64:(2 * hp + hd) * 64 + 64]
                mm(out, lhsT, ATT[zb][:, hd, c0:512], t["first"], t["last"], [r_V[kb], r_ATT[zb]], [r_OACC[ob]], skip=True)
            if t["last"]:
                dst = OT[:, hp, j * 512:(j + 1) * 512]
                src = OACC[ob]
                P.add("dve", lambda e: e.tensor_copy(dst, src), [r_OACC[ob]], [r_OT[hp][j]])

        emit_qk(0)
        for n in range(NTL + 2):
            if n + 1 < NTL:
                emit_qk(n + 1)
            if n < NTL:
                emit_exp1(n)
            if 2 <= n:
                emit_exp2(n - 2)
            if n < NTL:
                emit_ln(n)
                emit_tri(n)
                emit_dve(n)
            if 2 <= n:
                emit_av(n - 2)
            for op_ in sched.get(n, []):
                op_()
            if n == last_sched:
                p5_prefetch()

        if stop_after == "P4":
            P.frozen = True
        NXS = 6
        R1f = R1[:].bitcast(F32)
        XS = [R1f[:, 8192 + j * 1024:8192 + (j + 1) * 1024] for j in range(NXS)]
        r_xs = [Res("xs%d" % j) for j in range(NXS)]
        for j in range(NXS):
            r_xs[j].pending = r_V + r_PLT + [r_CCS[1]]
            dma("sp", XS[j], x_d[j * 128:(j + 1) * 128, :], "xs%d" % j, writes=[r_xs[j]])
        MT = R1[:, 0:16384].rearrange("p (c t) -> p c t", c=8)
        r_MT = [[Res("MT%d_%d" % (c, s)) for s in range(4)] for c in range(8)]
        oldq = [r for row in r_QT for r in row] + [r for row in r_KT for r in row]
        for c in range(8):
            for s in range(4):
                r_MT[c][s].pending = list(oldq)
        TM = [[RX[:, 4096 + (i * 4 + q) * 512:4096 + (i * 4 + q + 1) * 512] for q in range(4)] for i in range(2)]
        r_TM = [[Res("TM%d_%d" % (i, q)) for q in range(4)] for i in range(2)]
        olda = r_E + r_ARG + [r_CC] + r_SP
        for i in range(2):
            for q in range(4):
                r_TM[i][q].pending = list(olda)
        it = 0
        so = [None, None]
        for gq in range(4):
            ha, hb_, hc = p5hs[(gq, "a")], p5hs[(gq, "b")], p5hs[(gq, "c")]
            wa, wb, wc = hslot_view(ha), hslot_view(hb_), hslot_view(hc)
            for dcl in range(2):
                dc = 2 * gq + dcl
                cs = slice(dcl * 128, dcl * 128 + 128)
                for s in range(4):
                    pb = 4 * (it % 2)
                    ti = it % 2
                    it += 1
                    bgp, bgs, byp, bys = pb, pb + 1, pb + 2, pb + 3
                    for k in range(8):
                        mm(bank(bgp), wa[:, k, cs], hT_span(k, s), k == 0, k == 7,
                           [r_hs[ha]] + r_hT[4 * s:4 * s + 4], [rbank[bgp]])
                    for k in range(8):
                        mm(bank(bgs), wb[:, k, cs], hT_span(k, s), k == 0, k == 7,
                           [r_hs[hb_]] + r_hT[4 * s:4 * s + 4], [rbank[bgs]])
                    for k in range(4):
                        mm(bank(byp), wc[:, k, cs], YPT[:, k, s * 512:(s + 1) * 512], k == 0, k == 3,
                           [r_hs[hc], r_YPT[k][s]], [rbank[byp]])
                    for k in range(4):
                        mm(bank(bys), wc[:, 4 + k, cs], OT[:, k, s * 512:(s + 1) * 512], k == 0, k == 3,
                           [r_hs[hc], r_OT[k][s]], [rbank[bys]])
                    act(TM[ti][0], bank(bgp), AF.Sigmoid, [rbank[bgp], r_bgt], [r_TM[ti][0]], bias=BGT[:, dc:dc + 1])
                    act(TM[ti][1], bank(bgs), AF.Sigmoid, [rbank[bgs], r_bgt], [r_TM[ti][1]], bias=BGT[:, 8 + dc:9 + dc])
                    dve_tt(TM[ti][2], bank(byp), TM[ti][0], ALU.mult, [rbank[byp], r_TM[ti][0]], [r_TM[ti][2]])
                    dve_tt(TM[ti][3], bank(bys), TM[ti][1], ALU.mult, [rbank[bys], r_TM[ti][1]], [r_TM[ti][3]])
                    dve_tt(MT[:, dc, s * 512:(s + 1) * 512], TM[ti][2], TM[ti][3], ALU.add,
                           [r_TM[ti][2], r_TM[ti][3]], [r_MT[dc][s]])
            if gq == 0:
                p5hs[(2, "c")] = p5_load(2, "c")
                p5hs[(3, "a")] = p5_load(3, "a")
                p5hs[(3, "b")] = p5_load(3, "b")
            elif gq == 1:
                p5hs[(3, "c")] = p5_load(3, "c")
                assert hs_n[0] == 12
                for s_ in range(4):
                    r_ws[s_].pending = [r_hs[2 * s_], r_hs[2 * s_ + 1]]
                ws_n[0] = 6
                so[0] = wload(wcols(w_out_d, 0))
            elif gq == 2:
                so[1] = wload(wcols(w_out_d, 512))

        if stop_after == "P5":
            P.frozen = True
        oldrx = [r for row in r_YPT for r in row] + [r for row in r_OT for r in row] + \
                [r for row in r_TM for r in row] + r_ATT
        r_x1h = [[Res("x1_%d_0" % t), Res("x1_%d_1" % t)] for t in range(NT)]
        for tt in range(NT):
            r_x1h[tt][0].pending = list(oldrx)
            r_x1h[tt][1].pending = list(oldrx)
            if tt >= NXS:
                dma("sp", xt(tt), x_d[tt * 128:(tt + 1) * 128, :], "x%d" % tt, reads=[r_ws[so[1]]], writes=r_x1h[tt])
        r_h2T = [Res("h2T%d" % t) for t in range(NT)]
        for tt in range(NT):
            r_h2T[tt].pending = list(r_hT)
        ssA = stcol(32)
        ssS = stcol(16)
        lnA = stcol(16)
        rsA = stcol(16)
        ssB = stcol(16)
        lnB = stcol(16)
        rsB = stcol(16)
        r_stA = [Res("stA%d" % t) for t in range(NT)]
        r_stB = [Res("stB%d" % t) for t in range(NT)]
        HBS4 = HBS + [R1[:, 28672:29696], R1[:, 29696:30720]]
        r_hb4 = r_hb + [Res("hb2"), Res("hb3")]
        JK6 = R1[:, 30720:31232]
        r_jk6 = Res("jk6")
        for r in r_hb4[2:] + [r_jk6]:
            r.pending = r_V + r_PLT + [r_CCS[1]]

        JK6w = R1[:, 30720:31744]

        def p6_mm(T):
            b0 = 2 * (T % 3)
            for dh in range(2):
                wv = wslot_view(so[dh])
                for k in range(8):
                    mm(bank(b0 + dh), MT[:, k, T * 128:(T + 1) * 128], wv[:, k, :], k == 0, k == 7,
                       [r_ws[so[dh]], r_MT[k][T // 4]], [rbank[b0 + dh]])

        def p6_sqA(T):
            b0 = 2 * (T % 3)
            act(JK6w, PP[T % 3][:], AF.Square, [rbank[b0], rbank[b0 + 1]], [r_jk6, r_stA[T]], accum=ssS[:, T:T + 1])

        def p6_lnA(T):
            act(lnA[:, T:T + 1], ssS[:, T:T + 1], AF.Ln, [r_stA[T], r_st0], [r_stA[T]], bias=EPS, scale=1.0 / D)

        def p6_expA(T):
            act(rsA[:, T:T + 1], lnA[:, T:T + 1], AF.Exp, [r_stA[T]], [r_stA[T]], scale=-0.5)

        def p6_res(T):
            b0 = 2 * (T % 3)
            for dh in range(2):
                P.add("dve", (lambda b=b0 + dh, T=T, dh=dh: (lambda e: e.scalar_tensor_tensor(
                    out=bank(b), in0=bank(b), scalar=rsA[:, T:T + 1], in1=GB[:, dh * 512:(dh + 1) * 512],
                    op0=ALU.mult, op1=ALU.mult)))(),
                    [rbank[b0 + dh], r_stA[T], r_gb[0]], [rbank[b0 + dh]])
            for dh in range(2):
                xo = xt(T)[:, dh * 512:(dh + 1) * 512]
                if T < NXS:
                    xi = XS[T][:, dh * 512:(dh + 1) * 512]
                    dve_tt(xo, bank(b0 + dh), xi, ALU.add, [rbank[b0 + dh], r_xs[T]], [r_x1h[T][dh]])
                else:
                    dve_tt(xo, bank(b0 + dh), xo, ALU.add, [rbank[b0 + dh], r_x1h[T][dh]], [r_x1h[T][dh]])

        def p6_sqB(T):
            act(HBS4[T % 4], xt(T), AF.Square, r_x1h[T], [r_hb4[T % 4], r_stB[T]], accum=ssB[:, T:T + 1])

        def p6_lnB(T):
            act(lnB[:, T:T + 1], ssB[:, T:T + 1], AF.Ln, [r_stB[T], r_st0], [r_stB[T]], bias=EPS, scale=1.0 / D)

        def p6_expB(T):
            act(rsB[:, T:T + 1], lnB[:, T:T + 1], AF.Exp, [r_stB[T]], [r_stB[T]], scale=-0.5)

        def p6_h(T):
            hb = HBS4[T % 4]
            P.add("dve", (lambda hb=hb, T=T: (lambda e: e.scalar_tensor_tensor(
                out=hb, in0=xt(T), scalar=rsB[:, T:T + 1], in1=GB[:, 1024:2048], op0=ALU.mult, op1=ALU.mult)))(),
                r_x1h[T] + [r_stB[T], r_gb[1]], [r_hb4[T % 4]])

        def ok(T):
            return 0 <= T < NT

        for i in range(NT + 6):
            if ok(i - 5):
                norm_c(i - 5, r_h2T[i - 5], 6 + (i - 5) % 2, HBS4, r_hb4, evac="none")
            if ok(i):
                p6_mm(i)
            if ok(i - 1):
                p6_sqA(i - 1)
            if ok(i - 3):
                p6_sqB(i - 3)
            if ok(i - 1):
                p6_lnA(i - 1)
            if ok(i - 3):
                p6_lnB(i - 3)
            if ok(i - 1):
                p6_expA(i - 1)
            if ok(i - 3):
                p6_expB(i - 3)
            if ok(i - 6):
                norm_evac(i - 6, r_h2T[i - 6], 6 + (i - 6) % 2)
            if ok(i - 2):
                p6_res(i - 2)
            if ok(i - 4):
                p6_h(i - 4)

        if stop_after == "P6":
            P.frozen = True
        AT = R1[:].rearrange("p (c t) -> p c t", c=32)
        r_AT = [[Res("AT%d_%d" % (c, s)) for s in range(2)] for c in range(32)]
        oldm = [r for row in r_MT for r in row] + r_V + r_PLT + r_xs + r_hb4[2:] + [r_jk6, r_CCS[1]]
        dma("sp", GB[:, 0:1024], gains_d[3], "gb0", writes=[r_gb[0]])
        FFS = [R2[:, h * 4096:(h + 1) * 4096].rearrange("p (t e) -> p t e", t=8) for h in range(2)]
        ssC = stcol(32)
        ssT = stcol(16)
        lnC = stcol(16)
        rsC = stcol(16)
        r_stC = [Res("stC%d" % t) for t in range(NT)]
        r_rt = [Res("rt0"), Res("rt1")]
        for r in r_rt:
            r.pending = [r_gb[1]]
        for half in range(2):
            for c in range(32):
                for s in range(2):
                    r_AT[c][s].pending = list(oldm)
            oldm = []
            for f4 in range(8):
                su = wload(wcols(w_up_d, f4 * 512))
                wv = wslot_view(su)
                for fl in range(4):
                    fc = 4 * f4 + fl
                    for s in range(2):
                        b = nextbank()
                        for k in range(8):
                            mm(bank(b), wv[:, k, fl * 128:(fl + 1) * 128],
                               hT[:, half, k, s * 512:(s + 1) * 512], k == 0, k == 7,
                               [r_ws[su]] + r_h2T[8 * half + 4 * s:8 * half + 4 * s + 4], [rbank[b]])
                        dst = AT[:, fc, s * 512:(s + 1) * 512]
                        ti = b % 2
                        tmpr = GB[:, 1024 + ti * 512:1024 + (ti + 1) * 512]
                        act(tmpr, bank(b), AF.Relu, [rbank[b]], [r_rt[ti]])
                        dve_tt(dst, bank(b), tmpr, ALU.mult, [rbank[b], r_rt[ti]], [r_AT[fc][s]])
            if stop_after == "P7a":
                P.frozen = True
            r_ffs = [Res("ffs%d_%d" % (half, t)) for t in range(8)]
            for t in range(8):
                r_ffs[t].pending = list(r_h2T[8 * half:8 * half + 8])
            for dh in range(2):
                for f8 in range(4):
                    vd = w_down_d.rearrange("(c p) d -> p c d", p=128)
                    sd = wload([(0, vd[:, 8 * f8:8 * f8 + 4, dh * 512:(dh + 1) * 512]),
                                (4, vd[:, 8 * f8 + 4:8 * f8 + 8, dh * 512:(dh + 1) * 512])])
                    wv = wslot_view(sd)
                    if f8 == 3:
                        for t in range(8):
                            for fl in range(8):
                                fc = 8 * f8 + fl
                                mm(bank(t), AT[:, fc, t * 128:(t + 1) * 128], wv[:, fl, :], fc == 0, fc == 31,
                                   [r_ws[sd], r_AT[fc][t // 4]], [rbank[t]])
                    else:
                        for fl in range(8):
                            fc = 8 * f8 + fl
                            for t in range(8):
                                mm(bank(t), AT[:, fc, t * 128:(t + 1) * 128], wv[:, fl, :], fc == 0, fc == 31,
                                   [r_ws[sd], r_AT[fc][t // 4]], [rbank[t]])
                if stop_after == "P7b":
                    P.frozen = True
                def ev_stats(t, half=half, dh=dh):
                    tt = 8 * half + t
                    act(HBS[t % 2][:, 0:512], bank(t), AF.Square, [rbank[t]], [r_hb[t % 2], r_stC[tt]],
                        accum=ssC[:, 2 * tt + dh:2 * tt + dh + 1])
                    if dh == 0:
                        P.add("dve", (lambda t=t, half=half: (lambda e: e.tensor_copy(FFS[half][:, t, :], bank(t))))(),
                              [rbank[t]], [r_ffs[t]])
                    else:
                        dve_tt(ssT[:, tt:tt + 1], ssC[:, 2 * tt:2 * tt + 1], ssC[:, 2 * tt + 1:2 * tt + 2], ALU.add,
                               [r_stC[tt]], [r_stC[tt]])
                        act(lnC[:, tt:tt + 1], ssT[:, tt:tt + 1], AF.Ln, [r_stC[tt], r_st0], [r_stC[tt]],
                            bias=EPS, scale=1.0 / D)
                        act(rsC[:, tt:tt + 1], lnC[:, tt:tt + 1], AF.Exp, [r_stC[tt]], [r_stC[tt]], scale=-0.5)

                def ev_big(t, half=half):
                    tt = 8 * half + t
                    for d2 in range(2):
                        src = FFS[half][:, t, :] if d2 == 0 else bank(t)
                        rsrc = r_ffs[t] if d2 == 0 else rbank[t]
                        P.add("dve", (lambda src=src, tt=tt, d2=d2: (lambda e: e.scalar_tensor_tensor(
                            out=src, in0=src, scalar=rsC[:, tt:tt + 1], in1=GB[:, d2 * 512:(d2 + 1) * 512],
                            op0=ALU.mult, op1=ALU.mult)))(),
                            [rsrc, r_stC[tt], r_gb[0]], [rsrc])
                    for d2 in range(2):
                        src = FFS[half][:, t, :] if d2 == 0 else bank(t)
                        rsrc = r_ffs[t] if d2 == 0 else rbank[t]
                        xs = xt(tt)[:, d2 * 512:(d2 + 1) * 512]
                        P.add("dve", (lambda xs=xs, src=src: (lambda e: e.tensor_tensor(out=xs, in0=src, in1=xs, op=ALU.add)))(),
                              [rsrc, r_x1h[tt][d2]], [r_x1h[tt][d2]])
                    if stop_after != "P7d":
                        dma("sp", out_d[tt * 128:(tt + 1) * 128, :], xt(tt), "st%d" % tt, reads=r_x1h[tt])

                for t in range(9):
                    if t < 8:
                        ev_stats(t)
                    if dh == 1 and t >= 1:
                        ev_big(t - 1)
                if stop_after == "P7c" and dh == 0:
                    P.frozen = True

        P.frozen = False
        if debug:
            dbg_src = {
                "hT": (R2b, []), "R1": (R1[:], []), "RX": (RX[:], []),
            }
            for name, _, _ in debug:
                src, _r = dbg_src[name]
                P.add("sp", (lambda src=src, name=name: (lambda e: [e.dma_start(out=dbg_d[name], in_=src)]))(),
                      [], [], dma_key="dbg_" + name, ninst=1, barrier=True)

        P.finalize()
        P.plan_waits()
        for key in P.dma_cnt:
            sem_dma[key] = es.enter_context(nc.semaphore("sd_" + key))
        final_waits = [(k, v) for k, v in P.dma_cnt.items() if k.startswith("st") or k.startswith("dbg_")]

        def emit_engine(name, e):
            waited = {}
            for op in P.ops:
                if op.eng != name:
                    continue
                todo = [(sem_dma[key[1]] if key[0] == "d" else sem_eng[key[1]], val) for key, val in op.waits]
                embed = None
                if todo and op.dma_key is None and name in ("act", "dve", "pool", "pe") and not op.multi:
                    embed = todo.pop()
                for sem, val in todo:
                    e.wait_ge(sem, val)
                res = op.fn(e)
                if embed is not None:
                    res._wait_ge(embed[0], embed[1])
                if op.dma_key is not None:
                    for inst in res:
                        inst.then_inc(sem_dma[op.dma_key], 16)
                elif op.signal:
                    res.then_inc(sem_eng[name], 1)
            if name == "sp":
                if debug:
                    for en in ("pe", "act", "dve", "pool"):
                        pass
                for k, v in final_waits:
                    e.wait_ge(sem_dma[k], v)

        with nc.Block() as block:
            @block.tensor
            def _(e):
                emit_engine("pe", e)

            @block.scalar
            def _(e):
                emit_engine("act", e)

            @block.vector
            def _(e):
                emit_engine("dve", e)

            @block.gpsimd
            def _(e):
                emit_engine("pool", e)

            @block.sync
            def _(e):
                emit_engine("sp", e)
    return nc


_NC_CACHE = {}


def _consts():
    bf = ml_dtypes.bfloat16
    p = np.arange(128)[:, None]
    c = np.arange(128)[None, :]
    ident = (p == c).astype(np.float32)
    negtri = -(p >= c).astype(np.float32)
    ones = np.ones((128, 128), np.float32)
    maskneg = np.where(p < c, 0.0, NEG).astype(np.float32)
    cbf = np.concatenate([ident, negtri, ones, maskneg], axis=1).astype(bf)
    invcnt = np.zeros((128, 64), np.float32)
    for g, w in enumerate((2, 4, 8, 16)):
        t = np.arange(16)
        invcnt[:, g * 16:(g + 1) * 16] = (1.0 / np.minimum(t + 1, w))[None, :]
    return cbf, invcnt


def kernel(x, g_pre_mix, w_in, w_pool_mix, pool_scale, w_br_pool, w_br_sb, w_gate, b_gate,
           w_out, g_post_mix, g_pre_mlp, w_up, w_down, g_post_mlp, _debug=None, _stop=None):
    f = lambda a: np.ascontiguousarray(np.asarray(a, dtype=np.float32))
    x = f(x)
    B = x.shape[0]
    key = "dbg" if _debug else "main"
    if key not in _NC_CACHE:
        _NC_CACHE[key] = build_nc(_debug, _stop)
    nc = _NC_CACHE[key]
    cbf, invcnt = _consts()
    gains = np.stack([np.broadcast_to(f(g)[None, :], (128, D)) for g in
                      (g_pre_mix, g_post_mix, g_pre_mlp, g_post_mlp)]).copy()
    pscale = np.ascontiguousarray(f(pool_scale).reshape(4, 128).T)
    bgate = np.ascontiguousarray(f(b_gate).reshape(16, 128).T)
    shared = {
        "w_in": f(w_in), "w_gate": f(w_gate), "w_br_pool": f(w_br_pool), "w_br_sb": f(w_br_sb),
        "w_out": f(w_out), "w_up": f(w_up), "w_down": f(w_down), "w_pool_mix": f(w_pool_mix),
        "gains": gains, "pscale": pscale, "bgate": bgate, "invcnt": invcnt, "cbf": cbf,
    }
    in_maps = [dict(shared, x=x[b]) for b in range(B)]
    res = run_bass_kernel_spmd(nc, in_maps, core_ids=list(range(B)))
    out = np.stack([np.asarray(r["out"], dtype=np.float32) for r in res.results], axis=0)
    if _debug:
        return out, [{k: np.asarray(v) for k, v in r.items()} for r in res.results]
    return out
```
